# Optimizing a Trainium2 kernel written in Bass

```python
import math
import jax, jax.numpy as jnp
from jax import lax
import numpy as np

D_MODEL = 2048
BATCH = 4
SEQ = 4096
DEPTH = 1

GRID_W = 64
CTX_LEN = 256
HEAD_DIM = 128
N_HEADS = D_MODEL // HEAD_DIM
HEADS_A = N_HEADS // 2
HEADS_B = N_HEADS - HEADS_A
KV_A = 2
KV_B = 2
WINDOW = 128
BLOCK = 128
FFN_HIDDEN = -(-8 * D_MODEL // 768) * 256
ROPE_THETA = 10000.0
EPS = 1e-6
ATTN_SCALE = HEAD_DIM ** -0.5
DN_ALPHA = (2.0 * DEPTH) ** 0.25
DN_BETA = (8.0 * DEPTH) ** -0.25

QA_W = HEADS_A * HEAD_DIM
KA_W = KV_A * HEAD_DIM
QB_W = HEADS_B * HEAD_DIM
KB_W = KV_B * HEAD_DIM
IN_WIDTH = QA_W + 2 * KA_W + QB_W + 2 * KB_W
IN_SPLITS = [QA_W, QA_W + KA_W, QA_W + 2 * KA_W, QA_W + 2 * KA_W + QB_W, QA_W + 2 * KA_W + QB_W + KB_W]
MIX_WIDTH = QA_W + QB_W

kernel_name = 'hybrid_window_sink_global_qknorm_dit_layer'


def layer_norm(x, g, b):
    xf = x.astype(jnp.float32)
    mu = jnp.mean(xf, axis=-1, keepdims=True)
    var = jnp.mean(jnp.square(xf - mu), axis=-1, keepdims=True)
    return ((xf - mu) * lax.rsqrt(var + EPS) * g + b).astype(x.dtype)


def rms_norm(x, g):
    xf = x.astype(jnp.float32)
    return (xf * lax.rsqrt(jnp.mean(jnp.square(xf), axis=-1, keepdims=True) + EPS) * g).astype(x.dtype)


def axial_rope_tables(rows):
    row_ids = jnp.repeat(jnp.arange(rows, dtype=jnp.float32), GRID_W)
    col_ids = jnp.tile(jnp.arange(GRID_W, dtype=jnp.float32), rows)
    axis_dim = HEAD_DIM // 2
    inv_freq = jnp.power(ROPE_THETA, -jnp.arange(0, axis_dim, 2, dtype=jnp.float32) / axis_dim)
    ang_r = row_ids[:, None] * inv_freq
    ang_c = col_ids[:, None] * inv_freq
    ang = jnp.concatenate([ang_r, ang_r, ang_c, ang_c], axis=-1)
    return jnp.cos(ang)[:, None, :], jnp.sin(ang)[:, None, :]


def rotate_half(x):
    x1, x2 = jnp.split(x, 2, axis=-1)
    return jnp.concatenate([-x2, x1], axis=-1)


def apply_axial_rope(x, cos, sin):
    xf = x.astype(jnp.float32)
    x_row, x_col = jnp.split(xf, 2, axis=-1)
    rot = jnp.concatenate([rotate_half(x_row), rotate_half(x_col)], axis=-1)
    return (xf * cos + rot * sin).astype(x.dtype)


def ada_mods(cond, w_ada, b_ada):
    m = jnp.einsum('bd,de->be', jax.nn.silu(cond), w_ada) + b_ada
    return [t[:, None, :] for t in jnp.split(m, 6, axis=-1)]


def modulate(x, shift, scale):
    return x * (1.0 + scale) + shift


def mixer_qkv(u, w_in, q_norm_g, k_norm_g, rope):
    B, N, _ = u.shape
    h = jnp.einsum('bnd,de->bne', u, w_in)
    qa, ka, va, qb, kb, vb = jnp.split(h, IN_SPLITS, axis=-1)
    heads = lambda t, n: t.reshape(B, N, n, HEAD_DIM)
    qa, ka, va = heads(qa, HEADS_A), heads(ka, KV_A), heads(va, KV_A)
    qb = rms_norm(heads(qb, HEADS_B), q_norm_g)
    kb = rms_norm(heads(kb, KV_B), k_norm_g)
    vb = heads(vb, KV_B)
    if rope is not None:
        cos, sin = rope
        qa, ka, qb, kb = (apply_axial_rope(t, cos, sin) for t in (qa, ka, qb, kb))
    return qa, ka, va, qb, kb, vb


def window_sink_attention(q, k, v, k_ctx, v_ctx, sink_logit):
    B, N = q.shape[0], q.shape[1]
    nb = N // BLOCK
    G = HEADS_A // KV_A
    qb = q.reshape(B, nb, BLOCK, KV_A, G, HEAD_DIM)
    pad = ((0, 0), (BLOCK, BLOCK), (0, 0), (0, 0))
    kp = jnp.pad(k, pad).reshape(B, nb + 2, BLOCK, KV_A, HEAD_DIM)
    vp = jnp.pad(v, pad).reshape(B, nb + 2, BLOCK, KV_A, HEAD_DIM)
    band = lambda t: jnp.concatenate([t[:, :-2], t[:, 1:-1], t[:, 2:]], axis=2)
    kb, vb = band(kp), band(vp)
    s_loc = jnp.einsum('bnqkgd,bnskd->bnkgqs', qb, kb, preferred_element_type=jnp.float32) * ATTN_SCALE
    blk = jnp.arange(nb)[:, None] * BLOCK
    qpos = blk + jnp.arange(BLOCK)[None, :]
    kpos = blk - BLOCK + jnp.arange(3 * BLOCK)[None, :]
    valid = ((jnp.abs(qpos[:, :, None] - kpos[:, None, :]) <= WINDOW)
             & (kpos[:, None, :] >= 0) & (kpos[:, None, :] < N))
    s_loc = jnp.where(valid[None, :, None, None, :, :], s_loc, -jnp.inf)
    s_ctx = jnp.einsum('bnqkgd,bckd->bnkgqc', qb, k_ctx, preferred_element_type=jnp.float32) * ATTN_SCALE
    sink_col = jnp.broadcast_to(sink_logit.reshape(KV_A, G)[None, None, :, :, None, None].astype(jnp.float32),
                                s_loc.shape[:-1] + (1,))
    p = jax.nn.softmax(jnp.concatenate([s_loc, s_ctx, sink_col], axis=-1), axis=-1)
    p_loc = p[..., :3 * BLOCK].astype(v.dtype)
    p_ctx = p[..., 3 * BLOCK:-1].astype(v.dtype)
    o = (jnp.einsum('bnkgqs,bnskd->bnqkgd', p_loc, vb)
         + jnp.einsum('bnkgqc,bckd->bnqkgd', p_ctx, v_ctx))
    return o.reshape(B, N, HEADS_A * HEAD_DIM)


def global_attention(q, k, v, k_ctx, v_ctx):
    B, N = q.shape[0], q.shape[1]
    nb = N // BLOCK
    G = HEADS_B // KV_B
    keys = jnp.concatenate([k, k_ctx], axis=1)
    vals = jnp.concatenate([v, v_ctx], axis=1)
    qb = q.reshape(B, nb, BLOCK, KV_B, G, HEAD_DIM).transpose(1, 0, 2, 3, 4, 5)

    def one_block(q_blk):
        s = jnp.einsum('bqkgd,bskd->bkgqs', q_blk, keys, preferred_element_type=jnp.float32) * ATTN_SCALE
        p = jax.nn.softmax(s, axis=-1).astype(vals.dtype)
        return jnp.einsum('bkgqs,bskd->bqkgd', p, vals)

    o = lax.map(one_block, qb)
    return o.transpose(1, 0, 2, 3, 4, 5).reshape(B, N, HEADS_B * HEAD_DIM)


def context_attention(q, k, v, sink_logit=None):
    B, C, H = q.shape[0], q.shape[1], q.shape[2]
    KV = k.shape[2]
    G = H // KV
    qg = q.reshape(B, C, KV, G, HEAD_DIM)
    s = jnp.einsum('bqkgd,bskd->bkgqs', qg, k, preferred_element_type=jnp.float32) * ATTN_SCALE
    if sink_logit is not None:
        col = jnp.broadcast_to(sink_logit.reshape(KV, G)[None, :, :, None, None].astype(jnp.float32),
                               s.shape[:-1] + (1,))
        s = jnp.concatenate([s, col], axis=-1)
    p = jax.nn.softmax(s, axis=-1)[..., :C].astype(v.dtype)
    o = jnp.einsum('bkgqs,bskd->bqkgd', p, v)
    return o.reshape(B, C, H * HEAD_DIM)


def swiglu(u, w_gate, w_up, w_down):
    return (jax.nn.silu(u @ w_gate) * (u @ w_up)) @ w_down


def setup_inputs(seed: int = 0) -> dict:
    key = jax.random.key(seed)
    ks = jax.random.split(key, 20)
    f32 = jnp.float32
    nrm = lambda k, shape, s: jax.random.normal(k, shape, f32) * s
    D, F, L = D_MODEL, FFN_HIDDEN, DEPTH
    return {
        'x': nrm(ks[0], (BATCH, SEQ, D), 1.0),
        'c': nrm(ks[1], (BATCH, D), 1.0),
        'ctx': nrm(ks[2], (BATCH, CTX_LEN, D), 1.0),
        'c_ctx': nrm(ks[3], (D,), 1.0),
        'w_ada': nrm(ks[4], (L, D, 6 * D), 0.3 * D ** -0.5),
        'b_ada': nrm(ks[5], (L, 6 * D), 0.02),
        'w_in': nrm(ks[6], (L, D, IN_WIDTH), D ** -0.5),
        'q_norm_g': 1.0 + nrm(ks[7], (L, HEAD_DIM), 0.02),
        'k_norm_g': 1.0 + nrm(ks[8], (L, HEAD_DIM), 0.02),
        'sink_logit': nrm(ks[9], (L, HEADS_A), 0.5),
        'w_out': nrm(ks[10], (L, MIX_WIDTH, D), DN_BETA * MIX_WIDTH ** -0.5),
        'ln1_g': 1.0 + nrm(ks[11], (L, D), 0.02),
        'ln1_b': nrm(ks[12], (L, D), 0.02),
        'w_gate': nrm(ks[13], (L, D, F), D ** -0.5),
        'w_up': nrm(ks[14], (L, D, F), D ** -0.5),
        'w_down': nrm(ks[15], (L, F, D), DN_BETA * F ** -0.5),
        'ln2_g': 1.0 + nrm(ks[16], (L, D), 0.02),
        'ln2_b': nrm(ks[17], (L, D), 0.02),
    }


def reference(x, c, ctx, c_ctx, w_ada, b_ada, w_in, q_norm_g, k_norm_g, sink_logit,
              w_out, ln1_g, ln1_b, w_gate, w_up, w_down, ln2_g, ln2_b):
    ROWS = x.shape[1] // GRID_W
    rope = axial_rope_tables(ROWS)
    for layer in range(DEPTH):
        sh1, sc1, g1, sh2, sc2, g2 = ada_mods(c, w_ada[layer], b_ada[layer])
        csh1, csc1, cg1, csh2, csc2, cg2 = ada_mods(c_ctx[None, :], w_ada[layer], b_ada[layer])

        qa, ka, va, qb, kb, vb = mixer_qkv(modulate(x, sh1, sc1), w_in[layer],
                                           q_norm_g[layer], k_norm_g[layer], rope)
        qac, kac, vac, qbc, kbc, vbc = mixer_qkv(modulate(ctx, csh1, csc1), w_in[layer],
                                                 q_norm_g[layer], k_norm_g[layer], None)
        heads = jnp.concatenate([window_sink_attention(qa, ka, va, kac, vac, sink_logit[layer]),
                                 global_attention(qb, kb, vb, kbc, vbc)], axis=-1)
        x = layer_norm(DN_ALPHA * x + g1 * (heads @ w_out[layer]), ln1_g[layer], ln1_b[layer])

        x = layer_norm(DN_ALPHA * x + g2 * swiglu(modulate(x, sh2, sc2), w_gate[layer], w_up[layer], w_down[layer]),
                       ln2_g[layer], ln2_b[layer])

        if layer < DEPTH - 1:
            heads_c = jnp.concatenate([context_attention(qac, kac, vac, sink_logit[layer]),
                                       context_attention(qbc, kbc, vbc)], axis=-1)
            ctx = layer_norm(DN_ALPHA * ctx + cg1 * (heads_c @ w_out[layer]), ln1_g[layer], ln1_b[layer])
            ctx = layer_norm(DN_ALPHA * ctx + cg2 * swiglu(modulate(ctx, csh2, csc2), w_gate[layer], w_up[layer], w_down[layer]),
                             ln2_g[layer], ln2_b[layer])
    return x
```

```python
import contextlib
import numpy as np
import concourse.bass as bass
import concourse.mybir as mybir
from concourse.bass_utils import run_bass_kernel_spmd

F32 = mybir.dt.float32
BF16 = mybir.dt.bfloat16
ALU = mybir.AluOpType
AF = mybir.ActivationFunctionType
AX = mybir.AxisListType

D = 2048
SEQ = 4096
NOWN = 2048
CTXL = 256
NKEY = SEQ + CTXL
FF = 5632
NF = FF // 128
EPS = 1e-6
SCALE = 128 ** -0.5
DN_ALPHA = 2.0 ** 0.25
GRID_W = 64

DEBUG = False
PHASES = (0, 1, 2, 3, 4, 5)

COMPUTE = ("pe", "act", "dve", "pool")


class Op:
    __slots__ = ("eng", "fn", "deps", "signal", "sem", "val", "is_dma", "semkey", "ndma")

    def __init__(self, eng, fn, is_dma=False, semkey=None, ndma=0):
        self.eng = eng
        self.fn = fn
        self.deps = set()
        self.signal = False
        self.sem = None
        self.val = 0
        self.is_dma = is_dma
        self.semkey = semkey
        self.ndma = ndma


class Phase:
    def __init__(self, nc, name):
        self.nc = nc
        self.name = name
        self.eng_ops = {e: [] for e in ("pe", "act", "dve", "pool", "sp")}
        self.last_writer = {}
        self.readers = {}

    def _add(self, o, reads, writes):
        deps = o.deps
        lw = self.last_writer
        rd = self.readers
        for k in reads:
            w = lw.get(k)
            if w is not None:
                deps.add(w)
        for k in writes:
            w = lw.get(k)
            if w is not None:
                deps.add(w)
            r = rd.get(k)
            if r:
                deps.update(r)
        deps.discard(o)
        for k in reads:
            rd.setdefault(k, []).append(o)
        for k in writes:
            lw[k] = o
            rd[k] = []
        for d in deps:
            if d.eng == "pe" and o.eng == "pe" and not d.is_dma and not o.is_dma:
                continue
            d.signal = True
        self.eng_ops[o.eng].append(o)
        return o

    def op(self, eng, fn, reads=(), writes=()):
        return self._add(Op(eng, fn), reads, writes)

    def dma(self, eng, semkey, pairs, reads=(), writes=(), slow=False):
        pairs = list(pairs)

        def fn(e, pairs=pairs, slow=slow):
            if slow:
                return [e.dma_start(out=o_, in_=i_, allow_slow_non_contiguous=True) for (o_, i_) in pairs]
            return [e.dma_start(out=o_, in_=i_) for (o_, i_) in pairs]

        o = Op(eng, fn, is_dma=True, semkey=semkey, ndma=len(pairs))
        o.signal = True
        return self._add(o, reads, writes)

    semstack = None
    dsem = {}
    dma_counts = {}

    def emit(self):
        nc = self.nc
        dma_counts = Phase.dma_counts
        dsem = Phase.dsem
        slots = {}
        for e, ops in self.eng_ops.items():
            for o in ops:
                if o.is_dma:
                    o.semkey = ("slot", slots.setdefault(o.semkey, len(slots)))
        for e, ops in self.eng_ops.items():
            cnt = 0
            for o in ops:
                if o.is_dma:
                    if o.semkey not in dma_counts:
                        dma_counts[o.semkey] = 0
                        dsem[o.semkey] = Phase.semstack.enter_context(nc.semaphore(f"d{len(dsem)}"))
                    dma_counts[o.semkey] += 16 * o.ndma
                    o.val = dma_counts[o.semkey]
                elif o.signal:
                    cnt += 1
                    o.val = cnt
        with contextlib.ExitStack() as es:
            esem = {e: Phase.semstack.enter_context(nc.semaphore(f"{self.name}_{e}")) for e in COMPUTE}
            for e, ops in self.eng_ops.items():
                for o in ops:
                    o.sem = dsem[o.semkey] if o.is_dma else esem[e]
            block = es.enter_context(nc.Block())
            eng_ops = self.eng_ops

            def run(e, engname):
                waited = {}
                for o in eng_ops[engname]:
                    need = {}
                    for d in o.deps:
                        if (not d.is_dma) and d.eng == "pe" and engname == "pe" and not o.is_dma:
                            continue
                        s = d.sem
                        if d.val > need.get(s, 0):
                            need[s] = d.val
                    for s, v in need.items():
                        if waited.get(s, 0) < v:
                            e.wait_ge(s, v)
                            waited[s] = v
                    r = o.fn(e)
                    if o.is_dma:
                        for ins in r:
                            ins.then_inc(o.sem, 16)
                    elif o.signal:
                        r.then_inc(o.sem, 1)
                last = {}
                for o in eng_ops[engname]:
                    if o.is_dma:
                        last[o.sem] = o.val
                for s, v in last.items():
                    if waited.get(s, 0) < v:
                        e.wait_ge(s, v)

            @block.tensor
            def _(e):
                run(e, "pe")

            @block.scalar
            def _(e):
                run(e, "act")

            @block.vector
            def _(e):
                run(e, "dve")

            @block.gpsimd
            def _(e):
                run(e, "pool")

            @block.sync
            def _(e):
                run(e, "sp")


_UNIQ = [0]


def _uniq(name):
    _UNIQ[0] += 1
    return f"s{_UNIQ[0]}_{name}"


def build_program():
    nc = bass.Bass("TRN2", target_bir_lowering=False)

    def din(name, shape, dt=F32):
        return nc.dram_tensor(name, list(shape), dt, kind="ExternalInput")

    def dscr(name, shape, dt):
        return nc.dram_tensor(name, list(shape), dt, kind="ExternalOutput" if DEBUG else "Internal")

    x_h = din("x", [SEQ, D])
    ctx_h = din("ctx", [CTXL, D])
    condT_h = din("condT", [128, 16, 2])
    wada_h = din("w_ada", [D, 6 * D])
    bada_h = din("b_ada", [1, 6 * D])
    wq_h = din("wq", [D, 2048])
    wkv_h = din("wkv", [D, 1024])
    wout_h = din("w_out", [D, D])
    wg_h = din("wg", [NF, 128, 16, 128])
    wu_h = din("wu", [NF, 128, 16, 128])
    wd_h = din("wd", [4, 128, NF, 512])
    cos_h = din("cosT", [128, SEQ])
    sin_h = din("sinT", [128, SEQ])
    ident_h = din("ident", [128, 128])
    rotm_h = din("rotm", [128, 128])
    gcol_h = din("gcol", [128, 2])
    grow_h = din("grow", [1, 256])
    sink_h = din("sink", [1, 8])
    lnp_h = din("lnp", [4, D])
    masks_h = din("masks", [4, 128, 512])
    out_h = nc.dram_tensor("out", [NOWN, D], F32, kind="ExternalOutput")

    mrow_h = dscr("mrow", [2, 6 * D], F32)
    QT_h = dscr("QT", [16, 128, NOWN], BF16)
    KT_h = dscr("KT", [4, 128, NKEY], BF16)
    Vd_h = dscr("Vd", [NKEY, 512], BF16)
    OT_h = dscr("OT", [16, 128, NOWN], BF16)
    X1_h = dscr("X1", [NOWN, D], F32)
    X1T_h = dscr("X1T", [128, 16, NOWN], BF16)
    Z_h = dscr("Z", [NOWN, D], F32)

    x = x_h.ap()
    ctx = ctx_h.ap()
    out = out_h.ap()
    QT = QT_h.ap()
    KT = KT_h.ap()
    Vd = Vd_h.ap()
    OT = OT_h.ap()
    X1 = X1_h.ap()
    X1T = X1T_h.ap()
    Z = Z_h.ap()
    mrow = mrow_h.ap()

    def bc(handle, row_off, n, parts=128):
        return bass.AP(handle, row_off, [[0, parts], [1, n]])

    with contextlib.ExitStack() as top:
        Phase.semstack = top
        Phase.dsem = {}
        Phase.dma_counts = {}
        S = lambda name, shape, dt: top.enter_context(nc.sbuf_tensor(_uniq(name), list(shape), dt))
        ident = S("ident", [128, 128], F32)
        rotm = S("rotm", [128, 128], F32)
        onesd = S("onesd", [128, 128], F32)
        ones_f = S("ones_f", [128, 128], F32)
        ones_b = S("ones_b", [128, 128], BF16)
        modT = S("modT", [128, 128], F32)
        gcol = S("gcol", [128, 2], F32)
        gqs = S("gqs", [128, 1], F32)
        nrmmax = S("nrmmax", [1, 2], F32)
        negcA = S("negcA", [128, 1], F32)
        negcB = S("negcB", [128, 1], F32)
        sinkb = S("sinkb", [128, 8], F32)
        epst = S("epst", [128, 1], F32)
        condb2 = S("condb2", [128, 16, 1], BF16)
        rotb = S("rotb", [128, 128], BF16)
        onesdb = S("onesdb", [128, 128], BF16)

        if 0 in PHASES:
            with contextlib.ExitStack() as es:
                T = lambda name, shape, dt: es.enter_context(nc.sbuf_tensor(_uniq(name), list(shape), dt))
                condf = T("condf", [128, 16, 2], F32)
                condb = T("condb", [128, 16, 2], F32)
                wring = [T(f"wring{i}", [128, 16, 512], F32) for i in range(3)]
                mrow_sb = T("mrow_sb", [2, 2 * D], F32)
                bada_sb = T("bada_sb", [2, 2 * D], F32)
                grow = T("grow", [1, 256], F32)
                gtmp = T("gtmp", [1, 4], F32)
                ps = [es.enter_context(nc.psum_tensor(_uniq(f"p0ps{i}"), [128, 512], F32)) for i in range(2)]
                P = Phase(nc, "p0")
                P.dma("sp", "c0", [(ident[:], ident_h.ap()), (rotm[:], rotm_h.ap()), (gcol[:], gcol_h.ap()),
                                   (condf[:], condT_h.ap()), (grow[:], grow_h.ap())],
                      writes=["ident", "rotm", "gcol", "condf", "grow"])
                P.dma("sp", "c1", [(bada_sb[:], bc(bada_h, 0, 2 * D, 2))], writes=["bada"])
                P.op("dve", lambda e: e.memset(onesd[:], 1.0 / 128.0), writes=["onesd"])
                P.op("dve", lambda e: e.memset(onesdb[:], 1.0 / 128.0), writes=["onesdb"])
                P.op("dve", lambda e: e.tensor_copy(out=rotb[:], in_=rotm[:]), reads=["rotm"], writes=["rotb"])
                P.op("dve", lambda e: e.memset(ones_f[:], 1.0), writes=["ones_f"])
                P.op("dve", lambda e: e.memset(ones_b[:], 1.0), writes=["ones_b"])
                P.op("dve", lambda e: e.memset(epst[:], EPS), writes=["epst"])
                P.op("dve", lambda e: e.memset(nrmmax[:], 0.0), writes=["nrmmax"])
                P.op("dve", lambda e: e.tensor_scalar(out=gqs[:], in0=gcol[:, 0:1], scalar1=SCALE, scalar2=None, op0=ALU.mult),
                     reads=["gcol"], writes=["gqs"])
                P.op("dve", lambda e: e.tensor_reduce(out=gtmp[:, 0:1], in_=grow[:, 0:128], axis=AX.X, op=ALU.max, apply_absolute_value=True),
                     reads=["grow"], writes=["gtmp0"])
                P.op("dve", lambda e: e.tensor_reduce(out=gtmp[:, 1:2], in_=grow[:, 128:256], axis=AX.X, op=ALU.max, apply_absolute_value=True),
                     reads=["grow"], writes=["gtmp1"])
                P.op("dve", lambda e: e.tensor_tensor(out=gtmp[:, 2:3], in0=gtmp[:, 0:1], in1=gtmp[:, 1:2], op=ALU.mult),
                     reads=["gtmp0", "gtmp1"], writes=["gtmp2"])
                P.op("dve", lambda e: e.tensor_scalar(out=gtmp[:, 3:4], in0=gtmp[:, 2:3], scalar1=-128.0 * SCALE * 1.001, scalar2=None, op0=ALU.mult),
                     reads=["gtmp2"], writes=["gtmp3"])
                P.op("pe", lambda e: e.matmul(ps[1][:, 0:1], lhsT=ones_f[0:1, :], rhs=gtmp[:, 3:4], start=True, stop=True),
                     reads=["ones_f", "gtmp3"], writes=["ps1"])
                P.op("dve", lambda e: e.tensor_copy(out=negcB[:], in_=ps[1][:, 0:1]), reads=["ps1"], writes=["negcB"])
                P.op("act", lambda e: e.activation(out=condb[:], in_=condf[:], func=AF.Silu), reads=["condf"], writes=["condb"])
                P.op("act", lambda e: e.activation(out=condb2[:], in_=condf[:, :, 0:1], func=AF.Silu), reads=["condf"], writes=["condb2"])
                wv = wada_h.ap().rearrange("(k p) c -> p k c", p=128)
                NG = 8
                for g in range(NG):
                    r = g % 3
                    P.dma("sp" if g % 2 == 0 else "pool", ("wr", r, g % 2), [(wring[r][:], wv[:, :, g * 512:(g + 1) * 512])], writes=[("wr", r)])

                    def mm(e, r=r):
                        ins = None
                        for k in range(16):
                            ins = e.matmul(ps[0][0:2, :], lhsT=condb[:, k, :], rhs=wring[r][:, k, :], start=(k == 0), stop=(k == 15))
                        return ins
                    P.op("pe", mm, reads=["condb", ("wr", r)], writes=["ps0"])
                    P.op("dve", lambda e, g=g: e.tensor_tensor(out=mrow_sb[:, g * 512:(g + 1) * 512], in0=ps[0][0:2, :],
                                                              in1=bada_sb[:, g * 512:(g + 1) * 512], op=ALU.add),
                         reads=["ps0", "bada"], writes=[("mrow_sb", g)])
                P.dma("sp", "m0", [(mrow[:, 0:2 * D], mrow_sb[:])], reads=[("mrow_sb", g) for g in range(NG)], writes=["mrow"])
                P.dma("sp", "m1", [(modT[:, 0:32], mrow[0:1, 0:4096].rearrange("o (j p) -> p (o j)", p=128)),
                                   (modT[:, 96:128], mrow[1:2, 0:4096].rearrange("o (j p) -> p (o j)", p=128))],
                      reads=["mrow"], writes=["modT"], slow=True)
                for (a, b) in ((16, 32), (112, 128)):
                    P.op("dve", lambda e, a=a, b=b: e.tensor_scalar(out=modT[:, a:b], in0=modT[:, a:b], scalar1=1.0, scalar2=None, op0=ALU.add),
                         reads=["modT"], writes=["modT"])
                P.emit()

        if 1 in PHASES:
            with contextlib.ExitStack() as es:
                T = lambda name, shape, dt: es.enter_context(nc.sbuf_tensor(_uniq(name), list(shape), dt))
                wq_sb = T("wq_sb", [128, 16, 2048], BF16)
                wkv_sb = T("wkv_sb", [128, 16, 1024], BF16)
                xst = [T(f"xst{i}", [128, D], F32) for i in range(4)]
                xT = [T(f"xT{i}", [128, 16, 512], BF16) for i in range(2)]
                cs = [T(f"cs{i}", [128, 2, 512], F32) for i in range(2)]
                yb = [T(f"yb{i}", [128, 512], F32) for i in range(3)]
                sqf = [T(f"sqf{i}", [128, 512], BF16) for i in range(2)]
                ybb = [T(f"ybb{i}", [128, 512], BF16) for i in range(3)]
                sqb = [T(f"sqb{i}", [128, 512], BF16) for i in range(2)]
                t1 = [T(f"t1{i}", [128, 512], F32) for i in range(2)]
                t2 = [T(f"t2{i}", [128, 512], F32) for i in range(2)]
                rr = [T(f"rr{i}", [128, 512], F32) for i in range(2)]
                outb = [T(f"outb{i}", [128, 512], BF16) for i in range(4)]
                vout = [T(f"vout{i}", [128, 512], BF16) for i in range(2)]
                ntmp = T("ntmp", [1, 2], F32)
                tp = [es.enter_context(nc.psum_tensor(_uniq(f"tp{i}"), [128, 512], F32)) for i in range(2)]
                mmp = [es.enter_context(nc.psum_tensor(_uniq(f"mm{i}"), [128, 512], F32)) for i in range(3)]
                aux = [es.enter_context(nc.psum_tensor(_uniq(f"aux{i}"), [128, 512], F32)) for i in range(2)]
                nrp = es.enter_context(nc.psum_tensor(_uniq("nrp"), [128, 512], F32))
                P = Phase(nc, "p1")
                wkv_v = wkv_h.ap().rearrange("(k p) c -> p k c", p=128)
                wq_v = wq_h.ap().rearrange("(k p) c -> p k c", p=128)
                P.dma("pool", "wkv", [(wkv_sb[:, 4 * k:4 * k + 4, :], wkv_v[:, 4 * k:4 * k + 4, :]) for k in range(4)], writes=["wkv"])
                P.dma("pool", "wq", [(wq_sb[:, 4 * k:4 * k + 4, :], wq_v[:, 4 * k:4 * k + 4, :]) for k in range(4)], writes=["wq"])

                cnt = {"xT": 0, "tp": 0, "mm": 0, "aux": 0, "y": 0, "sq": 0, "t": 0, "rr": 0, "ob": 0, "vo": 0, "ev": 0, "cs": 0}

                def nxt(k, n):
                    v = cnt[k] % n
                    cnt[k] += 1
                    return v

                chunks = [("ctx", 0)] + [("tok", c) for c in (4, 5, 6, 7, 0, 1, 2, 3)]

                def load_x(ch):
                    kind, c = ch
                    if kind == "ctx":
                        for t in range(2):
                            P.dma("sp", ("xst", t), [(xst[t][:], ctx[t * 128:(t + 1) * 128, :])], writes=[("xst", t)])
                    else:
                        for t in range(4):
                            r0 = c * 512 + t * 128
                            P.dma("sp", ("xst", t), [(xst[t][:], x[r0:r0 + 128, :])], writes=[("xst", t)])

                load_x(chunks[0])
                for ci, ch in enumerate(chunks):
                    kind, c = ch
                    ntile = 2 if kind == "ctx" else 4
                    W = ntile * 128
                    own = (kind == "tok" and c < 4)
                    xi = nxt("xT", 2)
                    xTc = xT[xi]
                    moff = 96 if kind == "ctx" else 0
                    csb = None
                    if kind == "tok":
                        ci_ = nxt("cs", 2)
                        csb = cs[ci_]
                        P.dma("sp", ("cs", ci_), [(csb[:, 0, :], cos_h.ap()[:, c * 512:(c + 1) * 512]),
                                                  (csb[:, 1, :], sin_h.ap()[:, c * 512:(c + 1) * 512])], writes=[("cs", ci_)])
                    for k in range(16):
                        ti = nxt("tp", 2)

                        def trf(e, k=k, ti=ti, ntile=ntile):
                            ins = None
                            for t in range(ntile):
                                ins = e.transpose(out=tp[ti][:, t * 128:(t + 1) * 128], in_=xst[t][:, k * 128:(k + 1) * 128], identity=ident[:])
                            return ins
                        P.op("pe", trf, reads=[("xst", t) for t in range(ntile)] + ["ident"], writes=[("tp", ti)])
                        sc_ap = modT[:, moff + 16 + k: moff + 17 + k]
                        sh_ap = modT[:, moff + k: moff + k + 1]
                        if nxt("ev", 2) == 0:
                            P.op("act", lambda e, k=k, ti=ti, W=W, sc_ap=sc_ap, sh_ap=sh_ap, xTc=xTc:
                                 e.activation(out=xTc[:, k, 0:W], in_=tp[ti][:, 0:W], func=AF.Identity, bias=sh_ap, scale=sc_ap),
                                 reads=[("tp", ti), "modT"], writes=[("xT", xi, k)])
                        else:
                            P.op("dve", lambda e, k=k, ti=ti, W=W, sc_ap=sc_ap, sh_ap=sh_ap, xTc=xTc:
                                 e.tensor_scalar(out=xTc[:, k, 0:W], in0=tp[ti][:, 0:W], scalar1=sc_ap, scalar2=sh_ap, op0=ALU.mult, op1=ALU.add),
                                 reads=[("tp", ti), "modT"], writes=[("xT", xi, k)])
                    if ci + 1 < len(chunks):
                        load_x(chunks[ci + 1])
                    xkeys = [("xT", xi, k) for k in range(16)]

                    heads = []
                    for h in range(4):
                        heads.append(("k", h))
                    if own:
                        for h in range(16):
                            heads.append(("q", h))
                    for (hk, h) in heads:
                        mi = nxt("mm", 3)
                        if hk == "k":
                            wsb, col0, wkey = wkv_sb, h * 128, "wkv"
                            isB = h >= 2
                        else:
                            wsb, col0, wkey = wq_sb, h * 128, "wq"
                            isB = h >= 8

                        def mmf(e, wsb=wsb, col0=col0, mi=mi, W=W, xTc=xTc):
                            ins = None
                            for k in range(16):
                                ins = e.matmul(mmp[mi][:, 0:W], lhsT=wsb[:, k, col0:col0 + 128], rhs=xTc[:, k, 0:W], start=(k == 0), stop=(k == 15))
                            return ins
                        P.op("pe", mmf, reads=xkeys + [wkey], writes=[("mm", mi)])
                        yi = nxt("y", 3)
                        oi = nxt("ob", 4)
                        ob = outb[oi]
                        if isB:
                            gap = gqs[:, 0:1] if hk == "q" else gcol[:, 1:2]
                            P.op("act", lambda e, yi=yi, mi=mi, W=W, gap=gap: e.activation(out=yb[yi][:, 0:W], in_=mmp[mi][:, 0:W], func=AF.Identity, scale=gap),
                                 reads=[("mm", mi), "gqs", "gcol"], writes=[("y", yi)])
                            si = nxt("sq", 2)
                            P.op("act", lambda e, si=si, mi=mi, W=W: e.activation(out=sqf[si][:, 0:W], in_=mmp[mi][:, 0:W], func=AF.Square),
                                 reads=[("mm", mi)], writes=[("sqf", si)])
                            a_ms = nxt("aux", 2)
                            P.op("pe", lambda e, a_ms=a_ms, si=si, W=W: e.matmul(aux[a_ms][:, 0:W], lhsT=onesdb[:], rhs=sqf[si][:, 0:W], start=True, stop=True),
                                 reads=[("sqf", si), "onesd"], writes=[("aux", a_ms)])
                            ri = nxt("rr", 2)
                            P.op("act", lambda e, ri=ri, a_ms=a_ms, W=W: e.activation(out=rr[ri][:, 0:W], in_=aux[a_ms][:, 0:W], func=AF.Sqrt, bias=epst[:, 0:1], scale=1.0),
                                 reads=[("aux", a_ms), "epst"], writes=[("rr", ri)])
                            P.op("dve", lambda e, ri=ri, W=W: e.reciprocal(out=rr[ri][:, 0:W], in_=rr[ri][:, 0:W]),
                                 reads=[("rr", ri)], writes=[("rr", ri)])
                        else:
                            sc = SCALE if hk == "q" else 1.0
                            P.op("act", lambda e, yi=yi, mi=mi, W=W, sc=sc: e.activation(out=yb[yi][:, 0:W], in_=mmp[mi][:, 0:W], func=AF.Identity, scale=sc),
                                 reads=[("mm", mi)], writes=[("y", yi)])
                            si = nxt("sq", 2)
                            P.op("act", lambda e, si=si, yi=yi, W=W: e.activation(out=sqb[si][:, 0:W], in_=yb[yi][:, 0:W], func=AF.Square),
                                 reads=[("y", yi)], writes=[("sqb", si)])
                            P.op("pe", lambda e, si=si, W=W: e.matmul(nrp[0:1, 0:W], lhsT=ones_b[:, 0:1], rhs=sqb[si][:, 0:W], start=True, stop=True),
                                 reads=[("sqb", si), "ones_b"], writes=["nrp"])
                            col = 0 if hk == "q" else 1
                            P.op("dve", lambda e, W=W: e.tensor_reduce(out=ntmp[:, 0:1], in_=nrp[0:1, 0:W], axis=AX.X, op=ALU.max),
                                 reads=["nrp"], writes=["ntmp"])
                            P.op("dve", lambda e, col=col: e.tensor_tensor(out=nrmmax[:, col:col + 1], in0=nrmmax[:, col:col + 1], in1=ntmp[:, 0:1], op=ALU.max),
                                 reads=["ntmp", "nrmmax"], writes=["nrmmax"])
                        if kind == "tok":
                            a_r = nxt("aux", 2)
                            P.op("act", lambda e, yi=yi, W=W: e.activation(out=ybb[yi][:, 0:W], in_=yb[yi][:, 0:W], func=AF.Identity),
                                 reads=[("y", yi)], writes=[("ybb", yi)])
                            P.op("pe", lambda e, a_r=a_r, yi=yi, W=W: e.matmul(aux[a_r][:, 0:W], lhsT=rotb[:], rhs=ybb[yi][:, 0:W], start=True, stop=True),
                                 reads=[("ybb", yi), "rotm"], writes=[("aux", a_r)])
                            ti_ = nxt("t", 2)
                            P.op("pool", lambda e, ti_=ti_, yi=yi, csb=csb: e.tensor_tensor(out=t1[ti_][:], in0=yb[yi][:], in1=csb[:, 0, :], op=ALU.mult),
                                 reads=[("y", yi), ("cs", ci_)], writes=[("t1", ti_)])
                            P.op("dve", lambda e, ti_=ti_, a_r=a_r, csb=csb: e.tensor_tensor(out=t2[ti_][:], in0=aux[a_r][:], in1=csb[:, 1, :], op=ALU.mult),
                                 reads=[("aux", a_r), ("cs", ci_)], writes=[("t2", ti_)])
                            if isB:
                                P.op("pool", lambda e, ti_=ti_: e.tensor_tensor(out=t1[ti_][:], in0=t1[ti_][:], in1=t2[ti_][:], op=ALU.add),
                                     reads=[("t1", ti_), ("t2", ti_)], writes=[("t1", ti_)])
                                P.op("dve", lambda e, ti_=ti_, ri=ri, ob=ob: e.tensor_tensor(out=ob[:], in0=t1[ti_][:], in1=rr[ri][:], op=ALU.mult),
                                     reads=[("t1", ti_), ("rr", ri)], writes=[("ob", oi)])
                            else:
                                P.op("dve", lambda e, ti_=ti_, ob=ob: e.tensor_tensor(out=ob[:], in0=t1[ti_][:], in1=t2[ti_][:], op=ALU.add),
                                     reads=[("t1", ti_), ("t2", ti_)], writes=[("ob", oi)])
                        else:
                            if isB:
                                P.op("dve", lambda e, yi=yi, ri=ri, ob=ob, W=W: e.tensor_tensor(out=ob[:, 0:W], in0=yb[yi][:, 0:W], in1=rr[ri][:, 0:W], op=ALU.mult),
                                     reads=[("y", yi), ("rr", ri)], writes=[("ob", oi)])
                            else:
                                P.op("dve", lambda e, yi=yi, ob=ob, W=W: e.tensor_copy(out=ob[:, 0:W], in_=yb[yi][:, 0:W]),
                                     reads=[("y", yi)], writes=[("ob", oi)])
                        if hk == "k":
                            c0 = SEQ if kind == "ctx" else c * 512
                            P.dma("sp", ("ob", oi), [(KT[h, :, c0:c0 + W], ob[:, 0:W])], reads=[("ob", oi)], writes=[("KT", h, ci)])
                        else:
                            P.dma("sp", ("ob", oi), [(QT[h, :, c * 512:(c + 1) * 512], ob[:])], reads=[("ob", oi)], writes=[("QT", h, c)])
                    for t in range(ntile):
                        mi = nxt("mm", 3)

                        def mmv(e, mi=mi, t=t, xTc=xTc):
                            ins = None
                            for k in range(16):
                                ins = e.matmul(mmp[mi][:], lhsT=xTc[:, k, t * 128:(t + 1) * 128], rhs=wkv_sb[:, k, 512:1024], start=(k == 0), stop=(k == 15))
                            return ins
                        P.op("pe", mmv, reads=xkeys + ["wkv"], writes=[("mm", mi)])
                        vi = nxt("vo", 2)
                        P.op("act", lambda e, vi=vi, mi=mi: e.activation(out=vout[vi][:], in_=mmp[mi][:], func=AF.Identity),
                             reads=[("mm", mi)], writes=[("vo", vi)])
                        r0 = (SEQ if kind == "ctx" else c * 512) + t * 128
                        P.dma("sp", ("vo", vi), [(Vd[r0:r0 + 128, :], vout[vi][:])], reads=[("vo", vi)], writes=[("Vd", r0)])
                P.emit()

        if 2 in PHASES:
            with contextlib.ExitStack() as es:
                T = lambda name, shape, dt: es.enter_context(nc.sbuf_tensor(_uniq(name), list(shape), dt))
                KT_sb = T("KT_sb", [128, 4, NKEY], BF16)
                V_sb = T("V_sb", [128, 34, 512], BF16)
                QT_sb = [T(f"QT_sb{i}", [128, 16, 512], BF16) for i in range(2)]
                OT_sb = [T(f"OT_sb{i}", [128, 16, 512], BF16) for i in range(2)]
                NPT = 6
                pT = [T(f"pT{i}", [128, 512], BF16) for i in range(NPT)]
                accD = [T(f"accD{i}", [128, 512], F32) for i in range(2)]
                accP = [T(f"accP{i}", [128, 512], F32) for i in range(2)]
                wr2 = [T(f"wr2{i}", [128, 16, 512], BF16) for i in range(2)]
                bada2 = [T(f"bada2{i}", [1, 512], F32) for i in range(2)]
                mr2 = [T(f"mr2{i}", [1, 512], F32) for i in range(2)]
                msk = T("msk", [128, 4, 512], F32)
                rl = [T(f"rl{i}", [128, 512], F32) for i in range(2)]
                sinkrow = T("sinkrow", [128, 2, 512], F32)
                sm = T("sm", [1, 16], F32)
                Sps = [es.enter_context(nc.psum_tensor(_uniq(f"Sps{i}"), [128, 512], F32)) for i in range(3)]
                Ops = [es.enter_context(nc.psum_tensor(_uniq(f"Ops{i}"), [128, 512], F32)) for i in range(2)]
                Lps = [es.enter_context(nc.psum_tensor(_uniq(f"Lps{i}"), [128, 512], F32)) for i in range(2)]
                mps = es.enter_context(nc.psum_tensor(_uniq("mps"), [128, 512], F32))
                P = Phase(nc, "p2")
                for h in range(4):
                    P.dma("sp", ("kt", h), [(KT_sb[:, h, :], KT[h])], writes=[("KT", h)])
                Vv = Vd.rearrange("(b p) c -> p b c", p=128)
                for q4 in range(4):
                    b0, b1 = q4 * 9, min(34, q4 * 9 + 9)
                    P.dma("sp", ("v", q4), [(V_sb[:, b0:b1, :], Vv[:, b0:b1, :])], writes=[("V", b) for b in range(b0, b1)])
                P.dma("sp", "msk", [(msk[:, i, :], masks_h.ap()[i]) for i in range(4)], writes=["msk"])
                sink_sb = T("sink_sb", [1, 8], F32)
                P.dma("sp", "sink", [(sink_sb[:], sink_h.ap())], writes=["sink_sb"])
                P.op("dve", lambda e: e.tensor_tensor(out=sm[:, 0:1], in0=nrmmax[:, 0:1], in1=nrmmax[:, 1:2], op=ALU.mult), writes=["sm0"])
                P.op("act", lambda e: e.activation(out=sm[:, 1:2], in_=sm[:, 0:1], func=AF.Sqrt), reads=["sm0"], writes=["sm1"])
                P.op("dve", lambda e: e.tensor_scalar(out=sm[:, 2:3], in0=sm[:, 1:2], scalar1=-1.02, scalar2=None, op0=ALU.mult), reads=["sm1"], writes=["sm2"])
                P.op("act", lambda e: e.activation(out=sm[:, 8:16], in_=sink_sb[:], func=AF.Exp, bias=sm[:, 2:3], scale=1.0),
                     reads=["sm2", "sink_sb"], writes=["sm8"])
                P.op("pe", lambda e: e.matmul(mps[:, 0:1], lhsT=ones_f[0:1, :], rhs=sm[:, 2:3], start=True, stop=True), reads=["sm2"], writes=["mps"])
                P.op("dve", lambda e: e.tensor_copy(out=negcA[:], in_=mps[:, 0:1]), reads=["mps"], writes=["negcA"])
                P.op("pe", lambda e: e.matmul(mps[:, 0:8], lhsT=ones_f[0:1, :], rhs=sm[:, 8:16], start=True, stop=True), reads=["sm8", "negcA"], writes=["mps"])
                P.op("dve", lambda e: e.tensor_copy(out=sinkb[:], in_=mps[:, 0:8]), reads=["mps"], writes=["sinkb"])
                for g in range(2):
                    for h in range(4):
                        P.op("dve", lambda e, g=g, h=h: e.tensor_scalar(out=sinkrow[:, g, h * 128:(h + 1) * 128], in0=ones_f[:], scalar1=sinkb[:, 4 * g + h:4 * g + h + 1],
                                                                         scalar2=None, op0=ALU.mult),
                             reads=["sinkb"], writes=[("sinkrow", g)])

                steps = []
                for qc in range(4):
                    cb = qc % 2
                    for g in range(2):
                        for h in range(4):
                            head = 8 + 4 * g + h
                            for sb in range(34):
                                steps.append(dict(
                                    qc=qc, cb=cb, kind="B", first=(sb == 0), last=(sb == 33),
                                    lhsT=KT_sb[:, 2 + g, sb * 128:(sb + 1) * 128], kkey=("KT", 2 + g),
                                    rhs=QT_sb[cb][:, head, :],
                                    vs=V_sb[:, sb, 256 + g * 128:256 + (g + 1) * 128], vkey=("V", sb),
                                    bias=negcB, mask=None, head=head, g=g, i=None, sb=sb))
                    for g in range(2):
                        for i in range(4):
                            qb = qc * 4 + i
                            blocks = []
                            if qb == 0:
                                blocks.append((31, 2))
                            else:
                                blocks.append((qb - 1, 0))
                            blocks.append((qb, None))
                            if qb == 15:
                                blocks.append((16, 3))
                            else:
                                blocks.append((qb + 1, 1))
                            blocks.append((32, None))
                            blocks.append((33, None))
                            for bi, (kb, mk) in enumerate(blocks):
                                steps.append(dict(
                                    qc=qc, cb=cb, kind="A", first=(bi == 0), last=(bi == len(blocks) - 1),
                                    lhsT=KT_sb[:, g, kb * 128:(kb + 1) * 128], kkey=("KT", g),
                                    rhs=QT_sb[cb][:, 4 * g:4 * g + 4, i * 128:(i + 1) * 128],
                                    vs=V_sb[:, kb, g * 128:(g + 1) * 128], vkey=("V", kb),
                                    bias=negcA, mask=mk, head=None, g=g, i=i))
                nst = len(steps)
                acc_idx = 0
                for st in steps:
                    if st["first"]:
                        acc_idx += 1
                    st["acc"] = acc_idx % 2

                loaded_qc = set()

                def load_q(qc):
                    if qc in loaded_qc or qc >= 4:
                        return
                    loaded_qc.add(qc)
                    cb = qc % 2
                    P.dma("sp", ("qt", cb), [(QT_sb[cb][:, h, :], QT[h, :, qc * 512:(qc + 1) * 512]) for h in range(16)], writes=[("QT", cb)])

                def rec_qk(j):
                    st = steps[j]
                    si = j % 3
                    rhs = st["rhs"]
                    P.op("pe", lambda e, st=st, si=si: e.matmul(Sps[si][:], lhsT=st["lhsT"], rhs=st["rhs"], start=True, stop=True),
                         reads=[st["kkey"], ("QT", st["cb"])], writes=[("S", si)])
                    pi = j % NPT
                    P.op("act", lambda e, st=st, si=si, pi=pi: e.activation(out=pT[pi][:], in_=Sps[si][:], func=AF.Exp, bias=st["bias"][:, 0:1], scale=1.0),
                         reads=[("S", si), "negcA", "negcB"], writes=[("pT", pi)])
                    if st["mask"] is not None:
                        mk = st["mask"]
                        P.op("dve", lambda e, pi=pi, mk=mk: e.tensor_tensor(out=pT[pi][:], in0=pT[pi][:], in1=msk[:, mk, :], op=ALU.mult),
                             reads=[("pT", pi), "msk"], writes=[("pT", pi)])

                def rec_pv(j):
                    st = steps[j]
                    pi = j % NPT
                    a = st["acc"]
                    if st["kind"] == "A" or True:
                        def f(e, st=st, pi=pi, a=a):
                            e.matmul(Ops[a][:], lhsT=st["vs"], rhs=pT[pi][:], start=st["first"], stop=st["last"])
                            return e.matmul(Lps[a][:], lhsT=ones_b[:], rhs=pT[pi][:], start=st["first"], stop=st["last"])
                        P.op("pe", f, reads=[("pT", pi), st["vkey"], "ones_b"], writes=[("O", a), ("L", a)])
                    else:
                        P.op("pe", lambda e, st=st, pi=pi, a=a: e.matmul(Ops[a][:], lhsT=st["vs"], rhs=pT[pi][:], start=st["first"], stop=st["last"]),
                             reads=[("pT", pi), st["vkey"]], writes=[("O", a)])
                        sb = st["sb"]
                        if sb % 3 == 2:
                            eng, acc, akey = "pool", accP[a], ("accP", a)
                            first = (sb == 2)
                        else:
                            eng, acc, akey = "dve", accD[a], ("accD", a)
                            first = (sb == 0)
                        if first:
                            P.op(eng, lambda e, acc=acc, pi=pi: e.tensor_copy(out=acc[:], in_=pT[pi][:]), reads=[("pT", pi)], writes=[akey])
                        else:
                            P.op(eng, lambda e, acc=acc, pi=pi: e.tensor_tensor(out=acc[:], in0=acc[:], in1=pT[pi][:], op=ALU.add), reads=[("pT", pi), akey], writes=[akey])
                        if st["last"]:
                            P.op("dve", lambda e, a=a: e.tensor_tensor(out=accD[a][:], in0=accD[a][:], in1=accP[a][:], op=ALU.add),
                                 reads=[("accD", a), ("accP", a)], writes=[("accD", a)])
                            P.op("pe", lambda e, a=a: e.matmul(Lps[a][:], lhsT=ones_f[:], rhs=accD[a][:], start=True, stop=True),
                                 reads=[("accD", a)], writes=[("L", a)])
                    if st["last"]:
                        ri = a
                        cb = st["cb"]
                        if st["kind"] == "B":
                            P.op("dve", lambda e, a=a, ri=ri: e.reciprocal(out=rl[ri][:], in_=Lps[a][:]), reads=[("L", a)], writes=[("rl", ri)])
                            P.op("dve", lambda e, a=a, ri=ri, st=st, cb=cb: e.tensor_tensor(out=OT_sb[cb][:, st["head"], :], in0=Ops[a][:], in1=rl[ri][:], op=ALU.mult),
                                 reads=[("O", a), ("rl", ri)], writes=[("OT", cb, st["head"])])
                        else:
                            g, i = st["g"], st["i"]
                            P.op("dve", lambda e, a=a, ri=ri, g=g: e.tensor_tensor(out=rl[ri][:], in0=Lps[a][:], in1=sinkrow[:, g, :], op=ALU.add),
                                 reads=[("L", a), ("sinkrow", g)], writes=[("rl", ri)])
                            P.op("dve", lambda e, ri=ri: e.reciprocal(out=rl[ri][:], in_=rl[ri][:]), reads=[("rl", ri)], writes=[("rl", ri)])
                            okeys = [("OT", cb, 4 * g + h) for h in range(4)]
                            P.op("dve", lambda e, a=a, ri=ri, g=g, i=i, cb=cb: e.tensor_tensor(
                                out=OT_sb[cb][:, 4 * g:4 * g + 4, i * 128:(i + 1) * 128],
                                in0=Ops[a][:].rearrange("p (h q) -> p h q", h=4),
                                in1=rl[ri][:].rearrange("p (h q) -> p h q", h=4), op=ALU.mult),
                                 reads=[("O", a), ("rl", ri)] + okeys, writes=okeys)
                        if j + 1 == nst or steps[j + 1]["qc"] != st["qc"]:
                            qc = st["qc"]
                            P.dma("pool", ("ot", cb), [(OT[h, :, qc * 512:(qc + 1) * 512], OT_sb[cb][:, h, :]) for h in range(16)],
                                  reads=[("OT", cb, h) for h in range(16)], writes=[("OTd", qc)])

                wv2 = wada_h.ap().rearrange("(k p) c -> p k c", p=128)
                NG2 = 16

                def ada_load(g):
                    if g >= NG2:
                        return
                    r = g % 2
                    c0 = 4096 + g * 512
                    P.dma("pool", ("wr2", r), [(wr2[r][:], wv2[:, :, c0:c0 + 512])], writes=[("wr2", r)])
                    P.dma("sp", ("bada2", r), [(bada2[r][:], bada_h.ap()[0:1, c0:c0 + 512])], writes=[("bada2", r)])

                def ada_group(g):
                    r = g % 2
                    c0 = 4096 + g * 512

                    def mm(e, r=r):
                        ins = None
                        for k in range(16):
                            ins = e.matmul(mps[0:1, :], lhsT=condb2[:, k, 0:1], rhs=wr2[r][:, k, :], start=(k == 0), stop=(k == 15))
                        return ins
                    P.op("pe", mm, reads=[("wr2", r)], writes=["mps"])
                    P.op("dve", lambda e, r=r: e.tensor_tensor(out=mr2[r][:], in0=mps[0:1, :], in1=bada2[r][:], op=ALU.add),
                         reads=["mps", ("bada2", r)], writes=[("mr2", r)])
                    P.dma("sp", ("mr2", r), [(mrow[0:1, c0:c0 + 512], mr2[r][:])], reads=[("mr2", r)], writes=[("mrow2", g)])
                    ada_load(g + 2)

                ada_load(0)
                ada_load(1)
                ada_every = nst // (NG2 + 1)
                load_q(0)
                load_q(1)
                LOOK = 2
                for j in range(min(LOOK, nst)):
                    rec_qk(j)
                for j in range(nst):
                    if j + LOOK < nst:
                        nq = steps[j + LOOK]["qc"]
                        rec_qk(j + LOOK)
                    rec_pv(j)
                    if (j + 1) % ada_every == 0 and (j + 1) // ada_every <= NG2:
                        ada_group((j + 1) // ada_every - 1)
                    if steps[j]["last"] and (j + 1 < nst) and steps[j + 1]["qc"] != steps[j]["qc"]:
                        load_q(steps[j]["qc"] + 2)
                P.emit()

        if 3 in PHASES:
            with contextlib.ExitStack() as es:
                T = lambda name, shape, dt: es.enter_context(nc.sbuf_tensor(_uniq(name), list(shape), dt))
                wout_sb = T("wout_sb", [128, 16, D], BF16)
                OTc = [T(f"OTc{i}", [128, 16, 512], BF16) for i in range(2)]
                xt = [T(f"xt{i}", [128, D], F32) for i in range(2)]
                zt = [T(f"zt{i}", [128, D], F32) for i in range(3)]
                g1b = T("g1b", [128, D], F32)
                lg = T("lg", [128, D], F32)
                lb = T("lb", [128, D], F32)
                x1Tc = [T(f"x1Tc{i}", [128, 16, 512], BF16) for i in range(2)]
                st6 = [T(f"st6{i}", [128, 4, 6], F32) for i in range(2)]
                mv = [T(f"mv{i}", [128, 4], F32) for i in range(2)]
                aps = [es.enter_context(nc.psum_tensor(_uniq(f"aps{i}"), [128, 512], F32)) for i in range(6)]
                tps = [es.enter_context(nc.psum_tensor(_uniq(f"tps{i}"), [128, 512], F32)) for i in range(2)]
                P = Phase(nc, "p3")
                wo_v = wout_h.ap().rearrange("(k p) c -> p k c", p=128)
                P.dma("pool", "wo", [(wout_sb[:, 4 * k:4 * k + 4, :], wo_v[:, 4 * k:4 * k + 4, :]) for k in range(4)], writes=["wo"])
                P.dma("sp", "bc", [(g1b[:], bc(mrow_h, 2 * D, D)), (lg[:], bc(lnp_h, 0, D)), (lb[:], bc(lnp_h, D, D))], writes=["g1b", "lg", "lb"])
                P.dma("sp", "m2", [(modT[:, 32:96], mrow[0:1, 4096:12288].rearrange("o (j p) -> p (o j)", p=128))], writes=["modT"], slow=True)
                P.op("dve", lambda e: e.tensor_scalar(out=modT[:, 64:80], in0=modT[:, 64:80], scalar1=1.0, scalar2=None, op0=ALU.add), reads=["modT"], writes=["modT"])
                _ln_phase(nc, P, 16, lambda tt: x[tt * 128:(tt + 1) * 128, :], OT, OTc, wout_sb, "wo", xt, zt, g1b, lg, lb, st6, mv, aps, tps,
                          X1, X1T, x1Tc, modT, ident, epst)
                P.emit()

        if 4 in PHASES:
            with contextlib.ExitStack() as es4:
                hT = es4.enter_context(nc.sbuf_tensor(_uniq("hT"), [128, NF, 1024], BF16))
                for half in range(2):
                    t0 = half * 1024
                    with contextlib.ExitStack() as es:
                        T = lambda name, shape, dt: es.enter_context(nc.sbuf_tensor(_uniq(name), list(shape), dt))
                        x1T_sb = T("x1T_sb", [128, 16, 1024], BF16)
                        wgr = [T(f"wgr{i}", [128, 16, 128], BF16) for i in range(4)]
                        wur = [T(f"wur{i}", [128, 16, 128], BF16) for i in range(4)]
                        sg = [T(f"sg{i}", [128, 512], F32) for i in range(2)]
                        Gp = [es.enter_context(nc.psum_tensor(_uniq(f"Gp{i}"), [128, 512], F32)) for i in range(4)]
                        Up = [es.enter_context(nc.psum_tensor(_uniq(f"Up{i}"), [128, 512], F32)) for i in range(4)]
                        P = Phase(nc, f"p4a{half}")
                        xs1 = [T(f"xs1{i}", [128, D], F32) for i in range(2)]
                        evn = 0
                        tpn = 0
                        for tl in range(8):
                            xi = tl % 2
                            P.dma("sp", ("xs1", xi), [(xs1[xi][:], X1[t0 + tl * 128:t0 + (tl + 1) * 128, :])], writes=[("xs1", xi)])
                            for k in range(0, 16, 4):
                                ti = tpn % 4
                                tpn += 1

                                def trf(e, k=k, ti=ti, xi=xi):
                                    ins = None
                                    for kk in range(4):
                                        ins = e.transpose(out=Gp[ti][:, kk * 128:(kk + 1) * 128], in_=xs1[xi][:, (k + kk) * 128:(k + kk + 1) * 128], identity=ident[:])
                                    return ins
                                P.op("pe", trf, reads=[("xs1", xi)], writes=[("G", ti)])
                                for kk in range(4):
                                    kq = k + kk
                                    sc_ap = modT[:, 64 + kq:65 + kq]
                                    sh_ap = modT[:, 48 + kq:49 + kq]
                                    dst = x1T_sb[:, kq, tl * 128:(tl + 1) * 128]
                                    if tpn % 2 == 0:
                                        P.op("act", lambda e, ti=ti, kk=kk, sc_ap=sc_ap, sh_ap=sh_ap, dst=dst: e.activation(out=dst, in_=Gp[ti][:, kk * 128:(kk + 1) * 128], func=AF.Identity, bias=sh_ap, scale=sc_ap),
                                             reads=[("G", ti)], writes=[("x1T", tl, kq)])
                                    else:
                                        P.op("dve", lambda e, ti=ti, kk=kk, sc_ap=sc_ap, sh_ap=sh_ap, dst=dst: e.tensor_scalar(out=dst, in0=Gp[ti][:, kk * 128:(kk + 1) * 128], scalar1=sc_ap, scalar2=sh_ap, op0=ALU.mult, op1=ALU.add),
                                             reads=[("G", ti)], writes=[("x1T", tl, kq)])
                                    evn += 1
                        x1keys = {tc_: [("x1T", tl, kq) for tl in range(tc_ * 4, tc_ * 4 + 4) for kq in range(16)] for tc_ in range(2)}

                        def loadw(f):
                            r = f % 4
                            P.dma("pool", ("wg", r), [(wgr[r][:], wg_h.ap()[f]), (wur[r][:], wu_h.ap()[f])], writes=[("wg", r)])
                        for f in range(3):
                            loadw(f)
                        k_ = 0
                        for f in range(NF):
                            r = f % 4
                            for tc_ in range(2):
                                pi = (f * 2 + tc_) % 4

                                def mg(e, r=r, tc_=tc_, pi=pi):
                                    ins = None
                                    for k in range(16):
                                        ins = e.matmul(Gp[pi][:], lhsT=wgr[r][:, k, :], rhs=x1T_sb[:, k, tc_ * 512:(tc_ + 1) * 512], start=(k == 0), stop=(k == 15))
                                    return ins

                                def mu(e, r=r, tc_=tc_, pi=pi):
                                    ins = None
                                    for k in range(16):
                                        ins = e.matmul(Up[pi][:], lhsT=wur[r][:, k, :], rhs=x1T_sb[:, k, tc_ * 512:(tc_ + 1) * 512], start=(k == 0), stop=(k == 15))
                                    return ins
                                P.op("pe", mg, reads=[("wg", r)] + x1keys[tc_], writes=[("G", pi)])
                                P.op("pe", mu, reads=[("wg", r)] + x1keys[tc_], writes=[("U", pi)])
                                si = k_ % 2
                                k_ += 1
                                P.op("act", lambda e, si=si, pi=pi: e.activation(out=sg[si][:], in_=Gp[pi][:], func=AF.Silu), reads=[("G", pi)], writes=[("sg", si)])
                                P.op("dve", lambda e, si=si, pi=pi, f=f, tc_=tc_: e.tensor_tensor(out=hT[:, f, tc_ * 512:(tc_ + 1) * 512], in0=Up[pi][:], in1=sg[si][:], op=ALU.mult),
                                     reads=[("U", pi), ("sg", si)], writes=[("hT", f, tc_)])
                            if f + 3 < NF:
                                loadw(f + 3)
                        P.emit()
                    with contextlib.ExitStack() as es:
                        T = lambda name, shape, dt: es.enter_context(nc.sbuf_tensor(_uniq(name), list(shape), dt))
                        wdr = [T(f"wdr{i}", [128, NF, 512], BF16) for i in range(2)]
                        x1p = [T(f"x1p{i}", [128, 512], F32) for i in range(2)]
                        zp = [T(f"zp{i}", [128, 512], F32) for i in range(2)]
                        g2b = T("g2b", [128, D], F32)
                        Yp = [es.enter_context(nc.psum_tensor(_uniq(f"Yp{i}"), [128, 512], F32)) for i in range(4)]
                        P = Phase(nc, f"p4b{half}")
                        P.dma("sp", "g2b", [(g2b[:], bc(mrow_h, 5 * D, D))], writes=["g2b"])

                        def loadwd(dg):
                            r = dg % 2
                            for q4 in range(4):
                                P.dma("pool", ("wd", r, q4), [(wdr[r][:, q4 * 11:(q4 + 1) * 11, :], wd_h.ap()[dg][:, q4 * 11:(q4 + 1) * 11, :])], writes=[("wd", r, q4)])
                        loadwd(0)
                        loadwd(1)
                        n_ = 0
                        for dg in range(4):
                            r = dg % 2
                            for tl in range(8):
                                yi = n_ % 4
                                xi = n_ % 2
                                n_ += 1
                                row0 = t0 + tl * 128
                                P.dma("sp", ("x1p", xi), [(x1p[xi][:], X1[row0:row0 + 128, dg * 512:(dg + 1) * 512])], writes=[("x1p", xi)])

                                def md(e, r=r, tl=tl, yi=yi):
                                    ins = None
                                    for fk in range(NF):
                                        ins = e.matmul(Yp[yi][:], lhsT=hT[:, fk, tl * 128:(tl + 1) * 128], rhs=wdr[r][:, fk, :], start=(fk == 0), stop=(fk == NF - 1))
                                    return ins
                                P.op("pe", md, reads=[("wd", r, q4) for q4 in range(4)], writes=[("Y", yi)])
                                P.op("dve", lambda e, xi=xi, yi=yi, dg=dg: e.tensor_tensor(out=zp[xi][:], in0=Yp[yi][:], in1=g2b[:, dg * 512:(dg + 1) * 512], op=ALU.mult),
                                     reads=[("Y", yi), "g2b"], writes=[("zp", xi)])
                                P.op("dve", lambda e, xi=xi: e.scalar_tensor_tensor(out=zp[xi][:], in0=x1p[xi][:], scalar=DN_ALPHA, in1=zp[xi][:], op0=ALU.mult, op1=ALU.add),
                                     reads=[("zp", xi), ("x1p", xi)], writes=[("zp", xi)])
                                P.dma("sp", ("zp", xi), [(Z[row0:row0 + 128, dg * 512:(dg + 1) * 512], zp[xi][:])], reads=[("zp", xi)], writes=[("Z", n_)])
                            if dg + 2 < 4:
                                loadwd(dg + 2)
                        P.emit()

        if 5 in PHASES:
            with contextlib.ExitStack() as es:
                T = lambda name, shape, dt: es.enter_context(nc.sbuf_tensor(_uniq(name), list(shape), dt))
                zt = [T(f"z5{i}", [128, D], F32) for i in range(4)]
                lg = T("lg5", [128, D], F32)
                lb = T("lb5", [128, D], F32)
                st6 = [T(f"st65{i}", [128, 4, 6], F32) for i in range(2)]
                mv = [T(f"mv5{i}", [128, 4], F32) for i in range(2)]
                P = Phase(nc, "p5")
                P.dma("sp", "bc", [(lg[:], bc(lnp_h, 2 * D, D)), (lb[:], bc(lnp_h, 3 * D, D))], writes=["lg", "lb"])

                def ld5(tt):
                    if tt < 16:
                        P.dma("sp", ("z", tt % 4), [(zt[tt % 4][:], Z[tt * 128:(tt + 1) * 128, :])], writes=[("z", tt % 4)])

                def st5(tt):
                    zi = tt % 4
                    _ln_bias(P, zt[zi], [("z", zi)], lb)
                    P.dma("sp", ("z", zi), [(out[tt * 128:(tt + 1) * 128, :], zt[zi][:])], reads=[("z", zi)], writes=[("out", tt)])
                for tt in range(3):
                    ld5(tt)
                for tt in range(16):
                    zi = tt % 4
                    _layer_norm(P, zt[zi], [("z", zi)], st6[tt % 2], mv[tt % 2], tt % 2, lg, lb, epst, defer_bias=True)
                    if tt >= 1:
                        st5(tt - 1)
                    ld5(tt + 3)
                st5(15)
                P.emit()
    return nc


def _layer_norm(P, z, zkeys, st6, mv, si, lg, lb, epst, defer_bias=False):
    zkeys = list(zkeys)
    def stats(e):
        ins = None
        for q in range(4):
            ins = e.bn_stats(out=st6[:, q, :], in_=z[:, q * 512:(q + 1) * 512])
        return ins
    P.op("dve", stats, reads=zkeys, writes=[("st6", si)])
    P.op("dve", lambda e: e.bn_aggr(out=mv[:, 0:2], in_=st6[:].rearrange("p a b -> p (a b)")), reads=[("st6", si)], writes=[("mv", si)])
    P.op("act", lambda e: e.activation(out=mv[:, 2:3], in_=mv[:, 1:2], func=AF.Sqrt, bias=epst[:, 0:1], scale=1.0), reads=[("mv", si)], writes=[("mv2", si)])
    P.op("dve", lambda e: e.reciprocal(out=mv[:, 2:3], in_=mv[:, 2:3]), reads=[("mv2", si)], writes=[("mv2", si)])
    P.op("dve", lambda e: e.scalar_tensor_tensor(out=mv[:, 3:4], in0=mv[:, 0:1], scalar=-1.0, in1=mv[:, 2:3], op0=ALU.mult, op1=ALU.mult),
         reads=[("mv", si), ("mv2", si)], writes=[("mv3", si)])
    P.op("act", lambda e: e.activation(out=z[:], in_=z[:], func=AF.Identity, bias=mv[:, 3:4], scale=mv[:, 2:3]),
         reads=zkeys + [("mv2", si), ("mv3", si)], writes=zkeys)
    P.op("pool", lambda e: e.tensor_tensor(out=z[:], in0=z[:], in1=lg[:], op=ALU.mult), reads=zkeys + ["lg"], writes=zkeys)
    if not defer_bias:
        _ln_bias(P, z, zkeys, lb)


def _ln_bias(P, z, zkeys, lb):
    zkeys = list(zkeys)
    P.op("dve", lambda e: e.tensor_tensor(out=z[:], in0=z[:], in1=lb[:], op=ALU.add), reads=zkeys + ["lb"], writes=zkeys)


def _ln_phase(nc, P, ntiles, xsrc, OT, OTc, wout_sb, wkey, xt, zt, g1b, lg, lb, st6, mv, aps, tps, X1, X1T, x1Tc, modT, ident, epst):
    nb = [0]

    def stage_a(tt):
        qc, tl = tt // 4, tt % 4
        cb = qc % 2
        if tt == 0:
            P.dma("sp", ("otc", 0), [(OTc[0][:, h, :], OT[h, :, 0:512]) for h in range(16)], writes=[("OTc", 0)])
        if tl == 1 and qc + 1 < ntiles // 4:
            nq = qc + 1
            P.dma("sp", ("otc", nq % 2), [(OTc[nq % 2][:, h, :], OT[h, :, nq * 512:(nq + 1) * 512]) for h in range(16)], writes=[("OTc", nq % 2)])
        xi = tt % 2
        zi = tt % 3
        if tt == 0:
            P.dma("sp", ("xt", 0), [(xt[0][:], xsrc(0))], writes=[("xt", 0)])
        if tt + 1 < ntiles:
            P.dma("sp", ("xt", (tt + 1) % 2), [(xt[(tt + 1) % 2][:], xsrc(tt + 1))], writes=[("xt", (tt + 1) % 2)])
        z = zt[zi]
        zkeys = [("z", zi, dg) for dg in range(4)]
        for dg in range(4):
            bi = nb[0] % 6
            nb[0] += 1

            def mo(e, bi=bi, dg=dg, cb=cb, tl=tl):
                ins = None
                for h in range(16):
                    ins = e.matmul(aps[bi][:], lhsT=OTc[cb][:, h, tl * 128:(tl + 1) * 128], rhs=wout_sb[:, h, dg * 512:(dg + 1) * 512], start=(h == 0), stop=(h == 15))
                return ins
            P.op("pe", mo, reads=[("OTc", cb), wkey], writes=[("aps", bi)])
            P.op("dve", lambda e, bi=bi, dg=dg, z=z: e.tensor_tensor(out=z[:, dg * 512:(dg + 1) * 512], in0=aps[bi][:], in1=g1b[:, dg * 512:(dg + 1) * 512], op=ALU.mult),
                 reads=[("aps", bi), "g1b"], writes=[("z", zi, dg)])
        P.op("dve", lambda e, z=z, xi=xi: e.scalar_tensor_tensor(out=z[:], in0=xt[xi][:], scalar=DN_ALPHA, in1=z[:], op0=ALU.mult, op1=ALU.add),
             reads=[("xt", xi)] + zkeys, writes=zkeys)
        si = tt % 2
        _layer_norm(P, z, zkeys, st6[si], mv[si], si, lg, lb, epst, defer_bias=True)

    def stage_b(tt):
        zi = tt % 3
        z = zt[zi]
        zkeys = [("z", zi, dg) for dg in range(4)]
        _ln_bias(P, z, zkeys, lb)
        P.dma("sp", ("zst", zi), [(X1[tt * 128:(tt + 1) * 128, :], z[:])], reads=zkeys, writes=[("X1", tt)])

    for tt in range(ntiles):
        stage_a(tt)
        if tt >= 1:
            stage_b(tt - 1)
    stage_b(ntiles - 1)


def _rope_tables():
    rows = SEQ // GRID_W
    row_ids = np.repeat(np.arange(rows, dtype=np.float64), GRID_W)
    col_ids = np.tile(np.arange(GRID_W, dtype=np.float64), rows)
    axis_dim = 64
    inv_freq = np.power(10000.0, -np.arange(0, axis_dim, 2, dtype=np.float64) / axis_dim)
    ang_r = row_ids[:, None] * inv_freq
    ang_c = col_ids[:, None] * inv_freq
    ang = np.concatenate([ang_r, ang_r, ang_c, ang_c], axis=-1)
    return np.cos(ang).astype(np.float32), np.sin(ang).astype(np.float32)


def _consts():
    ident = np.eye(128, dtype=np.float32)
    rotm = np.zeros((128, 128), np.float32)
    for base in (0, 64):
        for j in range(32):
            rotm[base + 32 + j, base + j] = -1.0
            rotm[base + j, base + 32 + j] = 1.0
    jj = np.arange(128)[:, None]
    ii = np.arange(128)[None, :]
    m_prev = (jj >= ii).astype(np.float32)
    m_next = (jj <= ii).astype(np.float32)
    return ident, rotm, m_prev, m_next


_NC_CACHE = {}


def make_in_maps(x, c, ctx, c_ctx, w_ada, b_ada, w_in, q_norm_g, k_norm_g, sink_logit,
                 w_out, ln1_g, ln1_b, w_gate, w_up, w_down, ln2_g, ln2_b):
    f = lambda a: np.ascontiguousarray(np.asarray(a, dtype=np.float32))
    x, c, ctx, c_ctx = f(x), f(c), f(ctx), f(c_ctx)
    w_ada, b_ada, w_in = f(w_ada)[0], f(b_ada), f(w_in)[0]
    w_out, w_gate, w_up, w_down = f(w_out)[0], f(w_gate)[0], f(w_up)[0], f(w_down)[0]
    qg, kg = f(q_norm_g)[0], f(k_norm_g)[0]
    cosN, sinN = _rope_tables()
    ident, rotm, m_prev, m_next = _consts()
    wq = np.ascontiguousarray(np.concatenate([w_in[:, 0:1024], w_in[:, 1536:2560]], axis=1))
    wkv = np.ascontiguousarray(np.concatenate([w_in[:, 1024:1280], w_in[:, 2560:2816], w_in[:, 1280:1536], w_in[:, 2816:3072]], axis=1))
    wg = np.ascontiguousarray(w_gate.reshape(16, 128, NF, 128).transpose(2, 1, 0, 3))
    wu = np.ascontiguousarray(w_up.reshape(16, 128, NF, 128).transpose(2, 1, 0, 3))
    wd = np.ascontiguousarray(w_down.reshape(NF, 128, 4, 512).transpose(2, 1, 0, 3))
    gcol = np.ascontiguousarray(np.stack([qg, kg], axis=1))
    grow = np.ascontiguousarray(np.concatenate([qg, kg])[None, :])
    sink = f(sink_logit).reshape(1, 8)
    lnp = np.ascontiguousarray(np.stack([f(ln1_g)[0], f(ln1_b)[0], f(ln2_g)[0], f(ln2_b)[0]], axis=0))
    rep4 = lambda m: np.tile(m, (1, 4))
    zeros = np.zeros((128, 512), np.float32)
    in_maps = []
    for core in range(8):
        b, hf = core // 2, core % 2
        own = slice(hf * NOWN, (hf + 1) * NOWN)
        oth = slice((1 - hf) * NOWN, (2 - hf) * NOWN)
        xp = np.ascontiguousarray(np.concatenate([x[b, own], x[b, oth]], axis=0))
        cosT = np.ascontiguousarray(np.concatenate([cosN[own], cosN[oth]], axis=0).T)
        sinT = np.ascontiguousarray(np.concatenate([sinN[own], sinN[oth]], axis=0).T)
        cond = np.stack([c[b], c_ctx], axis=1)
        condT = np.ascontiguousarray(cond.reshape(16, 128, 2).transpose(1, 0, 2))
        masks = np.stack([rep4(m_prev), rep4(m_next),
                          rep4(m_prev) if hf == 1 else zeros,
                          rep4(m_next) if hf == 0 else zeros], axis=0)
        in_maps.append({
            "x": xp, "ctx": np.ascontiguousarray(ctx[b]), "condT": condT, "w_ada": w_ada, "b_ada": b_ada.reshape(1, -1),
            "wq": wq, "wkv": wkv, "w_out": w_out, "wg": wg, "wu": wu, "wd": wd,
            "cosT": cosT, "sinT": sinT, "ident": ident, "rotm": rotm, "gcol": gcol, "grow": grow,
            "sink": sink, "lnp": lnp, "masks": np.ascontiguousarray(masks),
        })
    return in_maps


def kernel(x, c, ctx, c_ctx, w_ada, b_ada, w_in, q_norm_g, k_norm_g, sink_logit,
           w_out, ln1_g, ln1_b, w_gate, w_up, w_down, ln2_g, ln2_b):
    in_maps = make_in_maps(x, c, ctx, c_ctx, w_ada, b_ada, w_in, q_norm_g, k_norm_g, sink_logit,
                           w_out, ln1_g, ln1_b, w_gate, w_up, w_down, ln2_g, ln2_b)
    nc = build_program()
    res = run_bass_kernel_spmd(nc, in_maps, core_ids=list(range(8)))
    outp = np.empty((4, SEQ, D), np.float32)
    for core in range(8):
        b, hf = core // 2, core % 2
        outp[b, hf * NOWN:(hf + 1) * NOWN] = res.results[core]["out"]
    return outp
```

```python
import contextlib
import numpy as np
import concourse.bass as bass
import concourse.mybir as mybir
from concourse.bass_utils import run_bass_kernel_spmd

F32 = mybir.dt.float32
BF16 = mybir.dt.bfloat16
ALU = mybir.AluOpType
AF = mybir.ActivationFunctionType
AX = mybir.AxisListType

D = 2048
SEQ = 4096
NOWN = 2048
CTXL = 256
NKEY = SEQ + CTXL
FF = 5632
NF = FF // 128
EPS = 1e-6
SCALE = 128 ** -0.5
DN_ALPHA = 2.0 ** 0.25
GRID_W = 64

DEBUG = False
PHASES = (0, 1, 2, 3, 4, 5)

COMPUTE = ("pe", "act", "dve", "pool")


class Op:
    __slots__ = ("eng", "fn", "deps", "signal", "sem", "val", "is_dma", "semkey", "ndma")

    def __init__(self, eng, fn, is_dma=False, semkey=None, ndma=0):
        self.eng = eng
        self.fn = fn
        self.deps = set()
        self.signal = False
        self.sem = None
        self.val = 0
        self.is_dma = is_dma
        self.semkey = semkey
        self.ndma = ndma


class Phase:
    def __init__(self, nc, name):
        self.nc = nc
        self.name = name
        self.eng_ops = {e: [] for e in ("pe", "act", "dve", "pool", "sp")}
        self.last_writer = {}
        self.readers = {}

    def _add(self, o, reads, writes):
        deps = o.deps
        lw = self.last_writer
        rd = self.readers
        for k in reads:
            w = lw.get(k)
            if w is not None:
                deps.add(w)
        for k in writes:
            w = lw.get(k)
            if w is not None:
                deps.add(w)
            r = rd.get(k)
            if r:
                deps.update(r)
        deps.discard(o)
        for k in reads:
            rd.setdefault(k, []).append(o)
        for k in writes:
            lw[k] = o
            rd[k] = []
        for d in deps:
            if d.eng == "pe" and o.eng == "pe" and not d.is_dma and not o.is_dma:
                continue
            d.signal = True
        self.eng_ops[o.eng].append(o)
        return o

    def op(self, eng, fn, reads=(), writes=()):
        return self._add(Op(eng, fn), reads, writes)

    def dma(self, eng, semkey, pairs, reads=(), writes=(), slow=False):
        pairs = list(pairs)

        def fn(e, pairs=pairs, slow=slow):
            if slow:
                return [e.dma_start(out=o_, in_=i_, allow_slow_non_contiguous=True) for (o_, i_) in pairs]
            return [e.dma_start(out=o_, in_=i_) for (o_, i_) in pairs]

        o = Op(eng, fn, is_dma=True, semkey=semkey, ndma=len(pairs))
        o.signal = True
        return self._add(o, reads, writes)

    semstack = None
    dsem = {}
    dma_counts = {}

    def emit(self):
        nc = self.nc
        dma_counts = Phase.dma_counts
        dsem = Phase.dsem
        slots = {}
        for e, ops in self.eng_ops.items():
            for o in ops:
                if o.is_dma:
                    o.semkey = ("slot", slots.setdefault(o.semkey, len(slots)))
        for e, ops in self.eng_ops.items():
            cnt = 0
            for o in ops:
                if o.is_dma:
                    if o.semkey not in dma_counts:
                        dma_counts[o.semkey] = 0
                        dsem[o.semkey] = Phase.semstack.enter_context(nc.semaphore(f"d{len(dsem)}"))
                    dma_counts[o.semkey] += 16 * o.ndma
                    o.val = dma_counts[o.semkey]
                elif o.signal:
                    cnt += 1
                    o.val = cnt
        with contextlib.ExitStack() as es:
            esem = {e: Phase.semstack.enter_context(nc.semaphore(f"{self.name}_{e}")) for e in COMPUTE}
            for e, ops in self.eng_ops.items():
                for o in ops:
                    o.sem = dsem[o.semkey] if o.is_dma else esem[e]
            block = es.enter_context(nc.Block())
            eng_ops = self.eng_ops

            def run(e, engname):
                waited = {}
                for o in eng_ops[engname]:
                    need = {}
                    for d in o.deps:
                        if (not d.is_dma) and d.eng == "pe" and engname == "pe" and not o.is_dma:
                            continue
                        s = d.sem
                        if d.val > need.get(s, 0):
                            need[s] = d.val
                    for s, v in need.items():
                        if waited.get(s, 0) < v:
                            e.wait_ge(s, v)
                            waited[s] = v
                    r = o.fn(e)
                    if o.is_dma:
                        for ins in r:
                            ins.then_inc(o.sem, 16)
                    elif o.signal:
                        r.then_inc(o.sem, 1)
                last = {}
                for o in eng_ops[engname]:
                    if o.is_dma:
                        last[o.sem] = o.val
                for s, v in last.items():
                    if waited.get(s, 0) < v:
                        e.wait_ge(s, v)

            @block.tensor
            def _(e):
                run(e, "pe")

            @block.scalar
            def _(e):
                run(e, "act")

            @block.vector
            def _(e):
                run(e, "dve")

            @block.gpsimd
            def _(e):
                run(e, "pool")

            @block.sync
            def _(e):
                run(e, "sp")


_UNIQ = [0]


def _uniq(name):
    _UNIQ[0] += 1
    return f"s{_UNIQ[0]}_{name}"


def build_program():
    nc = bass.Bass("TRN2", target_bir_lowering=False)

    def din(name, shape, dt=F32):
        return nc.dram_tensor(name, list(shape), dt, kind="ExternalInput")

    def dscr(name, shape, dt):
        return nc.dram_tensor(name, list(shape), dt, kind="ExternalOutput" if DEBUG else "Internal")

    x_h = din("x", [SEQ, D])
    ctx_h = din("ctx", [CTXL, D])
    condT_h = din("condT", [128, 16, 2])
    wada_h = din("w_ada", [D, 6 * D])
    bada_h = din("b_ada", [1, 6 * D])
    wq_h = din("wq", [D, 2048])
    wkv_h = din("wkv", [D, 1024])
    wout_h = din("w_out", [D, D])
    wg_h = din("wg", [NF, 128, 16, 128])
    wu_h = din("wu", [NF, 128, 16, 128])
    wd_h = din("wd", [4, 128, NF, 512])
    cos_h = din("cosT", [128, SEQ])
    sin_h = din("sinT", [128, SEQ])
    ident_h = din("ident", [128, 128])
    rotm_h = din("rotm", [128, 128])
    gcol_h = din("gcol", [128, 2])
    grow_h = din("grow", [1, 256])
    sink_h = din("sink", [1, 8])
    lnp_h = din("lnp", [4, D])
    masks_h = din("masks", [4, 128, 512])
    out_h = nc.dram_tensor("out", [NOWN, D], F32, kind="ExternalOutput")

    mrow_h = dscr("mrow", [2, 6 * D], F32)
    QT_h = dscr("QT", [16, 128, NOWN], BF16)
    KT_h = dscr("KT", [4, 128, NKEY], BF16)
    Vd_h = dscr("Vd", [NKEY, 512], BF16)
    OT_h = dscr("OT", [16, 128, NOWN], BF16)
    X1_h = dscr("X1", [NOWN, D], F32)
    X1T_h = dscr("X1T", [128, 16, NOWN], BF16)
    Z_h = dscr("Z", [NOWN, D], F32)

    x = x_h.ap()
    ctx = ctx_h.ap()
    out = out_h.ap()
    QT = QT_h.ap()
    KT = KT_h.ap()
    Vd = Vd_h.ap()
    OT = OT_h.ap()
    X1 = X1_h.ap()
    X1T = X1T_h.ap()
    Z = Z_h.ap()
    mrow = mrow_h.ap()

    def bc(handle, row_off, n, parts=128):
        return bass.AP(handle, row_off, [[0, parts], [1, n]])

    with contextlib.ExitStack() as top:
        Phase.semstack = top
        Phase.dsem = {}
        Phase.dma_counts = {}
        S = lambda name, shape, dt: top.enter_context(nc.sbuf_tensor(_uniq(name), list(shape), dt))
        ident = S("ident", [128, 128], F32)
        rotm = S("rotm", [128, 128], F32)
        onesd = S("onesd", [128, 128], F32)
        ones_f = S("ones_f", [128, 128], F32)
        ones_b = S("ones_b", [128, 128], BF16)
        modT = S("modT", [128, 128], F32)
        gcol = S("gcol", [128, 2], F32)
        gqs = S("gqs", [128, 1], F32)
        nrmmax = S("nrmmax", [1, 2], F32)
        negcA = S("negcA", [128, 1], F32)
        negcB = S("negcB", [128, 1], F32)
        sinkb = S("sinkb", [128, 8], F32)
        epst = S("epst", [128, 1], F32)
        condb2 = S("condb2", [128, 16, 1], BF16)
        rotb = S("rotb", [128, 128], BF16)
        onesdb = S("onesdb", [128, 128], BF16)

        if 0 in PHASES:
            with contextlib.ExitStack() as es:
                T = lambda name, shape, dt: es.enter_context(nc.sbuf_tensor(_uniq(name), list(shape), dt))
                condf = T("condf", [128, 16, 2], F32)
                condb = T("condb", [128, 16, 2], F32)
                wring = [T(f"wring{i}", [128, 16, 512], F32) for i in range(3)]
                mrow_sb = T("mrow_sb", [2, 2 * D], F32)
                bada_sb = T("bada_sb", [2, 2 * D], F32)
                grow = T("grow", [1, 256], F32)
                gtmp = T("gtmp", [1, 4], F32)
                ps = [es.enter_context(nc.psum_tensor(_uniq(f"p0ps{i}"), [128, 512], F32)) for i in range(2)]
                P = Phase(nc, "p0")
                P.dma("sp", "c0", [(ident[:], ident_h.ap()), (rotm[:], rotm_h.ap()), (gcol[:], gcol_h.ap()),
                                   (condf[:], condT_h.ap()), (grow[:], grow_h.ap())],
                      writes=["ident", "rotm", "gcol", "condf", "grow"])
                P.dma("sp", "c1", [(bada_sb[:], bc(bada_h, 0, 2 * D, 2))], writes=["bada"])
                P.op("dve", lambda e: e.memset(onesd[:], 1.0 / 128.0), writes=["onesd"])
                P.op("dve", lambda e: e.memset(onesdb[:], 1.0 / 128.0), writes=["onesdb"])
                P.op("dve", lambda e: e.tensor_copy(out=rotb[:], in_=rotm[:]), reads=["rotm"], writes=["rotb"])
                P.op("dve", lambda e: e.memset(ones_f[:], 1.0), writes=["ones_f"])
                P.op("dve", lambda e: e.memset(ones_b[:], 1.0), writes=["ones_b"])
                P.op("dve", lambda e: e.memset(epst[:], EPS), writes=["epst"])
                P.op("dve", lambda e: e.memset(nrmmax[:], 0.0), writes=["nrmmax"])
                P.op("dve", lambda e: e.tensor_scalar(out=gqs[:], in0=gcol[:, 0:1], scalar1=SCALE, scalar2=None, op0=ALU.mult),
                     reads=["gcol"], writes=["gqs"])
                P.op("dve", lambda e: e.tensor_reduce(out=gtmp[:, 0:1], in_=grow[:, 0:128], axis=AX.X, op=ALU.max, apply_absolute_value=True),
                     reads=["grow"], writes=["gtmp0"])
                P.op("dve", lambda e: e.tensor_reduce(out=gtmp[:, 1:2], in_=grow[:, 128:256], axis=AX.X, op=ALU.max, apply_absolute_value=True),
                     reads=["grow"], writes=["gtmp1"])
                P.op("dve", lambda e: e.tensor_tensor(out=gtmp[:, 2:3], in0=gtmp[:, 0:1], in1=gtmp[:, 1:2], op=ALU.mult),
                     reads=["gtmp0", "gtmp1"], writes=["gtmp2"])
                P.op("dve", lambda e: e.tensor_scalar(out=gtmp[:, 3:4], in0=gtmp[:, 2:3], scalar1=-128.0 * SCALE * 1.001, scalar2=None, op0=ALU.mult),
                     reads=["gtmp2"], writes=["gtmp3"])
                P.op("pe", lambda e: e.matmul(ps[1][:, 0:1], lhsT=ones_f[0:1, :], rhs=gtmp[:, 3:4], start=True, stop=True),
                     reads=["ones_f", "gtmp3"], writes=["ps1"])
                P.op("dve", lambda e: e.tensor_copy(out=negcB[:], in_=ps[1][:, 0:1]), reads=["ps1"], writes=["negcB"])
                P.op("act", lambda e: e.activation(out=condb[:], in_=condf[:], func=AF.Silu), reads=["condf"], writes=["condb"])
                P.op("act", lambda e: e.activation(out=condb2[:], in_=condf[:, :, 0:1], func=AF.Silu), reads=["condf"], writes=["condb2"])
                wv = wada_h.ap().rearrange("(k p) c -> p k c", p=128)
                NG = 8
                for g in range(NG):
                    r = g % 3
                    P.dma("sp" if g % 2 == 0 else "pool", ("wr", r, g % 2), [(wring[r][:], wv[:, :, g * 512:(g + 1) * 512])], writes=[("wr", r)])

                    def mm(e, r=r):
                        ins = None
                        for k in range(16):
                            ins = e.matmul(ps[0][0:2, :], lhsT=condb[:, k, :], rhs=wring[r][:, k, :], start=(k == 0), stop=(k == 15))
                        return ins
                    P.op("pe", mm, reads=["condb", ("wr", r)], writes=["ps0"])
                    P.op("dve", lambda e, g=g: e.tensor_tensor(out=mrow_sb[:, g * 512:(g + 1) * 512], in0=ps[0][0:2, :],
                                                              in1=bada_sb[:, g * 512:(g + 1) * 512], op=ALU.add),
                         reads=["ps0", "bada"], writes=[("mrow_sb", g)])
                P.dma("sp", "m0", [(mrow[:, 0:2 * D], mrow_sb[:])], reads=[("mrow_sb", g) for g in range(NG)], writes=["mrow"])
                P.dma("sp", "m1", [(modT[:, 0:32], mrow[0:1, 0:4096].rearrange("o (j p) -> p (o j)", p=128)),
                                   (modT[:, 96:128], mrow[1:2, 0:4096].rearrange("o (j p) -> p (o j)", p=128))],
                      reads=["mrow"], writes=["modT"], slow=True)
                for (a, b) in ((16, 32), (112, 128)):
                    P.op("dve", lambda e, a=a, b=b: e.tensor_scalar(out=modT[:, a:b], in0=modT[:, a:b], scalar1=1.0, scalar2=None, op0=ALU.add),
                         reads=["modT"], writes=["modT"])
                P.emit()

        if 1 in PHASES:
            with contextlib.ExitStack() as es:
                T = lambda name, shape, dt: es.enter_context(nc.sbuf_tensor(_uniq(name), list(shape), dt))
                wq_sb = T("wq_sb", [128, 16, 2048], BF16)
                wkv_sb = T("wkv_sb", [128, 16, 1024], BF16)
                xst = [T(f"xst{i}", [128, D], F32) for i in range(4)]
                xT = [T(f"xT{i}", [128, 16, 512], BF16) for i in range(2)]
                cs = [T(f"cs{i}", [128, 2, 512], F32) for i in range(2)]
                yb = [T(f"yb{i}", [128, 512], F32) for i in range(3)]
                sqf = [T(f"sqf{i}", [128, 512], BF16) for i in range(2)]
                ybb = [T(f"ybb{i}", [128, 512], BF16) for i in range(3)]
                sqb = [T(f"sqb{i}", [128, 512], BF16) for i in range(2)]
                t1 = [T(f"t1{i}", [128, 512], F32) for i in range(2)]
                t2 = [T(f"t2{i}", [128, 512], F32) for i in range(2)]
                rr = [T(f"rr{i}", [128, 512], F32) for i in range(2)]
                outb = [T(f"outb{i}", [128, 512], BF16) for i in range(4)]
                vout = [T(f"vout{i}", [128, 512], BF16) for i in range(2)]
                ntmp = T("ntmp", [1, 2], F32)
                tp = [es.enter_context(nc.psum_tensor(_uniq(f"tp{i}"), [128, 512], F32)) for i in range(2)]
                mmp = [es.enter_context(nc.psum_tensor(_uniq(f"mm{i}"), [128, 512], F32)) for i in range(3)]
                aux = [es.enter_context(nc.psum_tensor(_uniq(f"aux{i}"), [128, 512], F32)) for i in range(2)]
                nrp = es.enter_context(nc.psum_tensor(_uniq("nrp"), [128, 512], F32))
                P = Phase(nc, "p1")
                wkv_v = wkv_h.ap().rearrange("(k p) c -> p k c", p=128)
                wq_v = wq_h.ap().rearrange("(k p) c -> p k c", p=128)
                P.dma("pool", "wkv", [(wkv_sb[:, 4 * k:4 * k + 4, :], wkv_v[:, 4 * k:4 * k + 4, :]) for k in range(4)], writes=["wkv"])
                P.dma("pool", "wq", [(wq_sb[:, 4 * k:4 * k + 4, :], wq_v[:, 4 * k:4 * k + 4, :]) for k in range(4)], writes=["wq"])

                cnt = {"xT": 0, "tp": 0, "mm": 0, "aux": 0, "y": 0, "sq": 0, "t": 0, "rr": 0, "ob": 0, "vo": 0, "ev": 0, "cs": 0}

                def nxt(k, n):
                    v = cnt[k] % n
                    cnt[k] += 1
                    return v

                chunks = [("ctx", 0)] + [("tok", c) for c in (4, 5, 6, 7, 0, 1, 2, 3)]

                def load_x(ch):
                    kind, c = ch
                    if kind == "ctx":
                        for t in range(2):
                            P.dma("sp", ("xst", t), [(xst[t][:], ctx[t * 128:(t + 1) * 128, :])], writes=[("xst", t)])
                    else:
                        for t in range(4):
                            r0 = c * 512 + t * 128
                            P.dma("sp", ("xst", t), [(xst[t][:], x[r0:r0 + 128, :])], writes=[("xst", t)])

                load_x(chunks[0])
                for ci, ch in enumerate(chunks):
                    kind, c = ch
                    ntile = 2 if kind == "ctx" else 4
                    W = ntile * 128
                    own = (kind == "tok" and c < 4)
                    xi = nxt("xT", 2)
                    xTc = xT[xi]
                    moff = 96 if kind == "ctx" else 0
                    csb = None
                    if kind == "tok":
                        ci_ = nxt("cs", 2)
                        csb = cs[ci_]
                        P.dma("sp", ("cs", ci_), [(csb[:, 0, :], cos_h.ap()[:, c * 512:(c + 1) * 512]),
                                                  (csb[:, 1, :], sin_h.ap()[:, c * 512:(c + 1) * 512])], writes=[("cs", ci_)])
                    for k in range(16):
                        ti = nxt("tp", 2)

                        def trf(e, k=k, ti=ti, ntile=ntile):
                            ins = None
                            for t in range(ntile):
                                ins = e.transpose(out=tp[ti][:, t * 128:(t + 1) * 128], in_=xst[t][:, k * 128:(k + 1) * 128], identity=ident[:])
                            return ins
                        P.op("pe", trf, reads=[("xst", t) for t in range(ntile)] + ["ident"], writes=[("tp", ti)])
                        sc_ap = modT[:, moff + 16 + k: moff + 17 + k]
                        sh_ap = modT[:, moff + k: moff + k + 1]
                        if nxt("ev", 2) == 0:
                            P.op("act", lambda e, k=k, ti=ti, W=W, sc_ap=sc_ap, sh_ap=sh_ap, xTc=xTc:
                                 e.activation(out=xTc[:, k, 0:W], in_=tp[ti][:, 0:W], func=AF.Identity, bias=sh_ap, scale=sc_ap),
                                 reads=[("tp", ti), "modT"], writes=[("xT", xi, k)])
                        else:
                            P.op("dve", lambda e, k=k, ti=ti, W=W, sc_ap=sc_ap, sh_ap=sh_ap, xTc=xTc:
                                 e.tensor_scalar(out=xTc[:, k, 0:W], in0=tp[ti][:, 0:W], scalar1=sc_ap, scalar2=sh_ap, op0=ALU.mult, op1=ALU.add),
                                 reads=[("tp", ti), "modT"], writes=[("xT", xi, k)])
                    if ci + 1 < len(chunks):
                        load_x(chunks[ci + 1])
                    xkeys = [("xT", xi, k) for k in range(16)]

                    heads = []
                    for h in range(4):
                        heads.append(("k", h))
                    if own:
                        for h in range(16):
                            heads.append(("q", h))
                    items = []

                    ci__ = ci_ if kind == "tok" else None

                    def make_head(hk, h, W=W, xTc=xTc, csb=csb, ci_=ci__, c=c, kind=kind, xkeys=xkeys, ci=ci):
                        info = {}

                        def stage1():
                            mi = nxt("mm", 3)
                            if hk == "k":
                                wsb, col0, wkey = wkv_sb, h * 128, "wkv"
                                isB = h >= 2
                            else:
                                wsb, col0, wkey = wq_sb, h * 128, "wq"
                                isB = h >= 8

                            def mmf(e, wsb=wsb, col0=col0, mi=mi):
                                ins = None
                                for k in range(16):
                                    ins = e.matmul(mmp[mi][:, 0:W], lhsT=wsb[:, k, col0:col0 + 128], rhs=xTc[:, k, 0:W], start=(k == 0), stop=(k == 15))
                                return ins
                            P.op("pe", mmf, reads=xkeys + [wkey], writes=[("mm", mi)])
                            yi = nxt("y", 3)
                            si = nxt("sq", 2)
                            if isB:
                                gap = gqs[:, 0:1] if hk == "q" else gcol[:, 1:2]
                                P.op("act", lambda e, yi=yi, mi=mi, gap=gap: e.activation(out=yb[yi][:, 0:W], in_=mmp[mi][:, 0:W], func=AF.Identity, scale=gap),
                                     reads=[("mm", mi), "gqs", "gcol"], writes=[("y", yi)])
                                P.op("act", lambda e, si=si, mi=mi: e.activation(out=sqf[si][:, 0:W], in_=mmp[mi][:, 0:W], func=AF.Square),
                                     reads=[("mm", mi)], writes=[("sqf", si)])
                            else:
                                sc = SCALE if hk == "q" else 1.0
                                P.op("act", lambda e, yi=yi, mi=mi, sc=sc: e.activation(out=yb[yi][:, 0:W], in_=mmp[mi][:, 0:W], func=AF.Identity, scale=sc),
                                     reads=[("mm", mi)], writes=[("y", yi)])
                                P.op("act", lambda e, si=si, yi=yi: e.activation(out=sqb[si][:, 0:W], in_=yb[yi][:, 0:W], func=AF.Square),
                                     reads=[("y", yi)], writes=[("sqb", si)])
                            if kind == "tok":
                                P.op("act", lambda e, yi=yi: e.activation(out=ybb[yi][:, 0:W], in_=yb[yi][:, 0:W], func=AF.Identity),
                                     reads=[("y", yi)], writes=[("ybb", yi)])
                            info.update(mi=mi, yi=yi, si=si, isB=isB)

                        def stage2():
                            mi, yi, si, isB = info["mi"], info["yi"], info["si"], info["isB"]
                            oi = nxt("ob", 4)
                            ob = outb[oi]
                            ri = None
                            if isB:
                                a_ms = nxt("aux", 2)
                                P.op("pe", lambda e, a_ms=a_ms, si=si: e.matmul(aux[a_ms][:, 0:W], lhsT=onesdb[:], rhs=sqf[si][:, 0:W], start=True, stop=True),
                                     reads=[("sqf", si), "onesd"], writes=[("aux", a_ms)])
                                ri = nxt("rr", 2)
                                P.op("act", lambda e, ri=ri, a_ms=a_ms: e.activation(out=rr[ri][:, 0:W], in_=aux[a_ms][:, 0:W], func=AF.Sqrt, bias=epst[:, 0:1], scale=1.0),
                                     reads=[("aux", a_ms), "epst"], writes=[("rr", ri)])
                                P.op("dve", lambda e, ri=ri: e.reciprocal(out=rr[ri][:, 0:W], in_=rr[ri][:, 0:W]),
                                     reads=[("rr", ri)], writes=[("rr", ri)])
                            else:
                                P.op("pe", lambda e, si=si: e.matmul(nrp[0:1, 0:W], lhsT=ones_b[:, 0:1], rhs=sqb[si][:, 0:W], start=True, stop=True),
                                     reads=[("sqb", si), "ones_b"], writes=["nrp"])
                                col = 0 if hk == "q" else 1
                                P.op("dve", lambda e: e.tensor_reduce(out=ntmp[:, 0:1], in_=nrp[0:1, 0:W], axis=AX.X, op=ALU.max),
                                     reads=["nrp"], writes=["ntmp"])
                                P.op("dve", lambda e, col=col: e.tensor_tensor(out=nrmmax[:, col:col + 1], in0=nrmmax[:, col:col + 1], in1=ntmp[:, 0:1], op=ALU.max),
                                     reads=["ntmp", "nrmmax"], writes=["nrmmax"])
                            if kind == "tok":
                                a_r = nxt("aux", 2)
                                P.op("pe", lambda e, a_r=a_r, yi=yi: e.matmul(aux[a_r][:, 0:W], lhsT=rotb[:], rhs=ybb[yi][:, 0:W], start=True, stop=True),
                                     reads=[("ybb", yi), "rotm"], writes=[("aux", a_r)])
                                ti_ = nxt("t", 2)
                                P.op("pool", lambda e, ti_=ti_, yi=yi: e.tensor_tensor(out=t1[ti_][:], in0=yb[yi][:], in1=csb[:, 0, :], op=ALU.mult),
                                     reads=[("y", yi), ("cs", ci_)], writes=[("t1", ti_)])
                                P.op("dve", lambda e, ti_=ti_, a_r=a_r: e.tensor_tensor(out=t2[ti_][:], in0=aux[a_r][:], in1=csb[:, 1, :], op=ALU.mult),
                                     reads=[("aux", a_r), ("cs", ci_)], writes=[("t2", ti_)])
                                if isB:
                                    P.op("pool", lambda e, ti_=ti_: e.tensor_tensor(out=t1[ti_][:], in0=t1[ti_][:], in1=t2[ti_][:], op=ALU.add),
                                         reads=[("t1", ti_), ("t2", ti_)], writes=[("t1", ti_)])
                                    P.op("dve", lambda e, ti_=ti_, ri=ri, ob=ob: e.tensor_tensor(out=ob[:], in0=t1[ti_][:], in1=rr[ri][:], op=ALU.mult),
                                         reads=[("t1", ti_), ("rr", ri)], writes=[("ob", oi)])
                                else:
                                    P.op("dve", lambda e, ti_=ti_, ob=ob: e.tensor_tensor(out=ob[:], in0=t1[ti_][:], in1=t2[ti_][:], op=ALU.add),
                                         reads=[("t1", ti_), ("t2", ti_)], writes=[("ob", oi)])
                            else:
                                if isB:
                                    P.op("dve", lambda e, yi=yi, ri=ri, ob=ob: e.tensor_tensor(out=ob[:, 0:W], in0=yb[yi][:, 0:W], in1=rr[ri][:, 0:W], op=ALU.mult),
                                         reads=[("y", yi), ("rr", ri)], writes=[("ob", oi)])
                                else:
                                    P.op("dve", lambda e, yi=yi, ob=ob: e.tensor_copy(out=ob[:, 0:W], in_=yb[yi][:, 0:W]),
                                         reads=[("y", yi)], writes=[("ob", oi)])
                            if hk == "k":
                                c0 = SEQ if kind == "ctx" else c * 512
                                P.dma("sp", ("ob", oi), [(KT[h, :, c0:c0 + W], ob[:, 0:W])], reads=[("ob", oi)], writes=[("KT", h, ci)])
                            else:
                                P.dma("sp", ("ob", oi), [(QT[h, :, c * 512:(c + 1) * 512], ob[:])], reads=[("ob", oi)], writes=[("QT", h, c)])
                        return stage1, stage2

                    def make_v(t, W=W, xTc=xTc, c=c, kind=kind, xkeys=xkeys):
                        def stage1():
                            mi = nxt("mm", 3)

                            def mmv(e, mi=mi):
                                ins = None
                                for k in range(16):
                                    ins = e.matmul(mmp[mi][:], lhsT=xTc[:, k, t * 128:(t + 1) * 128], rhs=wkv_sb[:, k, 512:1024], start=(k == 0), stop=(k == 15))
                                return ins
                            P.op("pe", mmv, reads=xkeys + ["wkv"], writes=[("mm", mi)])
                            vi = nxt("vo", 2)
                            P.op("act", lambda e, vi=vi, mi=mi: e.activation(out=vout[vi][:], in_=mmp[mi][:], func=AF.Identity),
                                 reads=[("mm", mi)], writes=[("vo", vi)])
                            r0 = (SEQ if kind == "ctx" else c * 512) + t * 128
                            P.dma("sp", ("vo", vi), [(Vd[r0:r0 + 128, :], vout[vi][:])], reads=[("vo", vi)], writes=[("Vd", r0)])
                        return stage1, (lambda: None)

                    for (hk, h) in heads:
                        items.append(make_head(hk, h))
                    vt = [make_v(t) for t in range(ntile)]
                    merged = []
                    step_v = max(1, len(items) // max(1, len(vt)))
                    for i, it in enumerate(items):
                        merged.append(it)
                        if (i + 1) % step_v == 0 and vt:
                            merged.append(vt.pop(0))
                    merged.extend(vt)
                    merged[0][0]()
                    for i in range(len(merged)):
                        if i + 1 < len(merged):
                            merged[i + 1][0]()
                        merged[i][1]()
                P.emit()

        if 2 in PHASES:
            with contextlib.ExitStack() as es:
                T = lambda name, shape, dt: es.enter_context(nc.sbuf_tensor(_uniq(name), list(shape), dt))
                KT_sb = T("KT_sb", [128, 4, NKEY], BF16)
                V_sb = T("V_sb", [128, 34, 512], BF16)
                QT_sb = [T(f"QT_sb{i}", [128, 16, 512], BF16) for i in range(2)]
                OT_sb = [T(f"OT_sb{i}", [128, 16, 512], BF16) for i in range(2)]
                NPT = 6
                pT = [T(f"pT{i}", [128, 512], BF16) for i in range(NPT)]
                accD = [T(f"accD{i}", [128, 512], F32) for i in range(2)]
                accP = [T(f"accP{i}", [128, 512], F32) for i in range(2)]
                wr2 = [T(f"wr2{i}", [128, 16, 512], BF16) for i in range(2)]
                bada2 = [T(f"bada2{i}", [1, 512], F32) for i in range(2)]
                mr2 = [T(f"mr2{i}", [1, 512], F32) for i in range(2)]
                msk = T("msk", [128, 4, 512], F32)
                rl = [T(f"rl{i}", [128, 512], F32) for i in range(2)]
                sinkrow = T("sinkrow", [128, 2, 512], F32)
                sm = T("sm", [1, 16], F32)
                Sps = [es.enter_context(nc.psum_tensor(_uniq(f"Sps{i}"), [128, 512], F32)) for i in range(3)]
                Ops = [es.enter_context(nc.psum_tensor(_uniq(f"Ops{i}"), [128, 512], F32)) for i in range(2)]
                Lps = [es.enter_context(nc.psum_tensor(_uniq(f"Lps{i}"), [128, 512], F32)) for i in range(2)]
                mps = es.enter_context(nc.psum_tensor(_uniq("mps"), [128, 512], F32))
                P = Phase(nc, "p2")
                for h in range(4):
                    P.dma("sp", ("kt", h), [(KT_sb[:, h, :], KT[h])], writes=[("KT", h)])
                Vv = Vd.rearrange("(b p) c -> p b c", p=128)
                for q4 in range(4):
                    b0, b1 = q4 * 9, min(34, q4 * 9 + 9)
                    P.dma("sp", ("v", q4), [(V_sb[:, b0:b1, :], Vv[:, b0:b1, :])], writes=[("V", b) for b in range(b0, b1)])
                P.dma("sp", "msk", [(msk[:, i, :], masks_h.ap()[i]) for i in range(4)], writes=["msk"])
                sink_sb = T("sink_sb", [1, 8], F32)
                P.dma("sp", "sink", [(sink_sb[:], sink_h.ap())], writes=["sink_sb"])
                P.op("dve", lambda e: e.tensor_tensor(out=sm[:, 0:1], in0=nrmmax[:, 0:1], in1=nrmmax[:, 1:2], op=ALU.mult), writes=["sm0"])
                P.op("act", lambda e: e.activation(out=sm[:, 1:2], in_=sm[:, 0:1], func=AF.Sqrt), reads=["sm0"], writes=["sm1"])
                P.op("dve", lambda e: e.tensor_scalar(out=sm[:, 2:3], in0=sm[:, 1:2], scalar1=-1.02, scalar2=None, op0=ALU.mult), reads=["sm1"], writes=["sm2"])
                P.op("act", lambda e: e.activation(out=sm[:, 8:16], in_=sink_sb[:], func=AF.Exp, bias=sm[:, 2:3], scale=1.0),
                     reads=["sm2", "sink_sb"], writes=["sm8"])
                P.op("pe", lambda e: e.matmul(mps[:, 0:1], lhsT=ones_f[0:1, :], rhs=sm[:, 2:3], start=True, stop=True), reads=["sm2"], writes=["mps"])
                P.op("dve", lambda e: e.tensor_copy(out=negcA[:], in_=mps[:, 0:1]), reads=["mps"], writes=["negcA"])
                P.op("pe", lambda e: e.matmul(mps[:, 0:8], lhsT=ones_f[0:1, :], rhs=sm[:, 8:16], start=True, stop=True), reads=["sm8", "negcA"], writes=["mps"])
                P.op("dve", lambda e: e.tensor_copy(out=sinkb[:], in_=mps[:, 0:8]), reads=["mps"], writes=["sinkb"])
                for g in range(2):
                    for h in range(4):
                        P.op("dve", lambda e, g=g, h=h: e.tensor_scalar(out=sinkrow[:, g, h * 128:(h + 1) * 128], in0=ones_f[:], scalar1=sinkb[:, 4 * g + h:4 * g + h + 1],
                                                                         scalar2=None, op0=ALU.mult),
                             reads=["sinkb"], writes=[("sinkrow", g)])

                steps = []
                for qc in range(4):
                    cb = qc % 2
                    for g in range(2):
                        for h in range(4):
                            head = 8 + 4 * g + h
                            for sb in range(34):
                                steps.append(dict(
                                    qc=qc, cb=cb, kind="B", first=(sb == 0), last=(sb == 33),
                                    lhsT=KT_sb[:, 2 + g, sb * 128:(sb + 1) * 128], kkey=("KT", 2 + g),
                                    rhs=QT_sb[cb][:, head, :],
                                    vs=V_sb[:, sb, 256 + g * 128:256 + (g + 1) * 128], vkey=("V", sb),
                                    bias=negcB, mask=None, head=head, g=g, i=None, sb=sb))
                    for g in range(2):
                        for i in range(4):
                            qb = qc * 4 + i
                            blocks = []
                            if qb == 0:
                                blocks.append((31, 2))
                            else:
                                blocks.append((qb - 1, 0))
                            blocks.append((qb, None))
                            if qb == 15:
                                blocks.append((16, 3))
                            else:
                                blocks.append((qb + 1, 1))
                            blocks.append((32, None))
                            blocks.append((33, None))
                            for bi, (kb, mk) in enumerate(blocks):
                                steps.append(dict(
                                    qc=qc, cb=cb, kind="A", first=(bi == 0), last=(bi == len(blocks) - 1),
                                    lhsT=KT_sb[:, g, kb * 128:(kb + 1) * 128], kkey=("KT", g),
                                    rhs=QT_sb[cb][:, 4 * g:4 * g + 4, i * 128:(i + 1) * 128],
                                    vs=V_sb[:, kb, g * 128:(g + 1) * 128], vkey=("V", kb),
                                    bias=negcA, mask=mk, head=None, g=g, i=i))
                nst = len(steps)
                acc_idx = 0
                for st in steps:
                    if st["first"]:
                        acc_idx += 1
                    st["acc"] = acc_idx % 2

                loaded_qc = set()

                def load_q(qc):
                    if qc in loaded_qc or qc >= 4:
                        return
                    loaded_qc.add(qc)
                    cb = qc % 2
                    P.dma("sp", ("qt", cb), [(QT_sb[cb][:, h, :], QT[h, :, qc * 512:(qc + 1) * 512]) for h in range(16)], writes=[("QT", cb)])

                def rec_qk(j):
                    st = steps[j]
                    si = j % 3
                    rhs = st["rhs"]
                    P.op("pe", lambda e, st=st, si=si: e.matmul(Sps[si][:], lhsT=st["lhsT"], rhs=st["rhs"], start=True, stop=True),
                         reads=[st["kkey"], ("QT", st["cb"])], writes=[("S", si)])
                    pi = j % NPT
                    P.op("act", lambda e, st=st, si=si, pi=pi: e.activation(out=pT[pi][:], in_=Sps[si][:], func=AF.Exp, bias=st["bias"][:, 0:1], scale=1.0),
                         reads=[("S", si), "negcA", "negcB"], writes=[("pT", pi)])
                    if st["mask"] is not None:
                        mk = st["mask"]
                        P.op("dve", lambda e, pi=pi, mk=mk: e.tensor_tensor(out=pT[pi][:], in0=pT[pi][:], in1=msk[:, mk, :], op=ALU.mult),
                             reads=[("pT", pi), "msk"], writes=[("pT", pi)])

                def rec_pv(j):
                    st = steps[j]
                    pi = j % NPT
                    a = st["acc"]
                    if st["kind"] == "A" or True:
                        def f(e, st=st, pi=pi, a=a):
                            e.matmul(Ops[a][:], lhsT=st["vs"], rhs=pT[pi][:], start=st["first"], stop=st["last"])
                            return e.matmul(Lps[a][:], lhsT=ones_b[:], rhs=pT[pi][:], start=st["first"], stop=st["last"])
                        P.op("pe", f, reads=[("pT", pi), st["vkey"], "ones_b"], writes=[("O", a), ("L", a)])
                    else:
                        P.op("pe", lambda e, st=st, pi=pi, a=a: e.matmul(Ops[a][:], lhsT=st["vs"], rhs=pT[pi][:], start=st["first"], stop=st["last"]),
                             reads=[("pT", pi), st["vkey"]], writes=[("O", a)])
                        sb = st["sb"]
                        if sb % 3 == 2:
                            eng, acc, akey = "pool", accP[a], ("accP", a)
                            first = (sb == 2)
                        else:
                            eng, acc, akey = "dve", accD[a], ("accD", a)
                            first = (sb == 0)
                        if first:
                            P.op(eng, lambda e, acc=acc, pi=pi: e.tensor_copy(out=acc[:], in_=pT[pi][:]), reads=[("pT", pi)], writes=[akey])
                        else:
                            P.op(eng, lambda e, acc=acc, pi=pi: e.tensor_tensor(out=acc[:], in0=acc[:], in1=pT[pi][:], op=ALU.add), reads=[("pT", pi), akey], writes=[akey])
                        if st["last"]:
                            P.op("dve", lambda e, a=a: e.tensor_tensor(out=accD[a][:], in0=accD[a][:], in1=accP[a][:], op=ALU.add),
                                 reads=[("accD", a), ("accP", a)], writes=[("accD", a)])
                            P.op("pe", lambda e, a=a: e.matmul(Lps[a][:], lhsT=ones_f[:], rhs=accD[a][:], start=True, stop=True),
                                 reads=[("accD", a)], writes=[("L", a)])
                    if st["last"]:
                        ri = a
                        cb = st["cb"]
                        if st["kind"] == "B":
                            P.op("dve", lambda e, a=a, ri=ri: e.reciprocal(out=rl[ri][:], in_=Lps[a][:]), reads=[("L", a)], writes=[("rl", ri)])
                            P.op("dve", lambda e, a=a, ri=ri, st=st, cb=cb: e.tensor_tensor(out=OT_sb[cb][:, st["head"], :], in0=Ops[a][:], in1=rl[ri][:], op=ALU.mult),
                                 reads=[("O", a), ("rl", ri)], writes=[("OT", cb, st["head"])])
                        else:
                            g, i = st["g"], st["i"]
                            P.op("dve", lambda e, a=a, ri=ri, g=g: e.tensor_tensor(out=rl[ri][:], in0=Lps[a][:], in1=sinkrow[:, g, :], op=ALU.add),
                                 reads=[("L", a), ("sinkrow", g)], writes=[("rl", ri)])
                            P.op("dve", lambda e, ri=ri: e.reciprocal(out=rl[ri][:], in_=rl[ri][:]), reads=[("rl", ri)], writes=[("rl", ri)])
                            okeys = [("OT", cb, 4 * g + h) for h in range(4)]
                            P.op("dve", lambda e, a=a, ri=ri, g=g, i=i, cb=cb: e.tensor_tensor(
                                out=OT_sb[cb][:, 4 * g:4 * g + 4, i * 128:(i + 1) * 128],
                                in0=Ops[a][:].rearrange("p (h q) -> p h q", h=4),
                                in1=rl[ri][:].rearrange("p (h q) -> p h q", h=4), op=ALU.mult),
                                 reads=[("O", a), ("rl", ri)] + okeys, writes=okeys)
                        if j + 1 == nst or steps[j + 1]["qc"] != st["qc"]:
                            qc = st["qc"]
                            P.dma("pool", ("ot", cb), [(OT[h, :, qc * 512:(qc + 1) * 512], OT_sb[cb][:, h, :]) for h in range(16)],
                                  reads=[("OT", cb, h) for h in range(16)], writes=[("OTd", qc)])

                wv2 = wada_h.ap().rearrange("(k p) c -> p k c", p=128)
                NG2 = 16

                def ada_load(g):
                    if g >= NG2:
                        return
                    r = g % 2
                    c0 = 4096 + g * 512
                    P.dma("pool", ("wr2", r), [(wr2[r][:], wv2[:, :, c0:c0 + 512])], writes=[("wr2", r)])
                    P.dma("sp", ("bada2", r), [(bada2[r][:], bada_h.ap()[0:1, c0:c0 + 512])], writes=[("bada2", r)])

                def ada_group(g):
                    r = g % 2
                    c0 = 4096 + g * 512

                    def mm(e, r=r):
                        ins = None
                        for k in range(16):
                            ins = e.matmul(mps[0:1, :], lhsT=condb2[:, k, 0:1], rhs=wr2[r][:, k, :], start=(k == 0), stop=(k == 15))
                        return ins
                    P.op("pe", mm, reads=[("wr2", r)], writes=["mps"])
                    P.op("dve", lambda e, r=r: e.tensor_tensor(out=mr2[r][:], in0=mps[0:1, :], in1=bada2[r][:], op=ALU.add),
                         reads=["mps", ("bada2", r)], writes=[("mr2", r)])
                    P.dma("sp", ("mr2", r), [(mrow[0:1, c0:c0 + 512], mr2[r][:])], reads=[("mr2", r)], writes=[("mrow2", g)])
                    ada_load(g + 2)

                ada_load(0)
                ada_load(1)
                ada_every = nst // (NG2 + 1)
                load_q(0)
                load_q(1)
                LOOK = 2
                for j in range(min(LOOK, nst)):
                    rec_qk(j)
                for j in range(nst):
                    if j + LOOK < nst:
                        nq = steps[j + LOOK]["qc"]
                        rec_qk(j + LOOK)
                    rec_pv(j)
                    if (j + 1) % ada_every == 0 and (j + 1) // ada_every <= NG2:
                        ada_group((j + 1) // ada_every - 1)
                    if steps[j]["last"] and (j + 1 < nst) and steps[j + 1]["qc"] != steps[j]["qc"]:
                        load_q(steps[j]["qc"] + 2)
                P.emit()

        if 3 in PHASES:
            with contextlib.ExitStack() as es:
                T = lambda name, shape, dt: es.enter_context(nc.sbuf_tensor(_uniq(name), list(shape), dt))
                wout_sb = T("wout_sb", [128, 16, D], BF16)
                OTc = [T(f"OTc{i}", [128, 16, 512], BF16) for i in range(2)]
                xt = [T(f"xt{i}", [128, D], F32) for i in range(2)]
                zt = [T(f"zt{i}", [128, D], F32) for i in range(3)]
                g1b = T("g1b", [128, D], F32)
                lg = T("lg", [128, D], F32)
                lb = T("lb", [128, D], F32)
                x1Tc = [T(f"x1Tc{i}", [128, 16, 512], BF16) for i in range(2)]
                st6 = [T(f"st6{i}", [128, 4, 6], F32) for i in range(2)]
                mv = [T(f"mv{i}", [128, 4], F32) for i in range(2)]
                aps = [es.enter_context(nc.psum_tensor(_uniq(f"aps{i}"), [128, 512], F32)) for i in range(6)]
                tps = [es.enter_context(nc.psum_tensor(_uniq(f"tps{i}"), [128, 512], F32)) for i in range(2)]
                P = Phase(nc, "p3")
                wo_v = wout_h.ap().rearrange("(k p) c -> p k c", p=128)
                P.dma("pool", "wo", [(wout_sb[:, 4 * k:4 * k + 4, :], wo_v[:, 4 * k:4 * k + 4, :]) for k in range(4)], writes=["wo"])
                P.dma("sp", "bc", [(g1b[:], bc(mrow_h, 2 * D, D)), (lg[:], bc(lnp_h, 0, D)), (lb[:], bc(lnp_h, D, D))], writes=["g1b", "lg", "lb"])
                P.dma("sp", "m2", [(modT[:, 32:96], mrow[0:1, 4096:12288].rearrange("o (j p) -> p (o j)", p=128))], writes=["modT"], slow=True)
                P.op("dve", lambda e: e.tensor_scalar(out=modT[:, 64:80], in0=modT[:, 64:80], scalar1=1.0, scalar2=None, op0=ALU.add), reads=["modT"], writes=["modT"])
                _ln_phase(nc, P, 16, lambda tt: x[tt * 128:(tt + 1) * 128, :], OT, OTc, wout_sb, "wo", xt, zt, g1b, lg, lb, st6, mv, aps, tps,
                          X1, X1T, x1Tc, modT, ident, epst)
                P.emit()

        if 4 in PHASES:
            with contextlib.ExitStack() as es4:
                hT = es4.enter_context(nc.sbuf_tensor(_uniq("hT"), [128, NF, 1024], BF16))
                for half in range(2):
                    t0 = half * 1024
                    with contextlib.ExitStack() as es:
                        T = lambda name, shape, dt: es.enter_context(nc.sbuf_tensor(_uniq(name), list(shape), dt))
                        x1T_sb = T("x1T_sb", [128, 16, 1024], BF16)
                        wgr = [T(f"wgr{i}", [128, 16, 128], BF16) for i in range(4)]
                        wur = [T(f"wur{i}", [128, 16, 128], BF16) for i in range(4)]
                        sg = [T(f"sg{i}", [128, 512], F32) for i in range(2)]
                        Gp = [es.enter_context(nc.psum_tensor(_uniq(f"Gp{i}"), [128, 512], F32)) for i in range(4)]
                        Up = [es.enter_context(nc.psum_tensor(_uniq(f"Up{i}"), [128, 512], F32)) for i in range(4)]
                        P = Phase(nc, f"p4a{half}")
                        xs1 = [T(f"xs1{i}", [128, D], F32) for i in range(2)]
                        evn = 0
                        tpn = 0
                        for tl in range(8):
                            xi = tl % 2
                            P.dma("sp", ("xs1", xi), [(xs1[xi][:], X1[t0 + tl * 128:t0 + (tl + 1) * 128, :])], writes=[("xs1", xi)])
                            for k in range(0, 16, 4):
                                ti = tpn % 4
                                tpn += 1

                                def trf(e, k=k, ti=ti, xi=xi):
                                    ins = None
                                    for kk in range(4):
                                        ins = e.transpose(out=Gp[ti][:, kk * 128:(kk + 1) * 128], in_=xs1[xi][:, (k + kk) * 128:(k + kk + 1) * 128], identity=ident[:])
                                    return ins
                                P.op("pe", trf, reads=[("xs1", xi)], writes=[("G", ti)])
                                for kk in range(4):
                                    kq = k + kk
                                    sc_ap = modT[:, 64 + kq:65 + kq]
                                    sh_ap = modT[:, 48 + kq:49 + kq]
                                    dst = x1T_sb[:, kq, tl * 128:(tl + 1) * 128]
                                    if tpn % 2 == 0:
                                        P.op("act", lambda e, ti=ti, kk=kk, sc_ap=sc_ap, sh_ap=sh_ap, dst=dst: e.activation(out=dst, in_=Gp[ti][:, kk * 128:(kk + 1) * 128], func=AF.Identity, bias=sh_ap, scale=sc_ap),
                                             reads=[("G", ti)], writes=[("x1T", tl, kq)])
                                    else:
                                        P.op("dve", lambda e, ti=ti, kk=kk, sc_ap=sc_ap, sh_ap=sh_ap, dst=dst: e.tensor_scalar(out=dst, in0=Gp[ti][:, kk * 128:(kk + 1) * 128], scalar1=sc_ap, scalar2=sh_ap, op0=ALU.mult, op1=ALU.add),
                                             reads=[("G", ti)], writes=[("x1T", tl, kq)])
                                    evn += 1
                        x1keys = {tc_: [("x1T", tl, kq) for tl in range(tc_ * 4, tc_ * 4 + 4) for kq in range(16)] for tc_ in range(2)}

                        def loadw(f):
                            r = f % 4
                            P.dma("pool", ("wg", r), [(wgr[r][:], wg_h.ap()[f]), (wur[r][:], wu_h.ap()[f])], writes=[("wg", r)])
                        for f in range(3):
                            loadw(f)
                        k_ = 0
                        for f in range(NF):
                            r = f % 4
                            for tc_ in range(2):
                                pi = (f * 2 + tc_) % 4

                                def mg(e, r=r, tc_=tc_, pi=pi):
                                    ins = None
                                    for k in range(16):
                                        ins = e.matmul(Gp[pi][:], lhsT=wgr[r][:, k, :], rhs=x1T_sb[:, k, tc_ * 512:(tc_ + 1) * 512], start=(k == 0), stop=(k == 15))
                                    return ins

                                def mu(e, r=r, tc_=tc_, pi=pi):
                                    ins = None
                                    for k in range(16):
                                        ins = e.matmul(Up[pi][:], lhsT=wur[r][:, k, :], rhs=x1T_sb[:, k, tc_ * 512:(tc_ + 1) * 512], start=(k == 0), stop=(k == 15))
                                    return ins
                                P.op("pe", mg, reads=[("wg", r)] + x1keys[tc_], writes=[("G", pi)])
                                P.op("pe", mu, reads=[("wg", r)] + x1keys[tc_], writes=[("U", pi)])
                                si = k_ % 2
                                k_ += 1
                                P.op("act", lambda e, si=si, pi=pi: e.activation(out=sg[si][:], in_=Gp[pi][:], func=AF.Silu), reads=[("G", pi)], writes=[("sg", si)])
                                P.op("dve", lambda e, si=si, pi=pi, f=f, tc_=tc_: e.tensor_tensor(out=hT[:, f, tc_ * 512:(tc_ + 1) * 512], in0=Up[pi][:], in1=sg[si][:], op=ALU.mult),
                                     reads=[("U", pi), ("sg", si)], writes=[("hT", f, tc_)])
                            if f + 3 < NF:
                                loadw(f + 3)
                        P.emit()
                    with contextlib.ExitStack() as es:
                        T = lambda name, shape, dt: es.enter_context(nc.sbuf_tensor(_uniq(name), list(shape), dt))
                        wdr = [T(f"wdr{i}", [128, NF, 512], BF16) for i in range(2)]
                        x1p = [T(f"x1p{i}", [128, 512], F32) for i in range(2)]
                        zp = [T(f"zp{i}", [128, 512], F32) for i in range(2)]
                        g2b = T("g2b", [128, D], F32)
                        Yp = [es.enter_context(nc.psum_tensor(_uniq(f"Yp{i}"), [128, 512], F32)) for i in range(4)]
                        P = Phase(nc, f"p4b{half}")
                        P.dma("sp", "g2b", [(g2b[:], bc(mrow_h, 5 * D, D))], writes=["g2b"])

                        def loadwd(dg):
                            r = dg % 2
                            for q4 in range(4):
                                P.dma("pool", ("wd", r, q4), [(wdr[r][:, q4 * 11:(q4 + 1) * 11, :], wd_h.ap()[dg][:, q4 * 11:(q4 + 1) * 11, :])], writes=[("wd", r, q4)])
                        loadwd(0)
                        loadwd(1)
                        n_ = 0
                        for dg in range(4):
                            r = dg % 2
                            for tl in range(8):
                                yi = n_ % 4
                                xi = n_ % 2
                                n_ += 1
                                row0 = t0 + tl * 128
                                P.dma("sp", ("x1p", xi), [(x1p[xi][:], X1[row0:row0 + 128, dg * 512:(dg + 1) * 512])], writes=[("x1p", xi)])

                                def md(e, r=r, tl=tl, yi=yi):
                                    ins = None
                                    for fk in range(NF):
                                        ins = e.matmul(Yp[yi][:], lhsT=hT[:, fk, tl * 128:(tl + 1) * 128], rhs=wdr[r][:, fk, :], start=(fk == 0), stop=(fk == NF - 1))
                                    return ins
                                P.op("pe", md, reads=[("wd", r, q4) for q4 in range(4)], writes=[("Y", yi)])
                                P.op("dve", lambda e, xi=xi, yi=yi, dg=dg: e.tensor_tensor(out=zp[xi][:], in0=Yp[yi][:], in1=g2b[:, dg * 512:(dg + 1) * 512], op=ALU.mult),
                                     reads=[("Y", yi), "g2b"], writes=[("zp", xi)])
                                P.op("dve", lambda e, xi=xi: e.scalar_tensor_tensor(out=zp[xi][:], in0=x1p[xi][:], scalar=DN_ALPHA, in1=zp[xi][:], op0=ALU.mult, op1=ALU.add),
                                     reads=[("zp", xi), ("x1p", xi)], writes=[("zp", xi)])
                                P.dma("sp", ("zp", xi), [(Z[row0:row0 + 128, dg * 512:(dg + 1) * 512], zp[xi][:])], reads=[("zp", xi)], writes=[("Z", n_)])
                            if dg + 2 < 4:
                                loadwd(dg + 2)
                        P.emit()

        if 5 in PHASES:
            with contextlib.ExitStack() as es:
                T = lambda name, shape, dt: es.enter_context(nc.sbuf_tensor(_uniq(name), list(shape), dt))
                zt = [T(f"z5{i}", [128, D], F32) for i in range(4)]
                lg = T("lg5", [128, D], F32)
                lb = T("lb5", [128, D], F32)
                st6 = [T(f"st65{i}", [128, 4, 6], F32) for i in range(2)]
                mv = [T(f"mv5{i}", [128, 4], F32) for i in range(2)]
                P = Phase(nc, "p5")
                P.dma("sp", "bc", [(lg[:], bc(lnp_h, 2 * D, D)), (lb[:], bc(lnp_h, 3 * D, D))], writes=["lg", "lb"])

                def ld5(tt):
                    if tt < 16:
                        P.dma("sp", ("z", tt % 4), [(zt[tt % 4][:], Z[tt * 128:(tt + 1) * 128, :])], writes=[("z", tt % 4)])

                def st5(tt):
                    zi = tt % 4
                    _ln_bias(P, zt[zi], [("z", zi)], lb)
                    P.dma("sp", ("z", zi), [(out[tt * 128:(tt + 1) * 128, :], zt[zi][:])], reads=[("z", zi)], writes=[("out", tt)])
                for tt in range(3):
                    ld5(tt)
                for tt in range(16):
                    zi = tt % 4
                    _layer_norm(P, zt[zi], [("z", zi)], st6[tt % 2], mv[tt % 2], tt % 2, lg, lb, epst, defer_bias=True)
                    if tt >= 1:
                        st5(tt - 1)
                    ld5(tt + 3)
                st5(15)
                P.emit()
    return nc


def _layer_norm(P, z, zkeys, st6, mv, si, lg, lb, epst, defer_bias=False):
    zkeys = list(zkeys)
    def stats(e):
        ins = None
        for q in range(4):
            ins = e.bn_stats(out=st6[:, q, :], in_=z[:, q * 512:(q + 1) * 512])
        return ins
    P.op("dve", stats, reads=zkeys, writes=[("st6", si)])
    P.op("dve", lambda e: e.bn_aggr(out=mv[:, 0:2], in_=st6[:].rearrange("p a b -> p (a b)")), reads=[("st6", si)], writes=[("mv", si)])
    P.op("act", lambda e: e.activation(out=mv[:, 2:3], in_=mv[:, 1:2], func=AF.Sqrt, bias=epst[:, 0:1], scale=1.0), reads=[("mv", si)], writes=[("mv2", si)])
    P.op("dve", lambda e: e.reciprocal(out=mv[:, 2:3], in_=mv[:, 2:3]), reads=[("mv2", si)], writes=[("mv2", si)])
    P.op("dve", lambda e: e.scalar_tensor_tensor(out=mv[:, 3:4], in0=mv[:, 0:1], scalar=-1.0, in1=mv[:, 2:3], op0=ALU.mult, op1=ALU.mult),
         reads=[("mv", si), ("mv2", si)], writes=[("mv3", si)])
    P.op("act", lambda e: e.activation(out=z[:], in_=z[:], func=AF.Identity, bias=mv[:, 3:4], scale=mv[:, 2:3]),
         reads=zkeys + [("mv2", si), ("mv3", si)], writes=zkeys)
    P.op("pool", lambda e: e.tensor_tensor(out=z[:], in0=z[:], in1=lg[:], op=ALU.mult), reads=zkeys + ["lg"], writes=zkeys)
    if not defer_bias:
        _ln_bias(P, z, zkeys, lb)


def _ln_bias(P, z, zkeys, lb):
    zkeys = list(zkeys)
    P.op("dve", lambda e: e.tensor_tensor(out=z[:], in0=z[:], in1=lb[:], op=ALU.add), reads=zkeys + ["lb"], writes=zkeys)


def _ln_phase(nc, P, ntiles, xsrc, OT, OTc, wout_sb, wkey, xt, zt, g1b, lg, lb, st6, mv, aps, tps, X1, X1T, x1Tc, modT, ident, epst):
    nb = [0]

    def stage_a(tt):
        qc, tl = tt // 4, tt % 4
        cb = qc % 2
        if tt == 0:
            P.dma("sp", ("otc", 0), [(OTc[0][:, h, :], OT[h, :, 0:512]) for h in range(16)], writes=[("OTc", 0)])
        if tl == 1 and qc + 1 < ntiles // 4:
            nq = qc + 1
            P.dma("sp", ("otc", nq % 2), [(OTc[nq % 2][:, h, :], OT[h, :, nq * 512:(nq + 1) * 512]) for h in range(16)], writes=[("OTc", nq % 2)])
        xi = tt % 2
        zi = tt % 3
        if tt == 0:
            P.dma("sp", ("xt", 0), [(xt[0][:], xsrc(0))], writes=[("xt", 0)])
        if tt + 1 < ntiles:
            P.dma("sp", ("xt", (tt + 1) % 2), [(xt[(tt + 1) % 2][:], xsrc(tt + 1))], writes=[("xt", (tt + 1) % 2)])
        z = zt[zi]
        zkeys = [("z", zi, dg) for dg in range(4)]
        for dg in range(4):
            bi = nb[0] % 6
            nb[0] += 1

            def mo(e, bi=bi, dg=dg, cb=cb, tl=tl):
                ins = None
                for h in range(16):
                    ins = e.matmul(aps[bi][:], lhsT=OTc[cb][:, h, tl * 128:(tl + 1) * 128], rhs=wout_sb[:, h, dg * 512:(dg + 1) * 512], start=(h == 0), stop=(h == 15))
                return ins
            P.op("pe", mo, reads=[("OTc", cb), wkey], writes=[("aps", bi)])
            P.op("dve", lambda e, bi=bi, dg=dg, z=z: e.tensor_tensor(out=z[:, dg * 512:(dg + 1) * 512], in0=aps[bi][:], in1=g1b[:, dg * 512:(dg + 1) * 512], op=ALU.mult),
                 reads=[("aps", bi), "g1b"], writes=[("z", zi, dg)])
        P.op("dve", lambda e, z=z, xi=xi: e.scalar_tensor_tensor(out=z[:], in0=xt[xi][:], scalar=DN_ALPHA, in1=z[:], op0=ALU.mult, op1=ALU.add),
             reads=[("xt", xi)] + zkeys, writes=zkeys)
        si = tt % 2
        _layer_norm(P, z, zkeys, st6[si], mv[si], si, lg, lb, epst, defer_bias=True)

    def stage_b(tt):
        zi = tt % 3
        z = zt[zi]
        zkeys = [("z", zi, dg) for dg in range(4)]
        _ln_bias(P, z, zkeys, lb)
        P.dma("sp", ("zst", zi), [(X1[tt * 128:(tt + 1) * 128, :], z[:])], reads=zkeys, writes=[("X1", tt)])

    for tt in range(ntiles):
        stage_a(tt)
        if tt >= 1:
            stage_b(tt - 1)
    stage_b(ntiles - 1)


def _rope_tables():
    rows = SEQ // GRID_W
    row_ids = np.repeat(np.arange(rows, dtype=np.float64), GRID_W)
    col_ids = np.tile(np.arange(GRID_W, dtype=np.float64), rows)
    axis_dim = 64
    inv_freq = np.power(10000.0, -np.arange(0, axis_dim, 2, dtype=np.float64) / axis_dim)
    ang_r = row_ids[:, None] * inv_freq
    ang_c = col_ids[:, None] * inv_freq
    ang = np.concatenate([ang_r, ang_r, ang_c, ang_c], axis=-1)
    return np.cos(ang).astype(np.float32), np.sin(ang).astype(np.float32)


def _consts():
    ident = np.eye(128, dtype=np.float32)
    rotm = np.zeros((128, 128), np.float32)
    for base in (0, 64):
        for j in range(32):
            rotm[base + 32 + j, base + j] = -1.0
            rotm[base + j, base + 32 + j] = 1.0
    jj = np.arange(128)[:, None]
    ii = np.arange(128)[None, :]
    m_prev = (jj >= ii).astype(np.float32)
    m_next = (jj <= ii).astype(np.float32)
    return ident, rotm, m_prev, m_next


_NC_CACHE = {}


def make_in_maps(x, c, ctx, c_ctx, w_ada, b_ada, w_in, q_norm_g, k_norm_g, sink_logit,
                 w_out, ln1_g, ln1_b, w_gate, w_up, w_down, ln2_g, ln2_b):
    f = lambda a: np.ascontiguousarray(np.asarray(a, dtype=np.float32))
    x, c, ctx, c_ctx = f(x), f(c), f(ctx), f(c_ctx)
    w_ada, b_ada, w_in = f(w_ada)[0], f(b_ada), f(w_in)[0]
    w_out, w_gate, w_up, w_down = f(w_out)[0], f(w_gate)[0], f(w_up)[0], f(w_down)[0]
    qg, kg = f(q_norm_g)[0], f(k_norm_g)[0]
    cosN, sinN = _rope_tables()
    ident, rotm, m_prev, m_next = _consts()
    wq = np.ascontiguousarray(np.concatenate([w_in[:, 0:1024], w_in[:, 1536:2560]], axis=1))
    wkv = np.ascontiguousarray(np.concatenate([w_in[:, 1024:1280], w_in[:, 2560:2816], w_in[:, 1280:1536], w_in[:, 2816:3072]], axis=1))
    wg = np.ascontiguousarray(w_gate.reshape(16, 128, NF, 128).transpose(2, 1, 0, 3))
    wu = np.ascontiguousarray(w_up.reshape(16, 128, NF, 128).transpose(2, 1, 0, 3))
    wd = np.ascontiguousarray(w_down.reshape(NF, 128, 4, 512).transpose(2, 1, 0, 3))
    gcol = np.ascontiguousarray(np.stack([qg, kg], axis=1))
    grow = np.ascontiguousarray(np.concatenate([qg, kg])[None, :])
    sink = f(sink_logit).reshape(1, 8)
    lnp = np.ascontiguousarray(np.stack([f(ln1_g)[0], f(ln1_b)[0], f(ln2_g)[0], f(ln2_b)[0]], axis=0))
    rep4 = lambda m: np.tile(m, (1, 4))
    zeros = np.zeros((128, 512), np.float32)
    in_maps = []
    for core in range(8):
        b, hf = core // 2, core % 2
        own = slice(hf * NOWN, (hf + 1) * NOWN)
        oth = slice((1 - hf) * NOWN, (2 - hf) * NOWN)
        xp = np.ascontiguousarray(np.concatenate([x[b, own], x[b, oth]], axis=0))
        cosT = np.ascontiguousarray(np.concatenate([cosN[own], cosN[oth]], axis=0).T)
        sinT = np.ascontiguousarray(np.concatenate([sinN[own], sinN[oth]], axis=0).T)
        cond = np.stack([c[b], c_ctx], axis=1)
        condT = np.ascontiguousarray(cond.reshape(16, 128, 2).transpose(1, 0, 2))
        masks = np.stack([rep4(m_prev), rep4(m_next),
                          rep4(m_prev) if hf == 1 else zeros,
                          rep4(m_next) if hf == 0 else zeros], axis=0)
        in_maps.append({
            "x": xp, "ctx": np.ascontiguousarray(ctx[b]), "condT": condT, "w_ada": w_ada, "b_ada": b_ada.reshape(1, -1),
            "wq": wq, "wkv": wkv, "w_out": w_out, "wg": wg, "wu": wu, "wd": wd,
            "cosT": cosT, "sinT": sinT, "ident": ident, "rotm": rotm, "gcol": gcol, "grow": grow,
            "sink": sink, "lnp": lnp, "masks": np.ascontiguousarray(masks),
        })
    return in_maps


def kernel(x, c, ctx, c_ctx, w_ada, b_ada, w_in, q_norm_g, k_norm_g, sink_logit,
           w_out, ln1_g, ln1_b, w_gate, w_up, w_down, ln2_g, ln2_b):
    in_maps = make_in_maps(x, c, ctx, c_ctx, w_ada, b_ada, w_in, q_norm_g, k_norm_g, sink_logit,
                           w_out, ln1_g, ln1_b, w_gate, w_up, w_down, ln2_g, ln2_b)
    nc = build_program()
    res = run_bass_kernel_spmd(nc, in_maps, core_ids=list(range(8)))
    outp = np.empty((4, SEQ, D), np.float32)
    for core in range(8):
        b, hf = core // 2, core % 2
        outp[b, hf * NOWN:(hf + 1) * NOWN] = res.results[core]["out"]
    return outp
```

```python
import contextlib
import numpy as np
import concourse.bass as bass
import concourse.mybir as mybir
from concourse.bass_utils import run_bass_kernel_spmd

F32 = mybir.dt.float32
BF16 = mybir.dt.bfloat16
ALU = mybir.AluOpType
AF = mybir.ActivationFunctionType
AX = mybir.AxisListType

D = 2048
SEQ = 4096
NOWN = 2048
CTXL = 256
NKEY = SEQ + CTXL
FF = 5632
NF = FF // 128
EPS = 1e-6
SCALE = 128 ** -0.5
DN_ALPHA = 2.0 ** 0.25
GRID_W = 64

DEBUG = False
PHASES = (0, 1, 2, 3, 4, 5)

COMPUTE = ("pe", "act", "dve", "pool")


class Op:
    __slots__ = ("eng", "fn", "deps", "signal", "sem", "val", "is_dma", "semkey", "ndma")

    def __init__(self, eng, fn, is_dma=False, semkey=None, ndma=0):
        self.eng = eng
        self.fn = fn
        self.deps = set()
        self.signal = False
        self.sem = None
        self.val = 0
        self.is_dma = is_dma
        self.semkey = semkey
        self.ndma = ndma


class Phase:
    def __init__(self, nc, name):
        self.nc = nc
        self.name = name
        self.eng_ops = {e: [] for e in ("pe", "act", "dve", "pool", "sp")}
        self.last_writer = {}
        self.readers = {}

    def _add(self, o, reads, writes):
        deps = o.deps
        lw = self.last_writer
        rd = self.readers
        for k in reads:
            w = lw.get(k)
            if w is not None:
                deps.add(w)
        for k in writes:
            w = lw.get(k)
            if w is not None:
                deps.add(w)
            r = rd.get(k)
            if r:
                deps.update(r)
        deps.discard(o)
        for k in reads:
            rd.setdefault(k, []).append(o)
        for k in writes:
            lw[k] = o
            rd[k] = []
        for d in deps:
            if d.eng == "pe" and o.eng == "pe" and not d.is_dma and not o.is_dma:
                continue
            d.signal = True
        self.eng_ops[o.eng].append(o)
        return o

    def op(self, eng, fn, reads=(), writes=()):
        return self._add(Op(eng, fn), reads, writes)

    def dma(self, eng, semkey, pairs, reads=(), writes=(), slow=False):
        pairs = list(pairs)

        def fn(e, pairs=pairs, slow=slow):
            if slow:
                return [e.dma_start(out=o_, in_=i_, allow_slow_non_contiguous=True) for (o_, i_) in pairs]
            return [e.dma_start(out=o_, in_=i_) for (o_, i_) in pairs]

        o = Op(eng, fn, is_dma=True, semkey=semkey, ndma=len(pairs))
        o.signal = True
        return self._add(o, reads, writes)

    semstack = None
    dsem = {}
    dma_counts = {}

    def emit(self):
        nc = self.nc
        dma_counts = Phase.dma_counts
        dsem = Phase.dsem
        slots = {}
        for e, ops in self.eng_ops.items():
            for o in ops:
                if o.is_dma:
                    o.semkey = ("slot", slots.setdefault(o.semkey, len(slots)))
        for e, ops in self.eng_ops.items():
            cnt = 0
            for o in ops:
                if o.is_dma:
                    if o.semkey not in dma_counts:
                        dma_counts[o.semkey] = 0
                        dsem[o.semkey] = Phase.semstack.enter_context(nc.semaphore(f"d{len(dsem)}"))
                    dma_counts[o.semkey] += 16 * o.ndma
                    o.val = dma_counts[o.semkey]
                elif o.signal:
                    cnt += 1
                    o.val = cnt
        with contextlib.ExitStack() as es:
            esem = {e: Phase.semstack.enter_context(nc.semaphore(f"{self.name}_{e}")) for e in COMPUTE}
            for e, ops in self.eng_ops.items():
                for o in ops:
                    o.sem = dsem[o.semkey] if o.is_dma else esem[e]
            block = es.enter_context(nc.Block())
            eng_ops = self.eng_ops

            def run(e, engname):
                waited = {}
                for o in eng_ops[engname]:
                    need = {}
                    for d in o.deps:
                        if (not d.is_dma) and d.eng == "pe" and engname == "pe" and not o.is_dma:
                            continue
                        s = d.sem
                        if d.val > need.get(s, 0):
                            need[s] = d.val
                    for s, v in need.items():
                        if waited.get(s, 0) < v:
                            e.wait_ge(s, v)
                            waited[s] = v
                    r = o.fn(e)
                    if o.is_dma:
                        for ins in r:
                            ins.then_inc(o.sem, 16)
                    elif o.signal:
                        r.then_inc(o.sem, 1)
                last = {}
                for o in eng_ops[engname]:
                    if o.is_dma:
                        last[o.sem] = o.val
                for s, v in last.items():
                    if waited.get(s, 0) < v:
                        e.wait_ge(s, v)

            @block.tensor
            def _(e):
                run(e, "pe")

            @block.scalar
            def _(e):
                run(e, "act")

            @block.vector
            def _(e):
                run(e, "dve")

            @block.gpsimd
            def _(e):
                run(e, "pool")

            @block.sync
            def _(e):
                run(e, "sp")


_UNIQ = [0]


def _uniq(name):
    _UNIQ[0] += 1
    return f"s{_UNIQ[0]}_{name}"


def build_program():
    nc = bass.Bass("TRN2", target_bir_lowering=False)

    def din(name, shape, dt=F32):
        return nc.dram_tensor(name, list(shape), dt, kind="ExternalInput")

    def dscr(name, shape, dt):
        return nc.dram_tensor(name, list(shape), dt, kind="ExternalOutput" if DEBUG else "Internal")

    x_h = din("x", [SEQ, D])
    ctx_h = din("ctx", [CTXL, D])
    condT_h = din("condT", [128, 16, 2])
    wada_h = din("w_ada", [D, 6 * D])
    bada_h = din("b_ada", [1, 6 * D])
    wq_h = din("wq", [D, 2048])
    wkv_h = din("wkv", [D, 1024])
    wout_h = din("w_out", [D, D])
    wg_h = din("wg", [NF, 128, 16, 128])
    wu_h = din("wu", [NF, 128, 16, 128])
    wd_h = din("wd", [4, 128, NF, 512])
    cos_h = din("cosT", [128, SEQ])
    sin_h = din("sinT", [128, SEQ])
    ident_h = din("ident", [128, 128])
    rotm_h = din("rotm", [128, 128])
    gcol_h = din("gcol", [128, 2])
    grow_h = din("grow", [1, 256])
    sink_h = din("sink", [1, 8])
    lnp_h = din("lnp", [4, D])
    masks_h = din("masks", [4, 128, 512])
    out_h = nc.dram_tensor("out", [NOWN, D], F32, kind="ExternalOutput")

    mrow_h = dscr("mrow", [2, 6 * D], F32)
    QT_h = dscr("QT", [16, 128, NOWN], BF16)
    KT_h = dscr("KT", [4, 128, NKEY], BF16)
    Vd_h = dscr("Vd", [NKEY, 512], BF16)
    OT_h = dscr("OT", [16, 128, NOWN], BF16)
    X1_h = dscr("X1", [NOWN, D], F32)
    X1T_h = dscr("X1T", [128, 16, NOWN], BF16)
    Z_h = dscr("Z", [NOWN, D], F32)

    x = x_h.ap()
    ctx = ctx_h.ap()
    out = out_h.ap()
    QT = QT_h.ap()
    KT = KT_h.ap()
    Vd = Vd_h.ap()
    OT = OT_h.ap()
    X1 = X1_h.ap()
    X1T = X1T_h.ap()
    Z = Z_h.ap()
    mrow = mrow_h.ap()

    def bc(handle, row_off, n, parts=128):
        return bass.AP(handle, row_off, [[0, parts], [1, n]])

    with contextlib.ExitStack() as top:
        Phase.semstack = top
        Phase.dsem = {}
        Phase.dma_counts = {}
        S = lambda name, shape, dt: top.enter_context(nc.sbuf_tensor(_uniq(name), list(shape), dt))
        ident = S("ident", [128, 128], F32)
        rotm = S("rotm", [128, 128], F32)
        onesd = S("onesd", [128, 128], F32)
        ones_f = S("ones_f", [128, 128], F32)
        ones_b = S("ones_b", [128, 128], BF16)
        modT = S("modT", [128, 128], F32)
        gcol = S("gcol", [128, 2], F32)
        gqs = S("gqs", [128, 1], F32)
        nrmmax = S("nrmmax", [1, 2], F32)
        negcA = S("negcA", [128, 1], F32)
        negcB = S("negcB", [128, 1], F32)
        sinkb = S("sinkb", [128, 8], F32)
        epst = S("epst", [128, 1], F32)
        condb2 = S("condb2", [128, 16, 1], BF16)
        rotb = S("rotb", [128, 128], BF16)
        onesdb = S("onesdb", [128, 128], BF16)

        if 0 in PHASES:
            with contextlib.ExitStack() as es:
                T = lambda name, shape, dt: es.enter_context(nc.sbuf_tensor(_uniq(name), list(shape), dt))
                condf = T("condf", [128, 16, 2], F32)
                condb = T("condb", [128, 16, 2], F32)
                wring = [T(f"wring{i}", [128, 16, 512], F32) for i in range(3)]
                mrow_sb = T("mrow_sb", [2, 2 * D], F32)
                bada_sb = T("bada_sb", [2, 2 * D], F32)
                grow = T("grow", [1, 256], F32)
                gtmp = T("gtmp", [1, 4], F32)
                ps = [es.enter_context(nc.psum_tensor(_uniq(f"p0ps{i}"), [128, 512], F32)) for i in range(2)]
                P = Phase(nc, "p0")
                P.dma("sp", "c0", [(ident[:], ident_h.ap()), (rotm[:], rotm_h.ap()), (gcol[:], gcol_h.ap()),
                                   (condf[:], condT_h.ap()), (grow[:], grow_h.ap())],
                      writes=["ident", "rotm", "gcol", "condf", "grow"])
                P.dma("sp", "c1", [(bada_sb[:], bc(bada_h, 0, 2 * D, 2))], writes=["bada"])
                P.op("dve", lambda e: e.memset(onesd[:], 1.0 / 128.0), writes=["onesd"])
                P.op("dve", lambda e: e.memset(onesdb[:], 1.0 / 128.0), writes=["onesdb"])
                P.op("dve", lambda e: e.tensor_copy(out=rotb[:], in_=rotm[:]), reads=["rotm"], writes=["rotb"])
                P.op("dve", lambda e: e.memset(ones_f[:], 1.0), writes=["ones_f"])
                P.op("dve", lambda e: e.memset(ones_b[:], 1.0), writes=["ones_b"])
                P.op("dve", lambda e: e.memset(epst[:], EPS), writes=["epst"])
                P.op("dve", lambda e: e.memset(nrmmax[:], 0.0), writes=["nrmmax"])
                P.op("dve", lambda e: e.tensor_scalar(out=gqs[:], in0=gcol[:, 0:1], scalar1=SCALE, scalar2=None, op0=ALU.mult),
                     reads=["gcol"], writes=["gqs"])
                P.op("dve", lambda e: e.tensor_reduce(out=gtmp[:, 0:1], in_=grow[:, 0:128], axis=AX.X, op=ALU.max, apply_absolute_value=True),
                     reads=["grow"], writes=["gtmp0"])
                P.op("dve", lambda e: e.tensor_reduce(out=gtmp[:, 1:2], in_=grow[:, 128:256], axis=AX.X, op=ALU.max, apply_absolute_value=True),
                     reads=["grow"], writes=["gtmp1"])
                P.op("dve", lambda e: e.tensor_tensor(out=gtmp[:, 2:3], in0=gtmp[:, 0:1], in1=gtmp[:, 1:2], op=ALU.mult),
                     reads=["gtmp0", "gtmp1"], writes=["gtmp2"])
                P.op("dve", lambda e: e.tensor_scalar(out=gtmp[:, 3:4], in0=gtmp[:, 2:3], scalar1=-128.0 * SCALE * 1.001, scalar2=None, op0=ALU.mult),
                     reads=["gtmp2"], writes=["gtmp3"])
                P.op("pe", lambda e: e.matmul(ps[1][:, 0:1], lhsT=ones_f[0:1, :], rhs=gtmp[:, 3:4], start=True, stop=True),
                     reads=["ones_f", "gtmp3"], writes=["ps1"])
                P.op("dve", lambda e: e.tensor_copy(out=negcB[:], in_=ps[1][:, 0:1]), reads=["ps1"], writes=["negcB"])
                P.op("act", lambda e: e.activation(out=condb[:], in_=condf[:], func=AF.Silu), reads=["condf"], writes=["condb"])
                P.op("act", lambda e: e.activation(out=condb2[:], in_=condf[:, :, 0:1], func=AF.Silu), reads=["condf"], writes=["condb2"])
                wv = wada_h.ap().rearrange("(k p) c -> p k c", p=128)
                NG = 8
                for g in range(NG):
                    r = g % 3
                    P.dma("sp" if g % 2 == 0 else "pool", ("wr", r, g % 2), [(wring[r][:], wv[:, :, g * 512:(g + 1) * 512])], writes=[("wr", r)])

                    def mm(e, r=r):
                        ins = None
                        for k in range(16):
                            ins = e.matmul(ps[0][0:2, :], lhsT=condb[:, k, :], rhs=wring[r][:, k, :], start=(k == 0), stop=(k == 15))
                        return ins
                    P.op("pe", mm, reads=["condb", ("wr", r)], writes=["ps0"])
                    P.op("dve", lambda e, g=g: e.tensor_tensor(out=mrow_sb[:, g * 512:(g + 1) * 512], in0=ps[0][0:2, :],
                                                              in1=bada_sb[:, g * 512:(g + 1) * 512], op=ALU.add),
                         reads=["ps0", "bada"], writes=[("mrow_sb", g)])
                P.dma("sp", "m0", [(mrow[:, 0:2 * D], mrow_sb[:])], reads=[("mrow_sb", g) for g in range(NG)], writes=["mrow"])
                P.dma("sp", "m1", [(modT[:, 0:32], mrow[0:1, 0:4096].rearrange("o (j p) -> p (o j)", p=128)),
                                   (modT[:, 96:128], mrow[1:2, 0:4096].rearrange("o (j p) -> p (o j)", p=128))],
                      reads=["mrow"], writes=["modT"], slow=True)
                for (a, b) in ((16, 32), (112, 128)):
                    P.op("dve", lambda e, a=a, b=b: e.tensor_scalar(out=modT[:, a:b], in0=modT[:, a:b], scalar1=1.0, scalar2=None, op0=ALU.add),
                         reads=["modT"], writes=["modT"])
                P.emit()

        if 1 in PHASES:
            with contextlib.ExitStack() as es:
                T = lambda name, shape, dt: es.enter_context(nc.sbuf_tensor(_uniq(name), list(shape), dt))
                wq_sb = T("wq_sb", [128, 16, 2048], BF16)
                wkv_sb = T("wkv_sb", [128, 16, 1024], BF16)
                xst = [T(f"xst{i}", [128, D], F32) for i in range(4)]
                xT = [T(f"xT{i}", [128, 16, 512], BF16) for i in range(2)]
                cs = [T(f"cs{i}", [128, 2, 512], F32) for i in range(2)]
                yb = [T(f"yb{i}", [128, 512], F32) for i in range(3)]
                sqf = [T(f"sqf{i}", [128, 512], BF16) for i in range(2)]
                ybb = [T(f"ybb{i}", [128, 512], BF16) for i in range(3)]
                sqb = [T(f"sqb{i}", [128, 512], BF16) for i in range(2)]
                t1 = [T(f"t1{i}", [128, 512], F32) for i in range(2)]
                t2 = [T(f"t2{i}", [128, 512], F32) for i in range(2)]
                rr = [T(f"rr{i}", [128, 512], F32) for i in range(2)]
                outb = [T(f"outb{i}", [128, 512], BF16) for i in range(4)]
                vout = [T(f"vout{i}", [128, 512], BF16) for i in range(2)]
                ntmp = T("ntmp", [1, 2], F32)
                tp = [es.enter_context(nc.psum_tensor(_uniq(f"tp{i}"), [128, 512], F32)) for i in range(2)]
                mmp = [es.enter_context(nc.psum_tensor(_uniq(f"mm{i}"), [128, 512], F32)) for i in range(3)]
                aux = [es.enter_context(nc.psum_tensor(_uniq(f"aux{i}"), [128, 512], F32)) for i in range(2)]
                nrp = es.enter_context(nc.psum_tensor(_uniq("nrp"), [128, 512], F32))
                P = Phase(nc, "p1")
                wkv_v = wkv_h.ap().rearrange("(k p) c -> p k c", p=128)
                wq_v = wq_h.ap().rearrange("(k p) c -> p k c", p=128)
                P.dma("pool", "wkv", [(wkv_sb[:, 4 * k:4 * k + 4, :], wkv_v[:, 4 * k:4 * k + 4, :]) for k in range(4)], writes=["wkv"])
                P.dma("pool", "wq", [(wq_sb[:, 4 * k:4 * k + 4, :], wq_v[:, 4 * k:4 * k + 4, :]) for k in range(4)], writes=["wq"])

                cnt = {"xT": 0, "tp": 0, "mm": 0, "aux": 0, "y": 0, "sq": 0, "t": 0, "rr": 0, "ob": 0, "vo": 0, "ev": 0, "cs": 0}

                def nxt(k, n):
                    v = cnt[k] % n
                    cnt[k] += 1
                    return v

                chunks = [("ctx", 0)] + [("tok", c) for c in (4, 5, 6, 7, 0, 1, 2, 3)]

                def load_x(ch):
                    kind, c = ch
                    if kind == "ctx":
                        for t in range(2):
                            P.dma("sp", ("xst", t), [(xst[t][:], ctx[t * 128:(t + 1) * 128, :])], writes=[("xst", t)])
                    else:
                        for t in range(4):
                            r0 = c * 512 + t * 128
                            P.dma("sp", ("xst", t), [(xst[t][:], x[r0:r0 + 128, :])], writes=[("xst", t)])

                load_x(chunks[0])
                for ci, ch in enumerate(chunks):
                    kind, c = ch
                    ntile = 2 if kind == "ctx" else 4
                    W = ntile * 128
                    own = (kind == "tok" and c < 4)
                    xi = nxt("xT", 2)
                    xTc = xT[xi]
                    moff = 96 if kind == "ctx" else 0
                    csb = None
                    if kind == "tok":
                        ci_ = nxt("cs", 2)
                        csb = cs[ci_]
                        P.dma("sp", ("cs", ci_), [(csb[:, 0, :], cos_h.ap()[:, c * 512:(c + 1) * 512]),
                                                  (csb[:, 1, :], sin_h.ap()[:, c * 512:(c + 1) * 512])], writes=[("cs", ci_)])
                    for k in range(16):
                        ti = nxt("tp", 2)

                        def trf(e, k=k, ti=ti, ntile=ntile):
                            ins = None
                            for t in range(ntile):
                                ins = e.transpose(out=tp[ti][:, t * 128:(t + 1) * 128], in_=xst[t][:, k * 128:(k + 1) * 128], identity=ident[:])
                            return ins
                        P.op("pe", trf, reads=[("xst", t) for t in range(ntile)] + ["ident"], writes=[("tp", ti)])
                        sc_ap = modT[:, moff + 16 + k: moff + 17 + k]
                        sh_ap = modT[:, moff + k: moff + k + 1]
                        if nxt("ev", 2) == 0:
                            P.op("act", lambda e, k=k, ti=ti, W=W, sc_ap=sc_ap, sh_ap=sh_ap, xTc=xTc:
                                 e.activation(out=xTc[:, k, 0:W], in_=tp[ti][:, 0:W], func=AF.Identity, bias=sh_ap, scale=sc_ap),
                                 reads=[("tp", ti), "modT"], writes=[("xT", xi, k)])
                        else:
                            P.op("dve", lambda e, k=k, ti=ti, W=W, sc_ap=sc_ap, sh_ap=sh_ap, xTc=xTc:
                                 e.tensor_scalar(out=xTc[:, k, 0:W], in0=tp[ti][:, 0:W], scalar1=sc_ap, scalar2=sh_ap, op0=ALU.mult, op1=ALU.add),
                                 reads=[("tp", ti), "modT"], writes=[("xT", xi, k)])
                    if ci + 1 < len(chunks):
                        load_x(chunks[ci + 1])
                    xkeys = [("xT", xi, k) for k in range(16)]

                    heads = []
                    for h in range(4):
                        heads.append(("k", h))
                    if own:
                        for h in range(16):
                            heads.append(("q", h))
                    items = []

                    ci__ = ci_ if kind == "tok" else None

                    def make_head(hk, h, W=W, xTc=xTc, csb=csb, ci_=ci__, c=c, kind=kind, xkeys=xkeys, ci=ci):
                        info = {}

                        def stage1():
                            mi = nxt("mm", 3)
                            if hk == "k":
                                wsb, col0, wkey = wkv_sb, h * 128, "wkv"
                                isB = h >= 2
                            else:
                                wsb, col0, wkey = wq_sb, h * 128, "wq"
                                isB = h >= 8

                            def mmf(e, wsb=wsb, col0=col0, mi=mi):
                                ins = None
                                for k in range(16):
                                    ins = e.matmul(mmp[mi][:, 0:W], lhsT=wsb[:, k, col0:col0 + 128], rhs=xTc[:, k, 0:W], start=(k == 0), stop=(k == 15))
                                return ins
                            P.op("pe", mmf, reads=xkeys + [wkey], writes=[("mm", mi)])
                            yi = nxt("y", 3)
                            si = nxt("sq", 2)
                            if isB:
                                gap = gqs[:, 0:1] if hk == "q" else gcol[:, 1:2]
                                P.op("act", lambda e, yi=yi, mi=mi, gap=gap: e.activation(out=yb[yi][:, 0:W], in_=mmp[mi][:, 0:W], func=AF.Identity, scale=gap),
                                     reads=[("mm", mi), "gqs", "gcol"], writes=[("y", yi)])
                                P.op("act", lambda e, si=si, mi=mi: e.activation(out=sqf[si][:, 0:W], in_=mmp[mi][:, 0:W], func=AF.Square),
                                     reads=[("mm", mi)], writes=[("sqf", si)])
                            else:
                                sc = SCALE if hk == "q" else 1.0
                                P.op("act", lambda e, yi=yi, mi=mi, sc=sc: e.activation(out=yb[yi][:, 0:W], in_=mmp[mi][:, 0:W], func=AF.Identity, scale=sc),
                                     reads=[("mm", mi)], writes=[("y", yi)])
                                P.op("act", lambda e, si=si, yi=yi: e.activation(out=sqb[si][:, 0:W], in_=yb[yi][:, 0:W], func=AF.Square),
                                     reads=[("y", yi)], writes=[("sqb", si)])
                            if kind == "tok":
                                P.op("act", lambda e, yi=yi: e.activation(out=ybb[yi][:, 0:W], in_=yb[yi][:, 0:W], func=AF.Identity),
                                     reads=[("y", yi)], writes=[("ybb", yi)])
                            info.update(mi=mi, yi=yi, si=si, isB=isB)

                        def stage2():
                            mi, yi, si, isB = info["mi"], info["yi"], info["si"], info["isB"]
                            oi = nxt("ob", 4)
                            ob = outb[oi]
                            ri = None
                            if isB:
                                a_ms = nxt("aux", 2)
                                P.op("pe", lambda e, a_ms=a_ms, si=si: e.matmul(aux[a_ms][:, 0:W], lhsT=onesdb[:], rhs=sqf[si][:, 0:W], start=True, stop=True),
                                     reads=[("sqf", si), "onesd"], writes=[("aux", a_ms)])
                                ri = nxt("rr", 2)
                                P.op("act", lambda e, ri=ri, a_ms=a_ms: e.activation(out=rr[ri][:, 0:W], in_=aux[a_ms][:, 0:W], func=AF.Sqrt, bias=epst[:, 0:1], scale=1.0),
                                     reads=[("aux", a_ms), "epst"], writes=[("rr", ri)])
                                P.op("dve", lambda e, ri=ri: e.reciprocal(out=rr[ri][:, 0:W], in_=rr[ri][:, 0:W]),
                                     reads=[("rr", ri)], writes=[("rr", ri)])
                            else:
                                P.op("pe", lambda e, si=si: e.matmul(nrp[0:1, 0:W], lhsT=ones_b[:, 0:1], rhs=sqb[si][:, 0:W], start=True, stop=True),
                                     reads=[("sqb", si), "ones_b"], writes=["nrp"])
                                col = 0 if hk == "q" else 1
                                P.op("dve", lambda e: e.tensor_reduce(out=ntmp[:, 0:1], in_=nrp[0:1, 0:W], axis=AX.X, op=ALU.max),
                                     reads=["nrp"], writes=["ntmp"])
                                P.op("dve", lambda e, col=col: e.tensor_tensor(out=nrmmax[:, col:col + 1], in0=nrmmax[:, col:col + 1], in1=ntmp[:, 0:1], op=ALU.max),
                                     reads=["ntmp", "nrmmax"], writes=["nrmmax"])
                            if kind == "tok":
                                a_r = nxt("aux", 2)
                                P.op("pe", lambda e, a_r=a_r, yi=yi: e.matmul(aux[a_r][:, 0:W], lhsT=rotb[:], rhs=ybb[yi][:, 0:W], start=True, stop=True),
                                     reads=[("ybb", yi), "rotm"], writes=[("aux", a_r)])
                                ti_ = nxt("t", 2)
                                P.op("pool", lambda e, ti_=ti_, yi=yi: e.tensor_tensor(out=t1[ti_][:], in0=yb[yi][:], in1=csb[:, 0, :], op=ALU.mult),
                                     reads=[("y", yi), ("cs", ci_)], writes=[("t1", ti_)])
                                P.op("dve", lambda e, ti_=ti_, a_r=a_r: e.tensor_tensor(out=t2[ti_][:], in0=aux[a_r][:], in1=csb[:, 1, :], op=ALU.mult),
                                     reads=[("aux", a_r), ("cs", ci_)], writes=[("t2", ti_)])
                                if isB:
                                    P.op("pool", lambda e, ti_=ti_: e.tensor_tensor(out=t1[ti_][:], in0=t1[ti_][:], in1=t2[ti_][:], op=ALU.add),
                                         reads=[("t1", ti_), ("t2", ti_)], writes=[("t1", ti_)])
                                    P.op("dve", lambda e, ti_=ti_, ri=ri, ob=ob: e.tensor_tensor(out=ob[:], in0=t1[ti_][:], in1=rr[ri][:], op=ALU.mult),
                                         reads=[("t1", ti_), ("rr", ri)], writes=[("ob", oi)])
                                else:
                                    P.op("dve", lambda e, ti_=ti_, ob=ob: e.tensor_tensor(out=ob[:], in0=t1[ti_][:], in1=t2[ti_][:], op=ALU.add),
                                         reads=[("t1", ti_), ("t2", ti_)], writes=[("ob", oi)])
                            else:
                                if isB:
                                    P.op("dve", lambda e, yi=yi, ri=ri, ob=ob: e.tensor_tensor(out=ob[:, 0:W], in0=yb[yi][:, 0:W], in1=rr[ri][:, 0:W], op=ALU.mult),
                                         reads=[("y", yi), ("rr", ri)], writes=[("ob", oi)])
                                else:
                                    P.op("dve", lambda e, yi=yi, ob=ob: e.tensor_copy(out=ob[:, 0:W], in_=yb[yi][:, 0:W]),
                                         reads=[("y", yi)], writes=[("ob", oi)])
                            if hk == "k":
                                c0 = SEQ if kind == "ctx" else c * 512
                                P.dma("sp", ("ob", oi), [(KT[h, :, c0:c0 + W], ob[:, 0:W])], reads=[("ob", oi)], writes=[("KT", h, ci)])
                            else:
                                P.dma("sp", ("ob", oi), [(QT[h, :, c * 512:(c + 1) * 512], ob[:])], reads=[("ob", oi)], writes=[("QT", h, c)])
                        return stage1, stage2

                    def make_v(t, W=W, xTc=xTc, c=c, kind=kind, xkeys=xkeys):
                        def stage1():
                            mi = nxt("mm", 3)

                            def mmv(e, mi=mi):
                                ins = None
                                for k in range(16):
                                    ins = e.matmul(mmp[mi][:], lhsT=xTc[:, k, t * 128:(t + 1) * 128], rhs=wkv_sb[:, k, 512:1024], start=(k == 0), stop=(k == 15))
                                return ins
                            P.op("pe", mmv, reads=xkeys + ["wkv"], writes=[("mm", mi)])
                            vi = nxt("vo", 2)
                            P.op("act", lambda e, vi=vi, mi=mi: e.activation(out=vout[vi][:], in_=mmp[mi][:], func=AF.Identity),
                                 reads=[("mm", mi)], writes=[("vo", vi)])
                            r0 = (SEQ if kind == "ctx" else c * 512) + t * 128
                            P.dma("sp", ("vo", vi), [(Vd[r0:r0 + 128, :], vout[vi][:])], reads=[("vo", vi)], writes=[("Vd", r0)])
                        return stage1, (lambda: None)

                    for (hk, h) in heads:
                        items.append(make_head(hk, h))
                    vt = [make_v(t) for t in range(ntile)]
                    merged = []
                    step_v = max(1, len(items) // max(1, len(vt)))
                    for i, it in enumerate(items):
                        merged.append(it)
                        if (i + 1) % step_v == 0 and vt:
                            merged.append(vt.pop(0))
                    merged.extend(vt)
                    merged[0][0]()
                    for i in range(len(merged)):
                        if i + 1 < len(merged):
                            merged[i + 1][0]()
                        merged[i][1]()
                P.emit()

        if 2 in PHASES:
            with contextlib.ExitStack() as es:
                T = lambda name, shape, dt: es.enter_context(nc.sbuf_tensor(_uniq(name), list(shape), dt))
                KT_sb = T("KT_sb", [128, 4, NKEY], BF16)
                V_sb = T("V_sb", [128, 34, 512], BF16)
                QT_sb = [T(f"QT_sb{i}", [128, 16, 512], BF16) for i in range(2)]
                OT_sb = [T(f"OT_sb{i}", [128, 16, 512], BF16) for i in range(2)]
                NPT = 6
                pT = [T(f"pT{i}", [128, 512], BF16) for i in range(NPT)]
                accD = [T(f"accD{i}", [128, 512], F32) for i in range(2)]
                accP = [T(f"accP{i}", [128, 512], F32) for i in range(2)]
                wr2 = [T(f"wr2{i}", [128, 16, 512], BF16) for i in range(2)]
                bada2 = [T(f"bada2{i}", [1, 512], F32) for i in range(2)]
                mr2 = [T(f"mr2{i}", [1, 512], F32) for i in range(2)]
                msk = T("msk", [128, 4, 512], F32)
                rl = [T(f"rl{i}", [128, 512], F32) for i in range(2)]
                sinkrow = T("sinkrow", [128, 2, 512], F32)
                sm = T("sm", [1, 16], F32)
                Sps = [es.enter_context(nc.psum_tensor(_uniq(f"Sps{i}"), [128, 512], F32)) for i in range(3)]
                Ops = [es.enter_context(nc.psum_tensor(_uniq(f"Ops{i}"), [128, 512], F32)) for i in range(2)]
                Lps = [es.enter_context(nc.psum_tensor(_uniq(f"Lps{i}"), [128, 512], F32)) for i in range(2)]
                mps = es.enter_context(nc.psum_tensor(_uniq("mps"), [128, 512], F32))
                P = Phase(nc, "p2")
                for h in range(4):
                    P.dma("sp", ("kt", h), [(KT_sb[:, h, :], KT[h])], writes=[("KT", h)])
                Vv = Vd.rearrange("(b p) c -> p b c", p=128)
                for q4 in range(4):
                    b0, b1 = q4 * 9, min(34, q4 * 9 + 9)
                    P.dma("sp", ("v", q4), [(V_sb[:, b0:b1, :], Vv[:, b0:b1, :])], writes=[("V", b) for b in range(b0, b1)])
                P.dma("sp", "msk", [(msk[:, i, :], masks_h.ap()[i]) for i in range(4)], writes=["msk"])
                sink_sb = T("sink_sb", [1, 8], F32)
                P.dma("sp", "sink", [(sink_sb[:], sink_h.ap())], writes=["sink_sb"])
                P.op("dve", lambda e: e.tensor_tensor(out=sm[:, 0:1], in0=nrmmax[:, 0:1], in1=nrmmax[:, 1:2], op=ALU.mult), writes=["sm0"])
                P.op("act", lambda e: e.activation(out=sm[:, 1:2], in_=sm[:, 0:1], func=AF.Sqrt), reads=["sm0"], writes=["sm1"])
                P.op("dve", lambda e: e.tensor_scalar(out=sm[:, 2:3], in0=sm[:, 1:2], scalar1=-1.02, scalar2=None, op0=ALU.mult), reads=["sm1"], writes=["sm2"])
                P.op("act", lambda e: e.activation(out=sm[:, 8:16], in_=sink_sb[:], func=AF.Exp, bias=sm[:, 2:3], scale=1.0),
                     reads=["sm2", "sink_sb"], writes=["sm8"])
                P.op("pe", lambda e: e.matmul(mps[:, 0:1], lhsT=ones_f[0:1, :], rhs=sm[:, 2:3], start=True, stop=True), reads=["sm2"], writes=["mps"])
                P.op("dve", lambda e: e.tensor_copy(out=negcA[:], in_=mps[:, 0:1]), reads=["mps"], writes=["negcA"])
                P.op("pe", lambda e: e.matmul(mps[:, 0:8], lhsT=ones_f[0:1, :], rhs=sm[:, 8:16], start=True, stop=True), reads=["sm8", "negcA"], writes=["mps"])
                P.op("dve", lambda e: e.tensor_copy(out=sinkb[:], in_=mps[:, 0:8]), reads=["mps"], writes=["sinkb"])
                for g in range(2):
                    for h in range(4):
                        P.op("dve", lambda e, g=g, h=h: e.tensor_scalar(out=sinkrow[:, g, h * 128:(h + 1) * 128], in0=ones_f[:], scalar1=sinkb[:, 4 * g + h:4 * g + h + 1],
                                                                         scalar2=None, op0=ALU.mult),
                             reads=["sinkb"], writes=[("sinkrow", g)])

                steps = []
                for qc in range(4):
                    cb = qc % 2
                    bgroups = []
                    agroups = []
                    for g in range(2):
                        for h in range(4):
                            head = 8 + 4 * g + h
                            bgroups.append([])
                            for sb in range(34):
                                bgroups[-1].append(dict(
                                    qc=qc, cb=cb, kind="B", first=(sb == 0), last=(sb == 33),
                                    lhsT=KT_sb[:, 2 + g, sb * 128:(sb + 1) * 128], kkey=("KT", 2 + g),
                                    rhs=QT_sb[cb][:, head, :],
                                    vs=V_sb[:, sb, 256 + g * 128:256 + (g + 1) * 128], vkey=("V", sb),
                                    bias=negcB, mask=None, head=head, g=g, i=None, sb=sb))
                    for g in range(2):
                        for i in range(4):
                            agroups.append([])
                            qb = qc * 4 + i
                            blocks = []
                            if qb == 0:
                                blocks.append((31, 2))
                            else:
                                blocks.append((qb - 1, 0))
                            blocks.append((qb, None))
                            if qb == 15:
                                blocks.append((16, 3))
                            else:
                                blocks.append((qb + 1, 1))
                            blocks.append((32, None))
                            blocks.append((33, None))
                            for bi, (kb, mk) in enumerate(blocks):
                                agroups[-1].append(dict(
                                    qc=qc, cb=cb, kind="A", first=(bi == 0), last=(bi == len(blocks) - 1),
                                    lhsT=KT_sb[:, g, kb * 128:(kb + 1) * 128], kkey=("KT", g),
                                    rhs=QT_sb[cb][:, 4 * g:4 * g + 4, i * 128:(i + 1) * 128],
                                    vs=V_sb[:, kb, g * 128:(g + 1) * 128], vkey=("V", kb),
                                    bias=negcA, mask=mk, head=None, g=g, i=i))
                    for bg, ag in zip(bgroups, agroups):
                        steps.extend(bg)
                        steps.extend(ag)
                nst = len(steps)
                acc_idx = 0
                for st in steps:
                    if st["first"]:
                        acc_idx += 1
                    st["acc"] = acc_idx % 2

                loaded_qc = set()

                def load_q(qc):
                    if qc in loaded_qc or qc >= 4:
                        return
                    loaded_qc.add(qc)
                    cb = qc % 2
                    P.dma("sp", ("qt", cb), [(QT_sb[cb][:, h, :], QT[h, :, qc * 512:(qc + 1) * 512]) for h in range(16)], writes=[("QT", cb)])

                def rec_qk(j):
                    st = steps[j]
                    si = j % 3
                    rhs = st["rhs"]
                    P.op("pe", lambda e, st=st, si=si: e.matmul(Sps[si][:], lhsT=st["lhsT"], rhs=st["rhs"], start=True, stop=True),
                         reads=[st["kkey"], ("QT", st["cb"])], writes=[("S", si)])
                    pi = j % NPT
                    P.op("act", lambda e, st=st, si=si, pi=pi: e.activation(out=pT[pi][:], in_=Sps[si][:], func=AF.Exp, bias=st["bias"][:, 0:1], scale=1.0),
                         reads=[("S", si), "negcA", "negcB"], writes=[("pT", pi)])
                    if st["mask"] is not None:
                        mk = st["mask"]
                        P.op("dve", lambda e, pi=pi, mk=mk: e.tensor_tensor(out=pT[pi][:], in0=pT[pi][:], in1=msk[:, mk, :], op=ALU.mult),
                             reads=[("pT", pi), "msk"], writes=[("pT", pi)])

                def rec_pv(j):
                    st = steps[j]
                    pi = j % NPT
                    a = st["acc"]
                    if st["kind"] == "A" or True:
                        def f(e, st=st, pi=pi, a=a):
                            e.matmul(Ops[a][:], lhsT=st["vs"], rhs=pT[pi][:], start=st["first"], stop=st["last"])
                            return e.matmul(Lps[a][:], lhsT=ones_b[:], rhs=pT[pi][:], start=st["first"], stop=st["last"])
                        P.op("pe", f, reads=[("pT", pi), st["vkey"], "ones_b"], writes=[("O", a), ("L", a)])
                    else:
                        P.op("pe", lambda e, st=st, pi=pi, a=a: e.matmul(Ops[a][:], lhsT=st["vs"], rhs=pT[pi][:], start=st["first"], stop=st["last"]),
                             reads=[("pT", pi), st["vkey"]], writes=[("O", a)])
                        sb = st["sb"]
                        if sb % 3 == 2:
                            eng, acc, akey = "pool", accP[a], ("accP", a)
                            first = (sb == 2)
                        else:
                            eng, acc, akey = "dve", accD[a], ("accD", a)
                            first = (sb == 0)
                        if first:
                            P.op(eng, lambda e, acc=acc, pi=pi: e.tensor_copy(out=acc[:], in_=pT[pi][:]), reads=[("pT", pi)], writes=[akey])
                        else:
                            P.op(eng, lambda e, acc=acc, pi=pi: e.tensor_tensor(out=acc[:], in0=acc[:], in1=pT[pi][:], op=ALU.add), reads=[("pT", pi), akey], writes=[akey])
                        if st["last"]:
                            P.op("dve", lambda e, a=a: e.tensor_tensor(out=accD[a][:], in0=accD[a][:], in1=accP[a][:], op=ALU.add),
                                 reads=[("accD", a), ("accP", a)], writes=[("accD", a)])
                            P.op("pe", lambda e, a=a: e.matmul(Lps[a][:], lhsT=ones_f[:], rhs=accD[a][:], start=True, stop=True),
                                 reads=[("accD", a)], writes=[("L", a)])
                    if st["last"]:
                        ri = a
                        cb = st["cb"]
                        if st["kind"] == "B":
                            P.op("dve", lambda e, a=a, ri=ri: e.reciprocal(out=rl[ri][:], in_=Lps[a][:]), reads=[("L", a)], writes=[("rl", ri)])
                            P.op("dve", lambda e, a=a, ri=ri, st=st, cb=cb: e.tensor_tensor(out=OT_sb[cb][:, st["head"], :], in0=Ops[a][:], in1=rl[ri][:], op=ALU.mult),
                                 reads=[("O", a), ("rl", ri)], writes=[("OT", cb, st["head"])])
                        else:
                            g, i = st["g"], st["i"]
                            P.op("dve", lambda e, a=a, ri=ri, g=g: e.tensor_tensor(out=rl[ri][:], in0=Lps[a][:], in1=sinkrow[:, g, :], op=ALU.add),
                                 reads=[("L", a), ("sinkrow", g)], writes=[("rl", ri)])
                            P.op("dve", lambda e, ri=ri: e.reciprocal(out=rl[ri][:], in_=rl[ri][:]), reads=[("rl", ri)], writes=[("rl", ri)])
                            okeys = [("OT", cb, 4 * g + h) for h in range(4)]
                            P.op("dve", lambda e, a=a, ri=ri, g=g, i=i, cb=cb: e.tensor_tensor(
                                out=OT_sb[cb][:, 4 * g:4 * g + 4, i * 128:(i + 1) * 128],
                                in0=Ops[a][:].rearrange("p (h q) -> p h q", h=4),
                                in1=rl[ri][:].rearrange("p (h q) -> p h q", h=4), op=ALU.mult),
                                 reads=[("O", a), ("rl", ri)] + okeys, writes=okeys)
                        if j + 1 == nst or steps[j + 1]["qc"] != st["qc"]:
                            qc = st["qc"]
                            P.dma("pool", ("ot", cb), [(OT[h, :, qc * 512:(qc + 1) * 512], OT_sb[cb][:, h, :]) for h in range(16)],
                                  reads=[("OT", cb, h) for h in range(16)], writes=[("OTd", qc)])

                wv2 = wada_h.ap().rearrange("(k p) c -> p k c", p=128)
                NG2 = 16

                def ada_load(g):
                    if g >= NG2:
                        return
                    r = g % 2
                    c0 = 4096 + g * 512
                    P.dma("pool", ("wr2", r), [(wr2[r][:], wv2[:, :, c0:c0 + 512])], writes=[("wr2", r)])
                    P.dma("sp", ("bada2", r), [(bada2[r][:], bada_h.ap()[0:1, c0:c0 + 512])], writes=[("bada2", r)])

                def ada_group(g):
                    r = g % 2
                    c0 = 4096 + g * 512

                    def mm(e, r=r):
                        ins = None
                        for k in range(16):
                            ins = e.matmul(mps[0:1, :], lhsT=condb2[:, k, 0:1], rhs=wr2[r][:, k, :], start=(k == 0), stop=(k == 15))
                        return ins
                    P.op("pe", mm, reads=[("wr2", r)], writes=["mps"])
                    P.op("dve", lambda e, r=r: e.tensor_tensor(out=mr2[r][:], in0=mps[0:1, :], in1=bada2[r][:], op=ALU.add),
                         reads=["mps", ("bada2", r)], writes=[("mr2", r)])
                    P.dma("sp", ("mr2", r), [(mrow[0:1, c0:c0 + 512], mr2[r][:])], reads=[("mr2", r)], writes=[("mrow2", g)])
                    ada_load(g + 2)

                ada_load(0)
                ada_load(1)
                ada_every = nst // (NG2 + 1)
                load_q(0)
                load_q(1)
                LOOK = 2
                for j in range(min(LOOK, nst)):
                    rec_qk(j)
                for j in range(nst):
                    if j + LOOK < nst:
                        nq = steps[j + LOOK]["qc"]
                        rec_qk(j + LOOK)
                    rec_pv(j)
                    if (j + 1) % ada_every == 0 and (j + 1) // ada_every <= NG2:
                        ada_group((j + 1) // ada_every - 1)
                    if steps[j]["last"] and (j + 1 < nst) and steps[j + 1]["qc"] != steps[j]["qc"]:
                        load_q(steps[j]["qc"] + 2)
                P.emit()

        if 3 in PHASES:
            with contextlib.ExitStack() as es:
                T = lambda name, shape, dt: es.enter_context(nc.sbuf_tensor(_uniq(name), list(shape), dt))
                wout_sb = T("wout_sb", [128, 16, D], BF16)
                OTc = [T(f"OTc{i}", [128, 16, 512], BF16) for i in range(2)]
                xt = [T(f"xt{i}", [128, D], F32) for i in range(2)]
                zt = [T(f"zt{i}", [128, D], F32) for i in range(3)]
                g1b = T("g1b", [128, D], F32)
                lg = T("lg", [128, D], F32)
                lb = T("lb", [128, D], F32)
                x1Tc = [T(f"x1Tc{i}", [128, 16, 512], BF16) for i in range(2)]
                st6 = [T(f"st6{i}", [128, 4, 6], F32) for i in range(2)]
                mv = [T(f"mv{i}", [128, 4], F32) for i in range(2)]
                aps = [es.enter_context(nc.psum_tensor(_uniq(f"aps{i}"), [128, 512], F32)) for i in range(6)]
                tps = [es.enter_context(nc.psum_tensor(_uniq(f"tps{i}"), [128, 512], F32)) for i in range(2)]
                P = Phase(nc, "p3")
                wo_v = wout_h.ap().rearrange("(k p) c -> p k c", p=128)
                P.dma("pool", "wo", [(wout_sb[:, 4 * k:4 * k + 4, :], wo_v[:, 4 * k:4 * k + 4, :]) for k in range(4)], writes=["wo"])
                P.dma("sp", "bc", [(g1b[:], bc(mrow_h, 2 * D, D)), (lg[:], bc(lnp_h, 0, D)), (lb[:], bc(lnp_h, D, D))], writes=["g1b", "lg", "lb"])
                P.dma("sp", "m2", [(modT[:, 32:96], mrow[0:1, 4096:12288].rearrange("o (j p) -> p (o j)", p=128))], writes=["modT"], slow=True)
                P.op("dve", lambda e: e.tensor_scalar(out=modT[:, 64:80], in0=modT[:, 64:80], scalar1=1.0, scalar2=None, op0=ALU.add), reads=["modT"], writes=["modT"])
                _ln_phase(nc, P, 16, lambda tt: x[tt * 128:(tt + 1) * 128, :], OT, OTc, wout_sb, "wo", xt, zt, g1b, lg, lb, st6, mv, aps, tps,
                          X1, X1T, x1Tc, modT, ident, epst)
                P.emit()

        if 4 in PHASES:
            with contextlib.ExitStack() as es4:
                hT = es4.enter_context(nc.sbuf_tensor(_uniq("hT"), [128, NF, 1024], BF16))
                for half in range(2):
                    t0 = half * 1024
                    with contextlib.ExitStack() as es:
                        T = lambda name, shape, dt: es.enter_context(nc.sbuf_tensor(_uniq(name), list(shape), dt))
                        x1T_sb = T("x1T_sb", [128, 16, 1024], BF16)
                        wgr = [T(f"wgr{i}", [128, 16, 128], BF16) for i in range(4)]
                        wur = [T(f"wur{i}", [128, 16, 128], BF16) for i in range(4)]
                        sg = [T(f"sg{i}", [128, 512], F32) for i in range(2)]
                        Gp = [es.enter_context(nc.psum_tensor(_uniq(f"Gp{i}"), [128, 512], F32)) for i in range(4)]
                        Up = [es.enter_context(nc.psum_tensor(_uniq(f"Up{i}"), [128, 512], F32)) for i in range(4)]
                        P = Phase(nc, f"p4a{half}")
                        xs1 = [T(f"xs1{i}", [128, D], F32) for i in range(2)]
                        evn = 0
                        tpn = 0
                        for tl in range(8):
                            xi = tl % 2
                            P.dma("sp", ("xs1", xi), [(xs1[xi][:], X1[t0 + tl * 128:t0 + (tl + 1) * 128, :])], writes=[("xs1", xi)])
                            for k in range(0, 16, 4):
                                ti = tpn % 4
                                tpn += 1

                                def trf(e, k=k, ti=ti, xi=xi):
                                    ins = None
                                    for kk in range(4):
                                        ins = e.transpose(out=Gp[ti][:, kk * 128:(kk + 1) * 128], in_=xs1[xi][:, (k + kk) * 128:(k + kk + 1) * 128], identity=ident[:])
                                    return ins
                                P.op("pe", trf, reads=[("xs1", xi)], writes=[("G", ti)])
                                for kk in range(4):
                                    kq = k + kk
                                    sc_ap = modT[:, 64 + kq:65 + kq]
                                    sh_ap = modT[:, 48 + kq:49 + kq]
                                    dst = x1T_sb[:, kq, tl * 128:(tl + 1) * 128]
                                    if tpn % 2 == 0:
                                        P.op("act", lambda e, ti=ti, kk=kk, sc_ap=sc_ap, sh_ap=sh_ap, dst=dst: e.activation(out=dst, in_=Gp[ti][:, kk * 128:(kk + 1) * 128], func=AF.Identity, bias=sh_ap, scale=sc_ap),
                                             reads=[("G", ti)], writes=[("x1T", tl, kq)])
                                    else:
                                        P.op("dve", lambda e, ti=ti, kk=kk, sc_ap=sc_ap, sh_ap=sh_ap, dst=dst: e.tensor_scalar(out=dst, in0=Gp[ti][:, kk * 128:(kk + 1) * 128], scalar1=sc_ap, scalar2=sh_ap, op0=ALU.mult, op1=ALU.add),
                                             reads=[("G", ti)], writes=[("x1T", tl, kq)])
                                    evn += 1
                        x1keys = {tc_: [("x1T", tl, kq) for tl in range(tc_ * 4, tc_ * 4 + 4) for kq in range(16)] for tc_ in range(2)}

                        def loadw(f):
                            r = f % 4
                            P.dma("pool", ("wg", r), [(wgr[r][:], wg_h.ap()[f]), (wur[r][:], wu_h.ap()[f])], writes=[("wg", r)])
                        for f in range(3):
                            loadw(f)
                        k_ = 0
                        for f in range(NF):
                            r = f % 4
                            for tc_ in range(2):
                                pi = (f * 2 + tc_) % 4

                                def mg(e, r=r, tc_=tc_, pi=pi):
                                    ins = None
                                    for k in range(16):
                                        ins = e.matmul(Gp[pi][:], lhsT=wgr[r][:, k, :], rhs=x1T_sb[:, k, tc_ * 512:(tc_ + 1) * 512], start=(k == 0), stop=(k == 15))
                                    return ins

                                def mu(e, r=r, tc_=tc_, pi=pi):
                                    ins = None
                                    for k in range(16):
                                        ins = e.matmul(Up[pi][:], lhsT=wur[r][:, k, :], rhs=x1T_sb[:, k, tc_ * 512:(tc_ + 1) * 512], start=(k == 0), stop=(k == 15))
                                    return ins
                                P.op("pe", mg, reads=[("wg", r)] + x1keys[tc_], writes=[("G", pi)])
                                P.op("pe", mu, reads=[("wg", r)] + x1keys[tc_], writes=[("U", pi)])
                                si = k_ % 2
                                k_ += 1
                                P.op("act", lambda e, si=si, pi=pi: e.activation(out=sg[si][:], in_=Gp[pi][:], func=AF.Silu), reads=[("G", pi)], writes=[("sg", si)])
                                P.op("dve", lambda e, si=si, pi=pi, f=f, tc_=tc_: e.tensor_tensor(out=hT[:, f, tc_ * 512:(tc_ + 1) * 512], in0=Up[pi][:], in1=sg[si][:], op=ALU.mult),
                                     reads=[("U", pi), ("sg", si)], writes=[("hT", f, tc_)])
                            if f + 3 < NF:
                                loadw(f + 3)
                        P.emit()
                    with contextlib.ExitStack() as es:
                        T = lambda name, shape, dt: es.enter_context(nc.sbuf_tensor(_uniq(name), list(shape), dt))
                        wdr = [T(f"wdr{i}", [128, NF, 512], BF16) for i in range(2)]
                        x1p = [T(f"x1p{i}", [128, 512], F32) for i in range(2)]
                        zp = [T(f"zp{i}", [128, 512], F32) for i in range(2)]
                        g2b = T("g2b", [128, D], F32)
                        Yp = [es.enter_context(nc.psum_tensor(_uniq(f"Yp{i}"), [128, 512], F32)) for i in range(4)]
                        P = Phase(nc, f"p4b{half}")
                        P.dma("sp", "g2b", [(g2b[:], bc(mrow_h, 5 * D, D))], writes=["g2b"])

                        def loadwd(dg):
                            r = dg % 2
                            for q4 in range(4):
                                P.dma("pool", ("wd", r, q4), [(wdr[r][:, q4 * 11:(q4 + 1) * 11, :], wd_h.ap()[dg][:, q4 * 11:(q4 + 1) * 11, :])], writes=[("wd", r, q4)])
                        loadwd(0)
                        loadwd(1)
                        n_ = 0
                        for dg in range(4):
                            r = dg % 2
                            for tl in range(8):
                                yi = n_ % 4
                                xi = n_ % 2
                                n_ += 1
                                row0 = t0 + tl * 128
                                P.dma("sp", ("x1p", xi), [(x1p[xi][:], X1[row0:row0 + 128, dg * 512:(dg + 1) * 512])], writes=[("x1p", xi)])

                                for q4 in range(4):
                                    def md(e, r=r, tl=tl, yi=yi, q4=q4):
                                        ins = None
                                        for fk in range(q4 * 11, (q4 + 1) * 11):
                                            ins = e.matmul(Yp[yi][:], lhsT=hT[:, fk, tl * 128:(tl + 1) * 128], rhs=wdr[r][:, fk, :], start=(fk == 0), stop=(fk == NF - 1))
                                        return ins
                                    P.op("pe", md, reads=[("wd", r, q4)], writes=[("Y", yi)])
                                P.op("dve", lambda e, xi=xi, yi=yi, dg=dg: e.tensor_tensor(out=zp[xi][:], in0=Yp[yi][:], in1=g2b[:, dg * 512:(dg + 1) * 512], op=ALU.mult),
                                     reads=[("Y", yi), "g2b"], writes=[("zp", xi)])
                                P.op("dve", lambda e, xi=xi: e.scalar_tensor_tensor(out=zp[xi][:], in0=x1p[xi][:], scalar=DN_ALPHA, in1=zp[xi][:], op0=ALU.mult, op1=ALU.add),
                                     reads=[("zp", xi), ("x1p", xi)], writes=[("zp", xi)])
                                P.dma("sp", ("zp", xi), [(Z[row0:row0 + 128, dg * 512:(dg + 1) * 512], zp[xi][:])], reads=[("zp", xi)], writes=[("Z", n_)])
                            if dg + 2 < 4:
                                loadwd(dg + 2)
                        P.emit()

        if 5 in PHASES:
            with contextlib.ExitStack() as es:
                T = lambda name, shape, dt: es.enter_context(nc.sbuf_tensor(_uniq(name), list(shape), dt))
                zt = [T(f"z5{i}", [128, D], F32) for i in range(4)]
                lg = T("lg5", [128, D], F32)
                lb = T("lb5", [128, D], F32)
                st6 = [T(f"st65{i}", [128, 4, 6], F32) for i in range(2)]
                mv = [T(f"mv5{i}", [128, 4], F32) for i in range(2)]
                P = Phase(nc, "p5")
                P.dma("sp", "bc", [(lg[:], bc(lnp_h, 2 * D, D)), (lb[:], bc(lnp_h, 3 * D, D))], writes=["lg", "lb"])

                def ld5(tt):
                    if tt < 16:
                        P.dma("sp", ("z", tt % 4), [(zt[tt % 4][:], Z[tt * 128:(tt + 1) * 128, :])], writes=[("z", tt % 4)])

                def st5(tt):
                    zi = tt % 4
                    _ln_bias(P, zt[zi], [("z", zi)], lb)
                    P.dma("sp", ("z", zi), [(out[tt * 128:(tt + 1) * 128, :], zt[zi][:])], reads=[("z", zi)], writes=[("out", tt)])
                for tt in range(3):
                    ld5(tt)
                for tt in range(16):
                    zi = tt % 4
                    _layer_norm(P, zt[zi], [("z", zi)], st6[tt % 2], mv[tt % 2], tt % 2, lg, lb, epst, defer_bias=True)
                    if tt >= 1:
                        st5(tt - 1)
                    ld5(tt + 3)
                st5(15)
                P.emit()
    return nc


def _layer_norm(P, z, zkeys, st6, mv, si, lg, lb, epst, defer_bias=False):
    zkeys = list(zkeys)
    def stats(e):
        ins = None
        for q in range(4):
            ins = e.bn_stats(out=st6[:, q, :], in_=z[:, q * 512:(q + 1) * 512])
        return ins
    P.op("dve", stats, reads=zkeys, writes=[("st6", si)])
    P.op("dve", lambda e: e.bn_aggr(out=mv[:, 0:2], in_=st6[:].rearrange("p a b -> p (a b)")), reads=[("st6", si)], writes=[("mv", si)])
    P.op("act", lambda e: e.activation(out=mv[:, 2:3], in_=mv[:, 1:2], func=AF.Sqrt, bias=epst[:, 0:1], scale=1.0), reads=[("mv", si)], writes=[("mv2", si)])
    P.op("dve", lambda e: e.reciprocal(out=mv[:, 2:3], in_=mv[:, 2:3]), reads=[("mv2", si)], writes=[("mv2", si)])
    P.op("dve", lambda e: e.scalar_tensor_tensor(out=mv[:, 3:4], in0=mv[:, 0:1], scalar=-1.0, in1=mv[:, 2:3], op0=ALU.mult, op1=ALU.mult),
         reads=[("mv", si), ("mv2", si)], writes=[("mv3", si)])
    P.op("act", lambda e: e.activation(out=z[:], in_=z[:], func=AF.Identity, bias=mv[:, 3:4], scale=mv[:, 2:3]),
         reads=zkeys + [("mv2", si), ("mv3", si)], writes=zkeys)
    P.op("pool", lambda e: e.tensor_tensor(out=z[:], in0=z[:], in1=lg[:], op=ALU.mult), reads=zkeys + ["lg"], writes=zkeys)
    if not defer_bias:
        _ln_bias(P, z, zkeys, lb)


def _ln_bias(P, z, zkeys, lb):
    zkeys = list(zkeys)
    P.op("dve", lambda e: e.tensor_tensor(out=z[:], in0=z[:], in1=lb[:], op=ALU.add), reads=zkeys + ["lb"], writes=zkeys)


def _ln_phase(nc, P, ntiles, xsrc, OT, OTc, wout_sb, wkey, xt, zt, g1b, lg, lb, st6, mv, aps, tps, X1, X1T, x1Tc, modT, ident, epst):
    nb = [0]

    def stage_a(tt):
        qc, tl = tt // 4, tt % 4
        cb = qc % 2
        if tt == 0:
            P.dma("sp", ("otc", 0), [(OTc[0][:, h, :], OT[h, :, 0:512]) for h in range(16)], writes=[("OTc", 0)])
        if tl == 1 and qc + 1 < ntiles // 4:
            nq = qc + 1
            P.dma("sp", ("otc", nq % 2), [(OTc[nq % 2][:, h, :], OT[h, :, nq * 512:(nq + 1) * 512]) for h in range(16)], writes=[("OTc", nq % 2)])
        xi = tt % 2
        zi = tt % 3
        if tt == 0:
            P.dma("sp", ("xt", 0), [(xt[0][:], xsrc(0))], writes=[("xt", 0)])
        if tt + 1 < ntiles:
            P.dma("sp", ("xt", (tt + 1) % 2), [(xt[(tt + 1) % 2][:], xsrc(tt + 1))], writes=[("xt", (tt + 1) % 2)])
        z = zt[zi]
        zkeys = [("z", zi, dg) for dg in range(4)]
        for dg in range(4):
            bi = nb[0] % 6
            nb[0] += 1

            def mo(e, bi=bi, dg=dg, cb=cb, tl=tl):
                ins = None
                for h in range(16):
                    ins = e.matmul(aps[bi][:], lhsT=OTc[cb][:, h, tl * 128:(tl + 1) * 128], rhs=wout_sb[:, h, dg * 512:(dg + 1) * 512], start=(h == 0), stop=(h == 15))
                return ins
            P.op("pe", mo, reads=[("OTc", cb), wkey], writes=[("aps", bi)])
            P.op("dve", lambda e, bi=bi, dg=dg, z=z: e.tensor_tensor(out=z[:, dg * 512:(dg + 1) * 512], in0=aps[bi][:], in1=g1b[:, dg * 512:(dg + 1) * 512], op=ALU.mult),
                 reads=[("aps", bi), "g1b"], writes=[("z", zi, dg)])
        P.op("dve", lambda e, z=z, xi=xi: e.scalar_tensor_tensor(out=z[:], in0=xt[xi][:], scalar=DN_ALPHA, in1=z[:], op0=ALU.mult, op1=ALU.add),
             reads=[("xt", xi)] + zkeys, writes=zkeys)
        si = tt % 2
        _layer_norm(P, z, zkeys, st6[si], mv[si], si, lg, lb, epst, defer_bias=True)

    def stage_b(tt):
        zi = tt % 3
        z = zt[zi]
        zkeys = [("z", zi, dg) for dg in range(4)]
        _ln_bias(P, z, zkeys, lb)
        P.dma("sp", ("zst", zi), [(X1[tt * 128:(tt + 1) * 128, :], z[:])], reads=zkeys, writes=[("X1", tt)])

    for tt in range(ntiles):
        stage_a(tt)
        if tt >= 1:
            stage_b(tt - 1)
    stage_b(ntiles - 1)


def _rope_tables():
    rows = SEQ // GRID_W
    row_ids = np.repeat(np.arange(rows, dtype=np.float64), GRID_W)
    col_ids = np.tile(np.arange(GRID_W, dtype=np.float64), rows)
    axis_dim = 64
    inv_freq = np.power(10000.0, -np.arange(0, axis_dim, 2, dtype=np.float64) / axis_dim)
    ang_r = row_ids[:, None] * inv_freq
    ang_c = col_ids[:, None] * inv_freq
    ang = np.concatenate([ang_r, ang_r, ang_c, ang_c], axis=-1)
    return np.cos(ang).astype(np.float32), np.sin(ang).astype(np.float32)


def _consts():
    ident = np.eye(128, dtype=np.float32)
    rotm = np.zeros((128, 128), np.float32)
    for base in (0, 64):
        for j in range(32):
            rotm[base + 32 + j, base + j] = -1.0
            rotm[base + j, base + 32 + j] = 1.0
    jj = np.arange(128)[:, None]
    ii = np.arange(128)[None, :]
    m_prev = (jj >= ii).astype(np.float32)
    m_next = (jj <= ii).astype(np.float32)
    return ident, rotm, m_prev, m_next


_NC_CACHE = {}


def make_in_maps(x, c, ctx, c_ctx, w_ada, b_ada, w_in, q_norm_g, k_norm_g, sink_logit,
                 w_out, ln1_g, ln1_b, w_gate, w_up, w_down, ln2_g, ln2_b):
    f = lambda a: np.ascontiguousarray(np.asarray(a, dtype=np.float32))
    x, c, ctx, c_ctx = f(x), f(c), f(ctx), f(c_ctx)
    w_ada, b_ada, w_in = f(w_ada)[0], f(b_ada), f(w_in)[0]
    w_out, w_gate, w_up, w_down = f(w_out)[0], f(w_gate)[0], f(w_up)[0], f(w_down)[0]
    qg, kg = f(q_norm_g)[0], f(k_norm_g)[0]
    cosN, sinN = _rope_tables()
    ident, rotm, m_prev, m_next = _consts()
    wq = np.ascontiguousarray(np.concatenate([w_in[:, 0:1024], w_in[:, 1536:2560]], axis=1))
    wkv = np.ascontiguousarray(np.concatenate([w_in[:, 1024:1280], w_in[:, 2560:2816], w_in[:, 1280:1536], w_in[:, 2816:3072]], axis=1))
    wg = np.ascontiguousarray(w_gate.reshape(16, 128, NF, 128).transpose(2, 1, 0, 3))
    wu = np.ascontiguousarray(w_up.reshape(16, 128, NF, 128).transpose(2, 1, 0, 3))
    wd = np.ascontiguousarray(w_down.reshape(NF, 128, 4, 512).transpose(2, 1, 0, 3))
    gcol = np.ascontiguousarray(np.stack([qg, kg], axis=1))
    grow = np.ascontiguousarray(np.concatenate([qg, kg])[None, :])
    sink = f(sink_logit).reshape(1, 8)
    lnp = np.ascontiguousarray(np.stack([f(ln1_g)[0], f(ln1_b)[0], f(ln2_g)[0], f(ln2_b)[0]], axis=0))
    rep4 = lambda m: np.tile(m, (1, 4))
    zeros = np.zeros((128, 512), np.float32)
    in_maps = []
    for core in range(8):
        b, hf = core // 2, core % 2
        own = slice(hf * NOWN, (hf + 1) * NOWN)
        oth = slice((1 - hf) * NOWN, (2 - hf) * NOWN)
        xp = np.ascontiguousarray(np.concatenate([x[b, own], x[b, oth]], axis=0))
        cosT = np.ascontiguousarray(np.concatenate([cosN[own], cosN[oth]], axis=0).T)
        sinT = np.ascontiguousarray(np.concatenate([sinN[own], sinN[oth]], axis=0).T)
        cond = np.stack([c[b], c_ctx], axis=1)
        condT = np.ascontiguousarray(cond.reshape(16, 128, 2).transpose(1, 0, 2))
        masks = np.stack([rep4(m_prev), rep4(m_next),
                          rep4(m_prev) if hf == 1 else zeros,
                          rep4(m_next) if hf == 0 else zeros], axis=0)
        in_maps.append({
            "x": xp, "ctx": np.ascontiguousarray(ctx[b]), "condT": condT, "w_ada": w_ada, "b_ada": b_ada.reshape(1, -1),
            "wq": wq, "wkv": wkv, "w_out": w_out, "wg": wg, "wu": wu, "wd": wd,
            "cosT": cosT, "sinT": sinT, "ident": ident, "rotm": rotm, "gcol": gcol, "grow": grow,
            "sink": sink, "lnp": lnp, "masks": np.ascontiguousarray(masks),
        })
    return in_maps


def kernel(x, c, ctx, c_ctx, w_ada, b_ada, w_in, q_norm_g, k_norm_g, sink_logit,
           w_out, ln1_g, ln1_b, w_gate, w_up, w_down, ln2_g, ln2_b):
    in_maps = make_in_maps(x, c, ctx, c_ctx, w_ada, b_ada, w_in, q_norm_g, k_norm_g, sink_logit,
                           w_out, ln1_g, ln1_b, w_gate, w_up, w_down, ln2_g, ln2_b)
    nc = build_program()
    res = run_bass_kernel_spmd(nc, in_maps, core_ids=list(range(8)))
    outp = np.empty((4, SEQ, D), np.float32)
    for core in range(8):
        b, hf = core // 2, core % 2
        outp[b, hf * NOWN:(hf + 1) * NOWN] = res.results[core]["out"]
    return outp
```

```python
import contextlib
import numpy as np
import concourse.bass as bass
import concourse.mybir as mybir
from concourse.bass_utils import run_bass_kernel_spmd

F32 = mybir.dt.float32
BF16 = mybir.dt.bfloat16
ALU = mybir.AluOpType
AF = mybir.ActivationFunctionType
AX = mybir.AxisListType

D = 2048
SEQ = 4096
NOWN = 2048
CTXL = 256
NKEY = SEQ + CTXL
FF = 5632
NF = FF // 128
EPS = 1e-6
SCALE = 128 ** -0.5
DN_ALPHA = 2.0 ** 0.25
GRID_W = 64

DEBUG = False
PHASES = (0, 1, 2, 3, 4, 5)

COMPUTE = ("pe", "act", "dve", "pool")


class Op:
    __slots__ = ("eng", "fn", "deps", "signal", "sem", "val", "is_dma", "semkey", "ndma")

    def __init__(self, eng, fn, is_dma=False, semkey=None, ndma=0):
        self.eng = eng
        self.fn = fn
        self.deps = set()
        self.signal = False
        self.sem = None
        self.val = 0
        self.is_dma = is_dma
        self.semkey = semkey
        self.ndma = ndma


class Phase:
    def __init__(self, nc, name):
        self.nc = nc
        self.name = name
        self.eng_ops = {e: [] for e in ("pe", "act", "dve", "pool", "sp")}
        self.last_writer = {}
        self.readers = {}

    def _add(self, o, reads, writes):
        deps = o.deps
        lw = self.last_writer
        rd = self.readers
        for k in reads:
            w = lw.get(k)
            if w is not None:
                deps.add(w)
        for k in writes:
            w = lw.get(k)
            if w is not None:
                deps.add(w)
            r = rd.get(k)
            if r:
                deps.update(r)
        deps.discard(o)
        for k in reads:
            rd.setdefault(k, []).append(o)
        for k in writes:
            lw[k] = o
            rd[k] = []
        for d in deps:
            if d.eng == "pe" and o.eng == "pe" and not d.is_dma and not o.is_dma:
                continue
            d.signal = True
        self.eng_ops[o.eng].append(o)
        return o

    def op(self, eng, fn, reads=(), writes=()):
        return self._add(Op(eng, fn), reads, writes)

    def dma(self, eng, semkey, pairs, reads=(), writes=(), slow=False):
        pairs = list(pairs)

        def fn(e, pairs=pairs, slow=slow):
            if slow:
                return [e.dma_start(out=o_, in_=i_, allow_slow_non_contiguous=True) for (o_, i_) in pairs]
            return [e.dma_start(out=o_, in_=i_) for (o_, i_) in pairs]

        o = Op(eng, fn, is_dma=True, semkey=semkey, ndma=len(pairs))
        o.signal = True
        return self._add(o, reads, writes)

    semstack = None
    dsem = {}
    dma_counts = {}

    def emit(self):
        nc = self.nc
        dma_counts = Phase.dma_counts
        dsem = Phase.dsem
        slots = {}
        for e, ops in self.eng_ops.items():
            for o in ops:
                if o.is_dma:
                    o.semkey = ("slot", slots.setdefault(o.semkey, len(slots)))
        for e, ops in self.eng_ops.items():
            cnt = 0
            for o in ops:
                if o.is_dma:
                    if o.semkey not in dma_counts:
                        dma_counts[o.semkey] = 0
                        dsem[o.semkey] = Phase.semstack.enter_context(nc.semaphore(f"d{len(dsem)}"))
                    dma_counts[o.semkey] += 16 * o.ndma
                    o.val = dma_counts[o.semkey]
                elif o.signal:
                    cnt += 1
                    o.val = cnt
        with contextlib.ExitStack() as es:
            esem = {e: Phase.semstack.enter_context(nc.semaphore(f"{self.name}_{e}")) for e in COMPUTE}
            for e, ops in self.eng_ops.items():
                for o in ops:
                    o.sem = dsem[o.semkey] if o.is_dma else esem[e]
            block = es.enter_context(nc.Block())
            eng_ops = self.eng_ops

            def run(e, engname):
                waited = {}
                for o in eng_ops[engname]:
                    need = {}
                    for d in o.deps:
                        if (not d.is_dma) and d.eng == "pe" and engname == "pe" and not o.is_dma:
                            continue
                        s = d.sem
                        if d.val > need.get(s, 0):
                            need[s] = d.val
                    for s, v in need.items():
                        if waited.get(s, 0) < v:
                            e.wait_ge(s, v)
                            waited[s] = v
                    r = o.fn(e)
                    if o.is_dma:
                        for ins in r:
                            ins.then_inc(o.sem, 16)
                    elif o.signal:
                        r.then_inc(o.sem, 1)
                last = {}
                for o in eng_ops[engname]:
                    if o.is_dma:
                        last[o.sem] = o.val
                for s, v in last.items():
                    if waited.get(s, 0) < v:
                        e.wait_ge(s, v)

            @block.tensor
            def _(e):
                run(e, "pe")

            @block.scalar
            def _(e):
                run(e, "act")

            @block.vector
            def _(e):
                run(e, "dve")

            @block.gpsimd
            def _(e):
                run(e, "pool")

            @block.sync
            def _(e):
                run(e, "sp")


_UNIQ = [0]


def _uniq(name):
    _UNIQ[0] += 1
    return f"s{_UNIQ[0]}_{name}"


def build_program():
    nc = bass.Bass("TRN2", target_bir_lowering=False)

    def din(name, shape, dt=F32):
        return nc.dram_tensor(name, list(shape), dt, kind="ExternalInput")

    def dscr(name, shape, dt):
        return nc.dram_tensor(name, list(shape), dt, kind="ExternalOutput" if DEBUG else "Internal")

    x_h = din("x", [SEQ, D])
    ctx_h = din("ctx", [CTXL, D])
    condT_h = din("condT", [128, 16, 2])
    wada_h = din("w_ada", [D, 6 * D])
    bada_h = din("b_ada", [1, 6 * D])
    wq_h = din("wq", [D, 2048])
    wkv_h = din("wkv", [D, 1024])
    wout_h = din("w_out", [D, D])
    wg_h = din("wg", [NF, 128, 16, 128])
    wu_h = din("wu", [NF, 128, 16, 128])
    wd_h = din("wd", [4, 128, NF, 512])
    cos_h = din("cosT", [128, SEQ])
    sin_h = din("sinT", [128, SEQ])
    ident_h = din("ident", [128, 128])
    rotm_h = din("rotm", [128, 128])
    gcol_h = din("gcol", [128, 2])
    grow_h = din("grow", [1, 256])
    sink_h = din("sink", [1, 8])
    lnp_h = din("lnp", [4, D])
    masks_h = din("masks", [4, 128, 512])
    out_h = nc.dram_tensor("out", [NOWN, D], F32, kind="ExternalOutput")

    mrow_h = dscr("mrow", [2, 6 * D], F32)
    QT_h = dscr("QT", [16, 128, NOWN], BF16)
    KT_h = dscr("KT", [4, 128, NKEY], BF16)
    Vd_h = dscr("Vd", [NKEY, 512], BF16)
    OT_h = dscr("OT", [16, 128, NOWN], BF16)
    X1_h = dscr("X1", [NOWN, D], F32)
    X1T_h = dscr("X1T", [128, 16, NOWN], BF16)
    Z_h = dscr("Z", [NOWN, D], F32)

    x = x_h.ap()
    ctx = ctx_h.ap()
    out = out_h.ap()
    QT = QT_h.ap()
    KT = KT_h.ap()
    Vd = Vd_h.ap()
    OT = OT_h.ap()
    X1 = X1_h.ap()
    X1T = X1T_h.ap()
    Z = Z_h.ap()
    mrow = mrow_h.ap()

    def bc(handle, row_off, n, parts=128):
        return bass.AP(handle, row_off, [[0, parts], [1, n]])

    with contextlib.ExitStack() as top:
        Phase.semstack = top
        Phase.dsem = {}
        Phase.dma_counts = {}
        S = lambda name, shape, dt: top.enter_context(nc.sbuf_tensor(_uniq(name), list(shape), dt))
        ident = S("ident", [128, 128], F32)
        rotm = S("rotm", [128, 128], F32)
        onesd = S("onesd", [128, 128], F32)
        ones_f = S("ones_f", [128, 128], F32)
        ones_b = S("ones_b", [128, 128], BF16)
        modT = S("modT", [128, 128], F32)
        gcol = S("gcol", [128, 2], F32)
        gqs = S("gqs", [128, 1], F32)
        nrmmax = S("nrmmax", [1, 2], F32)
        negcA = S("negcA", [128, 1], F32)
        negcB = S("negcB", [128, 1], F32)
        sinkb = S("sinkb", [128, 8], F32)
        epst = S("epst", [128, 1], F32)
        condb2 = S("condb2", [128, 16, 1], BF16)
        rotb = S("rotb", [128, 128], BF16)
        onesdb = S("onesdb", [128, 128], BF16)

        if 0 in PHASES:
            with contextlib.ExitStack() as es:
                T = lambda name, shape, dt: es.enter_context(nc.sbuf_tensor(_uniq(name), list(shape), dt))
                condf = T("condf", [128, 16, 2], F32)
                condb = T("condb", [128, 16, 2], F32)
                wring = [T(f"wring{i}", [128, 16, 512], F32) for i in range(3)]
                mrow_sb = T("mrow_sb", [2, 2 * D], F32)
                bada_sb = T("bada_sb", [2, 2 * D], F32)
                grow = T("grow", [1, 256], F32)
                gtmp = T("gtmp", [1, 4], F32)
                ps = [es.enter_context(nc.psum_tensor(_uniq(f"p0ps{i}"), [128, 512], F32)) for i in range(2)]
                P = Phase(nc, "p0")
                P.dma("sp", "c0", [(ident[:], ident_h.ap()), (rotm[:], rotm_h.ap()), (gcol[:], gcol_h.ap()),
                                   (condf[:], condT_h.ap()), (grow[:], grow_h.ap())],
                      writes=["ident", "rotm", "gcol", "condf", "grow"])
                P.dma("sp", "c1", [(bada_sb[:], bc(bada_h, 0, 2 * D, 2))], writes=["bada"])
                P.op("dve", lambda e: e.memset(onesd[:], 1.0 / 128.0), writes=["onesd"])
                P.op("dve", lambda e: e.memset(onesdb[:], 1.0 / 128.0), writes=["onesdb"])
                P.op("dve", lambda e: e.tensor_copy(out=rotb[:], in_=rotm[:]), reads=["rotm"], writes=["rotb"])
                P.op("dve", lambda e: e.memset(ones_f[:], 1.0), writes=["ones_f"])
                P.op("dve", lambda e: e.memset(ones_b[:], 1.0), writes=["ones_b"])
                P.op("dve", lambda e: e.memset(epst[:], EPS), writes=["epst"])
                P.op("dve", lambda e: e.memset(nrmmax[:], 0.0), writes=["nrmmax"])
                P.op("dve", lambda e: e.tensor_scalar(out=gqs[:], in0=gcol[:, 0:1], scalar1=SCALE, scalar2=None, op0=ALU.mult),
                     reads=["gcol"], writes=["gqs"])
                P.op("dve", lambda e: e.tensor_reduce(out=gtmp[:, 0:1], in_=grow[:, 0:128], axis=AX.X, op=ALU.max, apply_absolute_value=True),
                     reads=["grow"], writes=["gtmp0"])
                P.op("dve", lambda e: e.tensor_reduce(out=gtmp[:, 1:2], in_=grow[:, 128:256], axis=AX.X, op=ALU.max, apply_absolute_value=True),
                     reads=["grow"], writes=["gtmp1"])
                P.op("dve", lambda e: e.tensor_tensor(out=gtmp[:, 2:3], in0=gtmp[:, 0:1], in1=gtmp[:, 1:2], op=ALU.mult),
                     reads=["gtmp0", "gtmp1"], writes=["gtmp2"])
                P.op("dve", lambda e: e.tensor_scalar(out=gtmp[:, 3:4], in0=gtmp[:, 2:3], scalar1=-128.0 * SCALE * 1.001, scalar2=None, op0=ALU.mult),
                     reads=["gtmp2"], writes=["gtmp3"])
                P.op("pe", lambda e: e.matmul(ps[1][:, 0:1], lhsT=ones_f[0:1, :], rhs=gtmp[:, 3:4], start=True, stop=True),
                     reads=["ones_f", "gtmp3"], writes=["ps1"])
                P.op("dve", lambda e: e.tensor_copy(out=negcB[:], in_=ps[1][:, 0:1]), reads=["ps1"], writes=["negcB"])
                P.op("act", lambda e: e.activation(out=condb[:], in_=condf[:], func=AF.Silu), reads=["condf"], writes=["condb"])
                P.op("act", lambda e: e.activation(out=condb2[:], in_=condf[:, :, 0:1], func=AF.Silu), reads=["condf"], writes=["condb2"])
                wv = wada_h.ap().rearrange("(k p) c -> p k c", p=128)
                NG = 8
                for g in range(NG):
                    r = g % 3
                    P.dma("sp" if g % 2 == 0 else "pool", ("wr", r, g % 2), [(wring[r][:], wv[:, :, g * 512:(g + 1) * 512])], writes=[("wr", r)])

                    def mm(e, r=r):
                        ins = None
                        for k in range(16):
                            ins = e.matmul(ps[0][0:2, :], lhsT=condb[:, k, :], rhs=wring[r][:, k, :], start=(k == 0), stop=(k == 15))
                        return ins
                    P.op("pe", mm, reads=["condb", ("wr", r)], writes=["ps0"])
                    P.op("dve", lambda e, g=g: e.tensor_tensor(out=mrow_sb[:, g * 512:(g + 1) * 512], in0=ps[0][0:2, :],
                                                              in1=bada_sb[:, g * 512:(g + 1) * 512], op=ALU.add),
                         reads=["ps0", "bada"], writes=[("mrow_sb", g)])
                P.dma("sp", "m0", [(mrow[:, 0:2 * D], mrow_sb[:])], reads=[("mrow_sb", g) for g in range(NG)], writes=["mrow"])
                P.dma("sp", "m1", [(modT[:, 0:32], mrow[0:1, 0:4096].rearrange("o (j p) -> p (o j)", p=128)),
                                   (modT[:, 96:128], mrow[1:2, 0:4096].rearrange("o (j p) -> p (o j)", p=128))],
                      reads=["mrow"], writes=["modT"], slow=True)
                for (a, b) in ((16, 32), (112, 128)):
                    P.op("dve", lambda e, a=a, b=b: e.tensor_scalar(out=modT[:, a:b], in0=modT[:, a:b], scalar1=1.0, scalar2=None, op0=ALU.add),
                         reads=["modT"], writes=["modT"])
                P.emit()

        if 1 in PHASES:
            with contextlib.ExitStack() as es:
                T = lambda name, shape, dt: es.enter_context(nc.sbuf_tensor(_uniq(name), list(shape), dt))
                wq_sb = T("wq_sb", [128, 16, 2048], BF16)
                wkv_sb = T("wkv_sb", [128, 16, 1024], BF16)
                xst = [T(f"xst{i}", [128, D], F32) for i in range(4)]
                xT = [T(f"xT{i}", [128, 16, 512], BF16) for i in range(2)]
                cs = [T(f"cs{i}", [128, 2, 512], F32) for i in range(2)]
                yb = [T(f"yb{i}", [128, 512], F32) for i in range(3)]
                sqf = [T(f"sqf{i}", [128, 512], BF16) for i in range(2)]
                ybb = [T(f"ybb{i}", [128, 512], BF16) for i in range(3)]
                sqb = [T(f"sqb{i}", [128, 512], BF16) for i in range(2)]
                t1 = [T(f"t1{i}", [128, 512], F32) for i in range(2)]
                t2 = [T(f"t2{i}", [128, 512], F32) for i in range(2)]
                rr = [T(f"rr{i}", [128, 512], F32) for i in range(2)]
                outb = [T(f"outb{i}", [128, 512], BF16) for i in range(4)]
                vout = [T(f"vout{i}", [128, 512], BF16) for i in range(2)]
                ntmp = T("ntmp", [1, 2], F32)
                tp = [es.enter_context(nc.psum_tensor(_uniq(f"tp{i}"), [128, 512], F32)) for i in range(2)]
                mmp = [es.enter_context(nc.psum_tensor(_uniq(f"mm{i}"), [128, 512], F32)) for i in range(3)]
                aux = [es.enter_context(nc.psum_tensor(_uniq(f"aux{i}"), [128, 512], F32)) for i in range(2)]
                nrp = es.enter_context(nc.psum_tensor(_uniq("nrp"), [128, 512], F32))
                P = Phase(nc, "p1")
                wkv_v = wkv_h.ap().rearrange("(k p) c -> p k c", p=128)
                wq_v = wq_h.ap().rearrange("(k p) c -> p k c", p=128)
                P.dma("pool", "wkv", [(wkv_sb[:, 4 * k:4 * k + 4, :], wkv_v[:, 4 * k:4 * k + 4, :]) for k in range(4)], writes=["wkv"])
                P.dma("pool", "wq", [(wq_sb[:, 4 * k:4 * k + 4, :], wq_v[:, 4 * k:4 * k + 4, :]) for k in range(4)], writes=["wq"])

                cnt = {"xT": 0, "tp": 0, "mm": 0, "aux": 0, "y": 0, "sq": 0, "t": 0, "rr": 0, "ob": 0, "vo": 0, "ev": 0, "cs": 0}

                def nxt(k, n):
                    v = cnt[k] % n
                    cnt[k] += 1
                    return v

                chunks = [("ctx", 0)] + [("tok", c) for c in (4, 5, 6, 7, 0, 1, 2, 3)]

                def load_x(ch):
                    kind, c = ch
                    if kind == "ctx":
                        for t in range(2):
                            P.dma("sp", ("xst", t), [(xst[t][:], ctx[t * 128:(t + 1) * 128, :])], writes=[("xst", t)])
                    else:
                        for t in range(4):
                            r0 = c * 512 + t * 128
                            P.dma("sp", ("xst", t), [(xst[t][:], x[r0:r0 + 128, :])], writes=[("xst", t)])

                load_x(chunks[0])
                for ci, ch in enumerate(chunks):
                    kind, c = ch
                    ntile = 2 if kind == "ctx" else 4
                    W = ntile * 128
                    own = (kind == "tok" and c < 4)
                    xi = nxt("xT", 2)
                    xTc = xT[xi]
                    moff = 96 if kind == "ctx" else 0
                    csb = None
                    if kind == "tok":
                        ci_ = nxt("cs", 2)
                        csb = cs[ci_]
                        P.dma("sp", ("cs", ci_), [(csb[:, 0, :], cos_h.ap()[:, c * 512:(c + 1) * 512]),
                                                  (csb[:, 1, :], sin_h.ap()[:, c * 512:(c + 1) * 512])], writes=[("cs", ci_)])
                    for k in range(16):
                        ti = nxt("tp", 2)

                        def trf(e, k=k, ti=ti, ntile=ntile):
                            ins = None
                            for t in range(ntile):
                                ins = e.transpose(out=tp[ti][:, t * 128:(t + 1) * 128], in_=xst[t][:, k * 128:(k + 1) * 128], identity=ident[:])
                            return ins
                        P.op("pe", trf, reads=[("xst", t) for t in range(ntile)] + ["ident"], writes=[("tp", ti)])
                        sc_ap = modT[:, moff + 16 + k: moff + 17 + k]
                        sh_ap = modT[:, moff + k: moff + k + 1]
                        if nxt("ev", 2) == 0:
                            P.op("act", lambda e, k=k, ti=ti, W=W, sc_ap=sc_ap, sh_ap=sh_ap, xTc=xTc:
                                 e.activation(out=xTc[:, k, 0:W], in_=tp[ti][:, 0:W], func=AF.Identity, bias=sh_ap, scale=sc_ap),
                                 reads=[("tp", ti), "modT"], writes=[("xT", xi, k)])
                        else:
                            P.op("dve", lambda e, k=k, ti=ti, W=W, sc_ap=sc_ap, sh_ap=sh_ap, xTc=xTc:
                                 e.tensor_scalar(out=xTc[:, k, 0:W], in0=tp[ti][:, 0:W], scalar1=sc_ap, scalar2=sh_ap, op0=ALU.mult, op1=ALU.add),
                                 reads=[("tp", ti), "modT"], writes=[("xT", xi, k)])
                    if ci + 1 < len(chunks):
                        load_x(chunks[ci + 1])
                    xkeys = [("xT", xi, k) for k in range(16)]

                    heads = []
                    for h in range(4):
                        heads.append(("k", h))
                    if own:
                        for h in range(16):
                            heads.append(("q", h))
                    items = []

                    ci__ = ci_ if kind == "tok" else None

                    def make_head(hk, h, W=W, xTc=xTc, csb=csb, ci_=ci__, c=c, kind=kind, xkeys=xkeys, ci=ci):
                        info = {}

                        def stage1():
                            mi = nxt("mm", 3)
                            if hk == "k":
                                wsb, col0, wkey = wkv_sb, h * 128, "wkv"
                                isB = h >= 2
                            else:
                                wsb, col0, wkey = wq_sb, h * 128, "wq"
                                isB = h >= 8

                            def mmf(e, wsb=wsb, col0=col0, mi=mi):
                                ins = None
                                for k in range(16):
                                    ins = e.matmul(mmp[mi][:, 0:W], lhsT=wsb[:, k, col0:col0 + 128], rhs=xTc[:, k, 0:W], start=(k == 0), stop=(k == 15))
                                return ins
                            P.op("pe", mmf, reads=xkeys + [wkey], writes=[("mm", mi)])
                            yi = nxt("y", 3)
                            si = nxt("sq", 2)
                            if isB:
                                gap = gqs[:, 0:1] if hk == "q" else gcol[:, 1:2]
                                P.op("act", lambda e, yi=yi, mi=mi, gap=gap: e.activation(out=yb[yi][:, 0:W], in_=mmp[mi][:, 0:W], func=AF.Identity, scale=gap),
                                     reads=[("mm", mi), "gqs", "gcol"], writes=[("y", yi)])
                                P.op("act", lambda e, si=si, mi=mi: e.activation(out=sqf[si][:, 0:W], in_=mmp[mi][:, 0:W], func=AF.Square),
                                     reads=[("mm", mi)], writes=[("sqf", si)])
                            else:
                                sc = SCALE if hk == "q" else 1.0
                                P.op("act", lambda e, yi=yi, mi=mi, sc=sc: e.activation(out=yb[yi][:, 0:W], in_=mmp[mi][:, 0:W], func=AF.Identity, scale=sc),
                                     reads=[("mm", mi)], writes=[("y", yi)])
                                P.op("act", lambda e, si=si, yi=yi: e.activation(out=sqb[si][:, 0:W], in_=yb[yi][:, 0:W], func=AF.Square),
                                     reads=[("y", yi)], writes=[("sqb", si)])
                            if kind == "tok":
                                P.op("act", lambda e, yi=yi: e.activation(out=ybb[yi][:, 0:W], in_=yb[yi][:, 0:W], func=AF.Identity),
                                     reads=[("y", yi)], writes=[("ybb", yi)])
                            info.update(mi=mi, yi=yi, si=si, isB=isB)

                        def stage2():
                            mi, yi, si, isB = info["mi"], info["yi"], info["si"], info["isB"]
                            oi = nxt("ob", 4)
                            ob = outb[oi]
                            ri = None
                            if isB:
                                a_ms = nxt("aux", 2)
                                P.op("pe", lambda e, a_ms=a_ms, si=si: e.matmul(aux[a_ms][:, 0:W], lhsT=onesdb[:], rhs=sqf[si][:, 0:W], start=True, stop=True),
                                     reads=[("sqf", si), "onesd"], writes=[("aux", a_ms)])
                                ri = nxt("rr", 2)
                                P.op("act", lambda e, ri=ri, a_ms=a_ms: e.activation(out=rr[ri][:, 0:W], in_=aux[a_ms][:, 0:W], func=AF.Sqrt, bias=epst[:, 0:1], scale=1.0),
                                     reads=[("aux", a_ms), "epst"], writes=[("rr", ri)])
                                P.op("dve", lambda e, ri=ri: e.reciprocal(out=rr[ri][:, 0:W], in_=rr[ri][:, 0:W]),
                                     reads=[("rr", ri)], writes=[("rr", ri)])
                            else:
                                P.op("pe", lambda e, si=si: e.matmul(nrp[0:1, 0:W], lhsT=ones_b[:, 0:1], rhs=sqb[si][:, 0:W], start=True, stop=True),
                                     reads=[("sqb", si), "ones_b"], writes=["nrp"])
                                col = 0 if hk == "q" else 1
                                P.op("dve", lambda e: e.tensor_reduce(out=ntmp[:, 0:1], in_=nrp[0:1, 0:W], axis=AX.X, op=ALU.max),
                                     reads=["nrp"], writes=["ntmp"])
                                P.op("dve", lambda e, col=col: e.tensor_tensor(out=nrmmax[:, col:col + 1], in0=nrmmax[:, col:col + 1], in1=ntmp[:, 0:1], op=ALU.max),
                                     reads=["ntmp", "nrmmax"], writes=["nrmmax"])
                            if kind == "tok":
                                a_r = nxt("aux", 2)
                                P.op("pe", lambda e, a_r=a_r, yi=yi: e.matmul(aux[a_r][:, 0:W], lhsT=rotb[:], rhs=ybb[yi][:, 0:W], start=True, stop=True),
                                     reads=[("ybb", yi), "rotm"], writes=[("aux", a_r)])
                                ti_ = nxt("t", 2)
                                P.op("pool", lambda e, ti_=ti_, yi=yi: e.tensor_tensor(out=t1[ti_][:], in0=yb[yi][:], in1=csb[:, 0, :], op=ALU.mult),
                                     reads=[("y", yi), ("cs", ci_)], writes=[("t1", ti_)])
                                P.op("dve", lambda e, ti_=ti_, a_r=a_r: e.tensor_tensor(out=t2[ti_][:], in0=aux[a_r][:], in1=csb[:, 1, :], op=ALU.mult),
                                     reads=[("aux", a_r), ("cs", ci_)], writes=[("t2", ti_)])
                                if isB:
                                    P.op("pool", lambda e, ti_=ti_: e.tensor_tensor(out=t1[ti_][:], in0=t1[ti_][:], in1=t2[ti_][:], op=ALU.add),
                                         reads=[("t1", ti_), ("t2", ti_)], writes=[("t1", ti_)])
                                    P.op("dve", lambda e, ti_=ti_, ri=ri, ob=ob: e.tensor_tensor(out=ob[:], in0=t1[ti_][:], in1=rr[ri][:], op=ALU.mult),
                                         reads=[("t1", ti_), ("rr", ri)], writes=[("ob", oi)])
                                else:
                                    P.op("dve", lambda e, ti_=ti_, ob=ob: e.tensor_tensor(out=ob[:], in0=t1[ti_][:], in1=t2[ti_][:], op=ALU.add),
                                         reads=[("t1", ti_), ("t2", ti_)], writes=[("ob", oi)])
                            else:
                                if isB:
                                    P.op("dve", lambda e, yi=yi, ri=ri, ob=ob: e.tensor_tensor(out=ob[:, 0:W], in0=yb[yi][:, 0:W], in1=rr[ri][:, 0:W], op=ALU.mult),
                                         reads=[("y", yi), ("rr", ri)], writes=[("ob", oi)])
                                else:
                                    P.op("dve", lambda e, yi=yi, ob=ob: e.tensor_copy(out=ob[:, 0:W], in_=yb[yi][:, 0:W]),
                                         reads=[("y", yi)], writes=[("ob", oi)])
                            if hk == "k":
                                c0 = SEQ if kind == "ctx" else c * 512
                                P.dma("sp", ("ob", oi), [(KT[h, :, c0:c0 + W], ob[:, 0:W])], reads=[("ob", oi)], writes=[("KT", h, ci)])
                            else:
                                P.dma("sp", ("ob", oi), [(QT[h, :, c * 512:(c + 1) * 512], ob[:])], reads=[("ob", oi)], writes=[("QT", h, c)])
                        return stage1, stage2

                    def make_v(t, W=W, xTc=xTc, c=c, kind=kind, xkeys=xkeys):
                        def stage1():
                            mi = nxt("mm", 3)

                            def mmv(e, mi=mi):
                                ins = None
                                for k in range(16):
                                    ins = e.matmul(mmp[mi][:], lhsT=xTc[:, k, t * 128:(t + 1) * 128], rhs=wkv_sb[:, k, 512:1024], start=(k == 0), stop=(k == 15))
                                return ins
                            P.op("pe", mmv, reads=xkeys + ["wkv"], writes=[("mm", mi)])
                            vi = nxt("vo", 2)
                            P.op("act", lambda e, vi=vi, mi=mi: e.activation(out=vout[vi][:], in_=mmp[mi][:], func=AF.Identity),
                                 reads=[("mm", mi)], writes=[("vo", vi)])
                            r0 = (SEQ if kind == "ctx" else c * 512) + t * 128
                            P.dma("sp", ("vo", vi), [(Vd[r0:r0 + 128, :], vout[vi][:])], reads=[("vo", vi)], writes=[("Vd", r0)])
                        return stage1, (lambda: None)

                    for (hk, h) in heads:
                        items.append(make_head(hk, h))
                    vt = [make_v(t) for t in range(ntile)]
                    merged = []
                    step_v = max(1, len(items) // max(1, len(vt)))
                    for i, it in enumerate(items):
                        merged.append(it)
                        if (i + 1) % step_v == 0 and vt:
                            merged.append(vt.pop(0))
                    merged.extend(vt)
                    merged[0][0]()
                    for i in range(len(merged)):
                        if i + 1 < len(merged):
                            merged[i + 1][0]()
                        merged[i][1]()
                P.emit()

        if 2 in PHASES:
            with contextlib.ExitStack() as es:
                T = lambda name, shape, dt: es.enter_context(nc.sbuf_tensor(_uniq(name), list(shape), dt))
                KT_sb = T("KT_sb", [128, 4, NKEY], BF16)
                V_sb = T("V_sb", [128, 34, 512], BF16)
                QT_sb = [T(f"QT_sb{i}", [128, 16, 512], BF16) for i in range(2)]
                OT_sb = [T(f"OT_sb{i}", [128, 16, 512], BF16) for i in range(2)]
                NPT = 6
                pT = [T(f"pT{i}", [128, 512], BF16) for i in range(NPT)]
                accD = [T(f"accD{i}", [128, 512], F32) for i in range(2)]
                accP = [T(f"accP{i}", [128, 512], F32) for i in range(2)]
                wr2 = [T(f"wr2{i}", [128, 16, 512], BF16) for i in range(2)]
                bada2 = [T(f"bada2{i}", [1, 512], F32) for i in range(2)]
                mr2 = [T(f"mr2{i}", [1, 512], F32) for i in range(2)]
                msk = T("msk", [128, 4, 512], F32)
                rl = [T(f"rl{i}", [128, 512], F32) for i in range(2)]
                sinkrow = T("sinkrow", [128, 2, 512], F32)
                sm = T("sm", [1, 16], F32)
                Sps = [es.enter_context(nc.psum_tensor(_uniq(f"Sps{i}"), [128, 512], F32)) for i in range(3)]
                Ops = [es.enter_context(nc.psum_tensor(_uniq(f"Ops{i}"), [128, 512], F32)) for i in range(2)]
                Lps = [es.enter_context(nc.psum_tensor(_uniq(f"Lps{i}"), [128, 512], F32)) for i in range(2)]
                mps = es.enter_context(nc.psum_tensor(_uniq("mps"), [128, 512], F32))
                P = Phase(nc, "p2")
                for h in range(4):
                    P.dma("sp", ("kt", h), [(KT_sb[:, h, :], KT[h])], writes=[("KT", h)])
                Vv = Vd.rearrange("(b p) c -> p b c", p=128)
                for q4 in range(4):
                    b0, b1 = q4 * 9, min(34, q4 * 9 + 9)
                    P.dma("sp", ("v", q4), [(V_sb[:, b0:b1, :], Vv[:, b0:b1, :])], writes=[("V", b) for b in range(b0, b1)])
                P.dma("sp", "msk", [(msk[:, i, :], masks_h.ap()[i]) for i in range(4)], writes=["msk"])
                sink_sb = T("sink_sb", [1, 8], F32)
                P.dma("sp", "sink", [(sink_sb[:], sink_h.ap())], writes=["sink_sb"])
                P.op("dve", lambda e: e.tensor_tensor(out=sm[:, 0:1], in0=nrmmax[:, 0:1], in1=nrmmax[:, 1:2], op=ALU.mult), writes=["sm0"])
                P.op("act", lambda e: e.activation(out=sm[:, 1:2], in_=sm[:, 0:1], func=AF.Sqrt), reads=["sm0"], writes=["sm1"])
                P.op("dve", lambda e: e.tensor_scalar(out=sm[:, 2:3], in0=sm[:, 1:2], scalar1=-1.02, scalar2=None, op0=ALU.mult), reads=["sm1"], writes=["sm2"])
                P.op("act", lambda e: e.activation(out=sm[:, 8:16], in_=sink_sb[:], func=AF.Exp, bias=sm[:, 2:3], scale=1.0),
                     reads=["sm2", "sink_sb"], writes=["sm8"])
                P.op("pe", lambda e: e.matmul(mps[:, 0:1], lhsT=ones_f[0:1, :], rhs=sm[:, 2:3], start=True, stop=True), reads=["sm2"], writes=["mps"])
                P.op("dve", lambda e: e.tensor_copy(out=negcA[:], in_=mps[:, 0:1]), reads=["mps"], writes=["negcA"])
                P.op("pe", lambda e: e.matmul(mps[:, 0:8], lhsT=ones_f[0:1, :], rhs=sm[:, 8:16], start=True, stop=True), reads=["sm8", "negcA"], writes=["mps"])
                P.op("dve", lambda e: e.tensor_copy(out=sinkb[:], in_=mps[:, 0:8]), reads=["mps"], writes=["sinkb"])
                for g in range(2):
                    for h in range(4):
                        P.op("dve", lambda e, g=g, h=h: e.tensor_scalar(out=sinkrow[:, g, h * 128:(h + 1) * 128], in0=ones_f[:], scalar1=sinkb[:, 4 * g + h:4 * g + h + 1],
                                                                         scalar2=None, op0=ALU.mult),
                             reads=["sinkb"], writes=[("sinkrow", g)])

                steps = []
                for qc in range(4):
                    cb = qc % 2
                    bgroups = []
                    agroups = []
                    for g in range(2):
                        for h in range(4):
                            head = 8 + 4 * g + h
                            bgroups.append([])
                            for sb in range(34):
                                bgroups[-1].append(dict(
                                    qc=qc, cb=cb, kind="B", first=(sb == 0), last=(sb == 33),
                                    lhsT=KT_sb[:, 2 + g, sb * 128:(sb + 1) * 128], kkey=("KT", 2 + g),
                                    rhs=QT_sb[cb][:, head, :],
                                    vs=V_sb[:, sb, 256 + g * 128:256 + (g + 1) * 128], vkey=("V", sb),
                                    bias=negcB, mask=None, head=head, g=g, i=None, sb=sb))
                    for g in range(2):
                        for i in range(4):
                            agroups.append([])
                            qb = qc * 4 + i
                            blocks = []
                            if qb == 0:
                                blocks.append((31, 2))
                            else:
                                blocks.append((qb - 1, 0))
                            blocks.append((qb, None))
                            if qb == 15:
                                blocks.append((16, 3))
                            else:
                                blocks.append((qb + 1, 1))
                            blocks.append((32, None))
                            blocks.append((33, None))
                            for bi, (kb, mk) in enumerate(blocks):
                                agroups[-1].append(dict(
                                    qc=qc, cb=cb, kind="A", first=(bi == 0), last=(bi == len(blocks) - 1),
                                    lhsT=KT_sb[:, g, kb * 128:(kb + 1) * 128], kkey=("KT", g),
                                    rhs=QT_sb[cb][:, 4 * g:4 * g + 4, i * 128:(i + 1) * 128],
                                    vs=V_sb[:, kb, g * 128:(g + 1) * 128], vkey=("V", kb),
                                    bias=negcA, mask=mk, head=None, g=g, i=i))
                    for bg in bgroups:
                        steps.extend(bg)
                    for ag in agroups:
                        steps.extend(ag)
                nst = len(steps)
                acc_idx = 0
                for st in steps:
                    if st["first"]:
                        acc_idx += 1
                    st["acc"] = acc_idx % 2

                loaded_qc = set()

                def load_q(qc):
                    if qc in loaded_qc or qc >= 4:
                        return
                    loaded_qc.add(qc)
                    cb = qc % 2
                    P.dma("sp", ("qt", cb), [(QT_sb[cb][:, h, :], QT[h, :, qc * 512:(qc + 1) * 512]) for h in range(16)], writes=[("QT", cb)])

                def rec_qk(j):
                    st = steps[j]
                    si = j % 3
                    rhs = st["rhs"]
                    P.op("pe", lambda e, st=st, si=si: e.matmul(Sps[si][:], lhsT=st["lhsT"], rhs=st["rhs"], start=True, stop=True),
                         reads=[st["kkey"], ("QT", st["cb"])], writes=[("S", si)])
                    pi = j % NPT
                    P.op("act", lambda e, st=st, si=si, pi=pi: e.activation(out=pT[pi][:], in_=Sps[si][:], func=AF.Exp, bias=st["bias"][:, 0:1], scale=1.0),
                         reads=[("S", si), "negcA", "negcB"], writes=[("pT", pi)])
                    if st["mask"] is not None:
                        mk = st["mask"]
                        P.op("dve", lambda e, pi=pi, mk=mk: e.tensor_tensor(out=pT[pi][:], in0=pT[pi][:], in1=msk[:, mk, :], op=ALU.mult),
                             reads=[("pT", pi), "msk"], writes=[("pT", pi)])

                def rec_pv(j):
                    st = steps[j]
                    pi = j % NPT
                    a = st["acc"]
                    if st["kind"] == "A" or True:
                        def f(e, st=st, pi=pi, a=a):
                            e.matmul(Ops[a][:], lhsT=st["vs"], rhs=pT[pi][:], start=st["first"], stop=st["last"])
                            return e.matmul(Lps[a][:], lhsT=ones_b[:], rhs=pT[pi][:], start=st["first"], stop=st["last"])
                        P.op("pe", f, reads=[("pT", pi), st["vkey"], "ones_b"], writes=[("O", a), ("L", a)])
                    else:
                        P.op("pe", lambda e, st=st, pi=pi, a=a: e.matmul(Ops[a][:], lhsT=st["vs"], rhs=pT[pi][:], start=st["first"], stop=st["last"]),
                             reads=[("pT", pi), st["vkey"]], writes=[("O", a)])
                        sb = st["sb"]
                        if sb % 3 == 2:
                            eng, acc, akey = "pool", accP[a], ("accP", a)
                            first = (sb == 2)
                        else:
                            eng, acc, akey = "dve", accD[a], ("accD", a)
                            first = (sb == 0)
                        if first:
                            P.op(eng, lambda e, acc=acc, pi=pi: e.tensor_copy(out=acc[:], in_=pT[pi][:]), reads=[("pT", pi)], writes=[akey])
                        else:
                            P.op(eng, lambda e, acc=acc, pi=pi: e.tensor_tensor(out=acc[:], in0=acc[:], in1=pT[pi][:], op=ALU.add), reads=[("pT", pi), akey], writes=[akey])
                        if st["last"]:
                            P.op("dve", lambda e, a=a: e.tensor_tensor(out=accD[a][:], in0=accD[a][:], in1=accP[a][:], op=ALU.add),
                                 reads=[("accD", a), ("accP", a)], writes=[("accD", a)])
                            P.op("pe", lambda e, a=a: e.matmul(Lps[a][:], lhsT=ones_f[:], rhs=accD[a][:], start=True, stop=True),
                                 reads=[("accD", a)], writes=[("L", a)])
                    if st["last"]:
                        ri = a
                        cb = st["cb"]
                        if st["kind"] == "B":
                            P.op("dve", lambda e, a=a, ri=ri: e.reciprocal(out=rl[ri][:], in_=Lps[a][:]), reads=[("L", a)], writes=[("rl", ri)])
                            P.op("dve", lambda e, a=a, ri=ri, st=st, cb=cb: e.tensor_tensor(out=OT_sb[cb][:, st["head"], :], in0=Ops[a][:], in1=rl[ri][:], op=ALU.mult),
                                 reads=[("O", a), ("rl", ri)], writes=[("OT", cb, st["head"])])
                        else:
                            g, i = st["g"], st["i"]
                            P.op("dve", lambda e, a=a, ri=ri, g=g: e.tensor_tensor(out=rl[ri][:], in0=Lps[a][:], in1=sinkrow[:, g, :], op=ALU.add),
                                 reads=[("L", a), ("sinkrow", g)], writes=[("rl", ri)])
                            P.op("dve", lambda e, ri=ri: e.reciprocal(out=rl[ri][:], in_=rl[ri][:]), reads=[("rl", ri)], writes=[("rl", ri)])
                            okeys = [("OT", cb, 4 * g + h) for h in range(4)]
                            P.op("dve", lambda e, a=a, ri=ri, g=g, i=i, cb=cb: e.tensor_tensor(
                                out=OT_sb[cb][:, 4 * g:4 * g + 4, i * 128:(i + 1) * 128],
                                in0=Ops[a][:].rearrange("p (h q) -> p h q", h=4),
                                in1=rl[ri][:].rearrange("p (h q) -> p h q", h=4), op=ALU.mult),
                                 reads=[("O", a), ("rl", ri)] + okeys, writes=okeys)
                        if j + 1 == nst or steps[j + 1]["qc"] != st["qc"]:
                            qc = st["qc"]
                            P.dma("pool", ("ot", cb), [(OT[h, :, qc * 512:(qc + 1) * 512], OT_sb[cb][:, h, :]) for h in range(16)],
                                  reads=[("OT", cb, h) for h in range(16)], writes=[("OTd", qc)])

                wv2 = wada_h.ap().rearrange("(k p) c -> p k c", p=128)
                NG2 = 16

                def ada_load(g):
                    if g >= NG2:
                        return
                    r = g % 2
                    c0 = 4096 + g * 512
                    P.dma("pool", ("wr2", r), [(wr2[r][:], wv2[:, :, c0:c0 + 512])], writes=[("wr2", r)])
                    P.dma("sp", ("bada2", r), [(bada2[r][:], bada_h.ap()[0:1, c0:c0 + 512])], writes=[("bada2", r)])

                def ada_group(g):
                    r = g % 2
                    c0 = 4096 + g * 512

                    def mm(e, r=r):
                        ins = None
                        for k in range(16):
                            ins = e.matmul(mps[0:1, :], lhsT=condb2[:, k, 0:1], rhs=wr2[r][:, k, :], start=(k == 0), stop=(k == 15))
                        return ins
                    P.op("pe", mm, reads=[("wr2", r)], writes=["mps"])
                    P.op("dve", lambda e, r=r: e.tensor_tensor(out=mr2[r][:], in0=mps[0:1, :], in1=bada2[r][:], op=ALU.add),
                         reads=["mps", ("bada2", r)], writes=[("mr2", r)])
                    P.dma("sp", ("mr2", r), [(mrow[0:1, c0:c0 + 512], mr2[r][:])], reads=[("mr2", r)], writes=[("mrow2", g)])
                    ada_load(g + 2)

                ada_load(0)
                ada_load(1)
                ada_every = nst // (NG2 + 1)
                load_q(0)
                load_q(1)
                LOOK = 2
                for j in range(min(LOOK, nst)):
                    rec_qk(j)
                for j in range(nst):
                    if j + LOOK < nst:
                        nq = steps[j + LOOK]["qc"]
                        rec_qk(j + LOOK)
                    rec_pv(j)
                    if (j + 1) % ada_every == 0 and (j + 1) // ada_every <= NG2:
                        ada_group((j + 1) // ada_every - 1)
                    if steps[j]["last"] and (j + 1 < nst) and steps[j + 1]["qc"] != steps[j]["qc"]:
                        load_q(steps[j]["qc"] + 2)
                P.emit()

        if 3 in PHASES:
            with contextlib.ExitStack() as es:
                T = lambda name, shape, dt: es.enter_context(nc.sbuf_tensor(_uniq(name), list(shape), dt))
                wout_sb = T("wout_sb", [128, 16, D], BF16)
                OTc = [T(f"OTc{i}", [128, 16, 512], BF16) for i in range(2)]
                xt = [T(f"xt{i}", [128, D], F32) for i in range(2)]
                zt = [T(f"zt{i}", [128, D], F32) for i in range(3)]
                g1b = T("g1b", [128, D], F32)
                lg = T("lg", [128, D], F32)
                lb = T("lb", [128, D], F32)
                x1Tc = [T(f"x1Tc{i}", [128, 16, 512], BF16) for i in range(2)]
                st6 = [T(f"st6{i}", [128, 4, 6], F32) for i in range(2)]
                mv = [T(f"mv{i}", [128, 4], F32) for i in range(2)]
                aps = [es.enter_context(nc.psum_tensor(_uniq(f"aps{i}"), [128, 512], F32)) for i in range(6)]
                tps = [es.enter_context(nc.psum_tensor(_uniq(f"tps{i}"), [128, 512], F32)) for i in range(2)]
                P = Phase(nc, "p3")
                wo_v = wout_h.ap().rearrange("(k p) c -> p k c", p=128)
                P.dma("pool", "wo", [(wout_sb[:, 4 * k:4 * k + 4, :], wo_v[:, 4 * k:4 * k + 4, :]) for k in range(4)], writes=["wo"])
                P.dma("sp", "bc", [(g1b[:], bc(mrow_h, 2 * D, D)), (lg[:], bc(lnp_h, 0, D)), (lb[:], bc(lnp_h, D, D))], writes=["g1b", "lg", "lb"])
                P.dma("sp", "m2", [(modT[:, 32:96], mrow[0:1, 4096:12288].rearrange("o (j p) -> p (o j)", p=128))], writes=["modT"], slow=True)
                P.op("dve", lambda e: e.tensor_scalar(out=modT[:, 64:80], in0=modT[:, 64:80], scalar1=1.0, scalar2=None, op0=ALU.add), reads=["modT"], writes=["modT"])
                _ln_phase(nc, P, 16, lambda tt: x[tt * 128:(tt + 1) * 128, :], OT, OTc, wout_sb, "wo", xt, zt, g1b, lg, lb, st6, mv, aps, tps,
                          X1, X1T, x1Tc, modT, ident, epst)
                P.emit()

        if 4 in PHASES:
            with contextlib.ExitStack() as es4:
                hT = es4.enter_context(nc.sbuf_tensor(_uniq("hT"), [128, NF, 1024], BF16))
                for half in range(2):
                    t0 = half * 1024
                    with contextlib.ExitStack() as es:
                        T = lambda name, shape, dt: es.enter_context(nc.sbuf_tensor(_uniq(name), list(shape), dt))
                        x1T_sb = T("x1T_sb", [128, 16, 1024], BF16)
                        wgr = [T(f"wgr{i}", [128, 16, 128], BF16) for i in range(4)]
                        wur = [T(f"wur{i}", [128, 16, 128], BF16) for i in range(4)]
                        sg = [T(f"sg{i}", [128, 512], F32) for i in range(2)]
                        Gp = [es.enter_context(nc.psum_tensor(_uniq(f"Gp{i}"), [128, 512], F32)) for i in range(4)]
                        Up = [es.enter_context(nc.psum_tensor(_uniq(f"Up{i}"), [128, 512], F32)) for i in range(4)]
                        P = Phase(nc, f"p4a{half}")
                        xs1 = [T(f"xs1{i}", [128, D], F32) for i in range(2)]
                        evn = 0
                        tpn = 0
                        for tl in range(8):
                            xi = tl % 2
                            P.dma("sp", ("xs1", xi), [(xs1[xi][:], X1[t0 + tl * 128:t0 + (tl + 1) * 128, :])], writes=[("xs1", xi)])
                            for k in range(0, 16, 4):
                                ti = tpn % 4
                                tpn += 1

                                def trf(e, k=k, ti=ti, xi=xi):
                                    ins = None
                                    for kk in range(4):
                                        ins = e.transpose(out=Gp[ti][:, kk * 128:(kk + 1) * 128], in_=xs1[xi][:, (k + kk) * 128:(k + kk + 1) * 128], identity=ident[:])
                                    return ins
                                P.op("pe", trf, reads=[("xs1", xi)], writes=[("G", ti)])
                                for kk in range(4):
                                    kq = k + kk
                                    sc_ap = modT[:, 64 + kq:65 + kq]
                                    sh_ap = modT[:, 48 + kq:49 + kq]
                                    dst = x1T_sb[:, kq, tl * 128:(tl + 1) * 128]
                                    if tpn % 2 == 0:
                                        P.op("act", lambda e, ti=ti, kk=kk, sc_ap=sc_ap, sh_ap=sh_ap, dst=dst: e.activation(out=dst, in_=Gp[ti][:, kk * 128:(kk + 1) * 128], func=AF.Identity, bias=sh_ap, scale=sc_ap),
                                             reads=[("G", ti)], writes=[("x1T", tl, kq)])
                                    else:
                                        P.op("dve", lambda e, ti=ti, kk=kk, sc_ap=sc_ap, sh_ap=sh_ap, dst=dst: e.tensor_scalar(out=dst, in0=Gp[ti][:, kk * 128:(kk + 1) * 128], scalar1=sc_ap, scalar2=sh_ap, op0=ALU.mult, op1=ALU.add),
                                             reads=[("G", ti)], writes=[("x1T", tl, kq)])
                                    evn += 1
                        x1keys = {tc_: [("x1T", tl, kq) for tl in range(tc_ * 4, tc_ * 4 + 4) for kq in range(16)] for tc_ in range(2)}

                        def loadw(f):
                            r = f % 4
                            P.dma("pool", ("wg", r), [(wgr[r][:], wg_h.ap()[f]), (wur[r][:], wu_h.ap()[f])], writes=[("wg", r)])
                        for f in range(3):
                            loadw(f)
                        k_ = 0
                        for f in range(NF):
                            r = f % 4
                            for tc_ in range(2):
                                pi = (f * 2 + tc_) % 4

                                def mg(e, r=r, tc_=tc_, pi=pi):
                                    ins = None
                                    for k in range(16):
                                        ins = e.matmul(Gp[pi][:], lhsT=wgr[r][:, k, :], rhs=x1T_sb[:, k, tc_ * 512:(tc_ + 1) * 512], start=(k == 0), stop=(k == 15))
                                    return ins

                                def mu(e, r=r, tc_=tc_, pi=pi):
                                    ins = None
                                    for k in range(16):
                                        ins = e.matmul(Up[pi][:], lhsT=wur[r][:, k, :], rhs=x1T_sb[:, k, tc_ * 512:(tc_ + 1) * 512], start=(k == 0), stop=(k == 15))
                                    return ins
                                P.op("pe", mg, reads=[("wg", r)] + x1keys[tc_], writes=[("G", pi)])
                                P.op("pe", mu, reads=[("wg", r)] + x1keys[tc_], writes=[("U", pi)])
                                si = k_ % 2
                                k_ += 1
                                P.op("act", lambda e, si=si, pi=pi: e.activation(out=sg[si][:], in_=Gp[pi][:], func=AF.Silu), reads=[("G", pi)], writes=[("sg", si)])
                                P.op("dve", lambda e, si=si, pi=pi, f=f, tc_=tc_: e.tensor_tensor(out=hT[:, f, tc_ * 512:(tc_ + 1) * 512], in0=Up[pi][:], in1=sg[si][:], op=ALU.mult),
                                     reads=[("U", pi), ("sg", si)], writes=[("hT", f, tc_)])
                            if f + 3 < NF:
                                loadw(f + 3)
                        P.emit()
                    with contextlib.ExitStack() as es:
                        T = lambda name, shape, dt: es.enter_context(nc.sbuf_tensor(_uniq(name), list(shape), dt))
                        wdr = [T(f"wdr{i}", [128, NF, 512], BF16) for i in range(2)]
                        x1p = [T(f"x1p{i}", [128, 512], F32) for i in range(2)]
                        zp = [T(f"zp{i}", [128, 512], F32) for i in range(2)]
                        g2b = T("g2b", [128, D], F32)
                        Yp = [es.enter_context(nc.psum_tensor(_uniq(f"Yp{i}"), [128, 512], F32)) for i in range(4)]
                        P = Phase(nc, f"p4b{half}")
                        P.dma("sp", "g2b", [(g2b[:], bc(mrow_h, 5 * D, D))], writes=["g2b"])

                        def loadwd(dg):
                            r = dg % 2
                            for q4 in range(4):
                                P.dma("pool", ("wd", r, q4), [(wdr[r][:, q4 * 11:(q4 + 1) * 11, :], wd_h.ap()[dg][:, q4 * 11:(q4 + 1) * 11, :])], writes=[("wd", r, q4)])
                        loadwd(0)
                        loadwd(1)
                        n_ = 0
                        for dg in range(4):
                            r = dg % 2
                            for tl in range(8):
                                yi = n_ % 4
                                xi = n_ % 2
                                n_ += 1
                                row0 = t0 + tl * 128
                                P.dma("sp", ("x1p", xi), [(x1p[xi][:], X1[row0:row0 + 128, dg * 512:(dg + 1) * 512])], writes=[("x1p", xi)])

                                for q4 in range(4):
                                    def md(e, r=r, tl=tl, yi=yi, q4=q4):
                                        ins = None
                                        for fk in range(q4 * 11, (q4 + 1) * 11):
                                            ins = e.matmul(Yp[yi][:], lhsT=hT[:, fk, tl * 128:(tl + 1) * 128], rhs=wdr[r][:, fk, :], start=(fk == 0), stop=(fk == NF - 1))
                                        return ins
                                    P.op("pe", md, reads=[("wd", r, q4)], writes=[("Y", yi)])
                                P.op("dve", lambda e, xi=xi, yi=yi, dg=dg: e.tensor_tensor(out=zp[xi][:], in0=Yp[yi][:], in1=g2b[:, dg * 512:(dg + 1) * 512], op=ALU.mult),
                                     reads=[("Y", yi), "g2b"], writes=[("zp", xi)])
                                P.op("dve", lambda e, xi=xi: e.scalar_tensor_tensor(out=zp[xi][:], in0=x1p[xi][:], scalar=DN_ALPHA, in1=zp[xi][:], op0=ALU.mult, op1=ALU.add),
                                     reads=[("zp", xi), ("x1p", xi)], writes=[("zp", xi)])
                                P.dma("sp", ("zp", xi), [(Z[row0:row0 + 128, dg * 512:(dg + 1) * 512], zp[xi][:])], reads=[("zp", xi)], writes=[("Z", n_)])
                            if dg + 2 < 4:
                                loadwd(dg + 2)
                        P.emit()

        if 5 in PHASES:
            with contextlib.ExitStack() as es:
                T = lambda name, shape, dt: es.enter_context(nc.sbuf_tensor(_uniq(name), list(shape), dt))
                zt = [T(f"z5{i}", [128, D], F32) for i in range(4)]
                lg = T("lg5", [128, D], F32)
                lb = T("lb5", [128, D], F32)
                st6 = [T(f"st65{i}", [128, 4, 6], F32) for i in range(2)]
                mv = [T(f"mv5{i}", [128, 4], F32) for i in range(2)]
                P = Phase(nc, "p5")
                P.dma("sp", "bc", [(lg[:], bc(lnp_h, 2 * D, D)), (lb[:], bc(lnp_h, 3 * D, D))], writes=["lg", "lb"])

                def ld5(tt):
                    if tt < 16:
                        P.dma("sp", ("z", tt % 4), [(zt[tt % 4][:], Z[tt * 128:(tt + 1) * 128, :])], writes=[("z", tt % 4)])

                def st5(tt):
                    zi = tt % 4
                    _ln_bias(P, zt[zi], [("z", zi)], lb)
                    P.dma("sp", ("z", zi), [(out[tt * 128:(tt + 1) * 128, :], zt[zi][:])], reads=[("z", zi)], writes=[("out", tt)])
                for tt in range(3):
                    ld5(tt)
                for tt in range(16):
                    zi = tt % 4
                    _layer_norm(P, zt[zi], [("z", zi)], st6[tt % 2], mv[tt % 2], tt % 2, lg, lb, epst, defer_bias=True)
                    if tt >= 1:
                        st5(tt - 1)
                    ld5(tt + 3)
                st5(15)
                P.emit()
    return nc


def _layer_norm(P, z, zkeys, st6, mv, si, lg, lb, epst, defer_bias=False):
    zkeys = list(zkeys)
    def stats(e):
        ins = None
        for q in range(4):
            ins = e.bn_stats(out=st6[:, q, :], in_=z[:, q * 512:(q + 1) * 512])
        return ins
    P.op("dve", stats, reads=zkeys, writes=[("st6", si)])
    P.op("dve", lambda e: e.bn_aggr(out=mv[:, 0:2], in_=st6[:].rearrange("p a b -> p (a b)")), reads=[("st6", si)], writes=[("mv", si)])
    P.op("act", lambda e: e.activation(out=mv[:, 2:3], in_=mv[:, 1:2], func=AF.Sqrt, bias=epst[:, 0:1], scale=1.0), reads=[("mv", si)], writes=[("mv2", si)])
    P.op("dve", lambda e: e.reciprocal(out=mv[:, 2:3], in_=mv[:, 2:3]), reads=[("mv2", si)], writes=[("mv2", si)])
    P.op("dve", lambda e: e.scalar_tensor_tensor(out=mv[:, 3:4], in0=mv[:, 0:1], scalar=-1.0, in1=mv[:, 2:3], op0=ALU.mult, op1=ALU.mult),
         reads=[("mv", si), ("mv2", si)], writes=[("mv3", si)])
    P.op("act", lambda e: e.activation(out=z[:], in_=z[:], func=AF.Identity, bias=mv[:, 3:4], scale=mv[:, 2:3]),
         reads=zkeys + [("mv2", si), ("mv3", si)], writes=zkeys)
    P.op("pool", lambda e: e.tensor_tensor(out=z[:], in0=z[:], in1=lg[:], op=ALU.mult), reads=zkeys + ["lg"], writes=zkeys)
    if not defer_bias:
        _ln_bias(P, z, zkeys, lb)


def _ln_bias(P, z, zkeys, lb):
    zkeys = list(zkeys)
    P.op("dve", lambda e: e.tensor_tensor(out=z[:], in0=z[:], in1=lb[:], op=ALU.add), reads=zkeys + ["lb"], writes=zkeys)


def _ln_phase(nc, P, ntiles, xsrc, OT, OTc, wout_sb, wkey, xt, zt, g1b, lg, lb, st6, mv, aps, tps, X1, X1T, x1Tc, modT, ident, epst):
    nb = [0]

    def stage_a(tt):
        qc, tl = tt // 4, tt % 4
        cb = qc % 2
        if tt == 0:
            P.dma("sp", ("otc", 0), [(OTc[0][:, h, :], OT[h, :, 0:512]) for h in range(16)], writes=[("OTc", 0)])
        if tl == 1 and qc + 1 < ntiles // 4:
            nq = qc + 1
            P.dma("sp", ("otc", nq % 2), [(OTc[nq % 2][:, h, :], OT[h, :, nq * 512:(nq + 1) * 512]) for h in range(16)], writes=[("OTc", nq % 2)])
        xi = tt % 2
        zi = tt % 3
        if tt == 0:
            P.dma("sp", ("xt", 0), [(xt[0][:], xsrc(0))], writes=[("xt", 0)])
        if tt + 1 < ntiles:
            P.dma("sp", ("xt", (tt + 1) % 2), [(xt[(tt + 1) % 2][:], xsrc(tt + 1))], writes=[("xt", (tt + 1) % 2)])
        z = zt[zi]
        zkeys = [("z", zi, dg) for dg in range(4)]
        for dg in range(4):
            bi = nb[0] % 6
            nb[0] += 1

            def mo(e, bi=bi, dg=dg, cb=cb, tl=tl):
                ins = None
                for h in range(16):
                    ins = e.matmul(aps[bi][:], lhsT=OTc[cb][:, h, tl * 128:(tl + 1) * 128], rhs=wout_sb[:, h, dg * 512:(dg + 1) * 512], start=(h == 0), stop=(h == 15))
                return ins
            P.op("pe", mo, reads=[("OTc", cb), wkey], writes=[("aps", bi)])
            P.op("dve", lambda e, bi=bi, dg=dg, z=z: e.tensor_tensor(out=z[:, dg * 512:(dg + 1) * 512], in0=aps[bi][:], in1=g1b[:, dg * 512:(dg + 1) * 512], op=ALU.mult),
                 reads=[("aps", bi), "g1b"], writes=[("z", zi, dg)])
        P.op("dve", lambda e, z=z, xi=xi: e.scalar_tensor_tensor(out=z[:], in0=xt[xi][:], scalar=DN_ALPHA, in1=z[:], op0=ALU.mult, op1=ALU.add),
             reads=[("xt", xi)] + zkeys, writes=zkeys)
        si = tt % 2
        _layer_norm(P, z, zkeys, st6[si], mv[si], si, lg, lb, epst, defer_bias=True)

    def stage_b(tt):
        zi = tt % 3
        z = zt[zi]
        zkeys = [("z", zi, dg) for dg in range(4)]
        _ln_bias(P, z, zkeys, lb)
        P.dma("sp", ("zst", zi), [(X1[tt * 128:(tt + 1) * 128, :], z[:])], reads=zkeys, writes=[("X1", tt)])

    for tt in range(ntiles):
        stage_a(tt)
        if tt >= 1:
            stage_b(tt - 1)
    stage_b(ntiles - 1)


def _rope_tables():
    rows = SEQ // GRID_W
    row_ids = np.repeat(np.arange(rows, dtype=np.float64), GRID_W)
    col_ids = np.tile(np.arange(GRID_W, dtype=np.float64), rows)
    axis_dim = 64
    inv_freq = np.power(10000.0, -np.arange(0, axis_dim, 2, dtype=np.float64) / axis_dim)
    ang_r = row_ids[:, None] * inv_freq
    ang_c = col_ids[:, None] * inv_freq
    ang = np.concatenate([ang_r, ang_r, ang_c, ang_c], axis=-1)
    return np.cos(ang).astype(np.float32), np.sin(ang).astype(np.float32)


def _consts():
    ident = np.eye(128, dtype=np.float32)
    rotm = np.zeros((128, 128), np.float32)
    for base in (0, 64):
        for j in range(32):
            rotm[base + 32 + j, base + j] = -1.0
            rotm[base + j, base + 32 + j] = 1.0
    jj = np.arange(128)[:, None]
    ii = np.arange(128)[None, :]
    m_prev = (jj >= ii).astype(np.float32)
    m_next = (jj <= ii).astype(np.float32)
    return ident, rotm, m_prev, m_next


_NC_CACHE = {}


def make_in_maps(x, c, ctx, c_ctx, w_ada, b_ada, w_in, q_norm_g, k_norm_g, sink_logit,
                 w_out, ln1_g, ln1_b, w_gate, w_up, w_down, ln2_g, ln2_b):
    f = lambda a: np.ascontiguousarray(np.asarray(a, dtype=np.float32))
    x, c, ctx, c_ctx = f(x), f(c), f(ctx), f(c_ctx)
    w_ada, b_ada, w_in = f(w_ada)[0], f(b_ada), f(w_in)[0]
    w_out, w_gate, w_up, w_down = f(w_out)[0], f(w_gate)[0], f(w_up)[0], f(w_down)[0]
    qg, kg = f(q_norm_g)[0], f(k_norm_g)[0]
    cosN, sinN = _rope_tables()
    ident, rotm, m_prev, m_next = _consts()
    wq = np.ascontiguousarray(np.concatenate([w_in[:, 0:1024], w_in[:, 1536:2560]], axis=1))
    wkv = np.ascontiguousarray(np.concatenate([w_in[:, 1024:1280], w_in[:, 2560:2816], w_in[:, 1280:1536], w_in[:, 2816:3072]], axis=1))
    wg = np.ascontiguousarray(w_gate.reshape(16, 128, NF, 128).transpose(2, 1, 0, 3))
    wu = np.ascontiguousarray(w_up.reshape(16, 128, NF, 128).transpose(2, 1, 0, 3))
    wd = np.ascontiguousarray(w_down.reshape(NF, 128, 4, 512).transpose(2, 1, 0, 3))
    gcol = np.ascontiguousarray(np.stack([qg, kg], axis=1))
    grow = np.ascontiguousarray(np.concatenate([qg, kg])[None, :])
    sink = f(sink_logit).reshape(1, 8)
    lnp = np.ascontiguousarray(np.stack([f(ln1_g)[0], f(ln1_b)[0], f(ln2_g)[0], f(ln2_b)[0]], axis=0))
    rep4 = lambda m: np.tile(m, (1, 4))
    zeros = np.zeros((128, 512), np.float32)
    in_maps = []
    for core in range(8):
        b, hf = core // 2, core % 2
        own = slice(hf * NOWN, (hf + 1) * NOWN)
        oth = slice((1 - hf) * NOWN, (2 - hf) * NOWN)
        xp = np.ascontiguousarray(np.concatenate([x[b, own], x[b, oth]], axis=0))
        cosT = np.ascontiguousarray(np.concatenate([cosN[own], cosN[oth]], axis=0).T)
        sinT = np.ascontiguousarray(np.concatenate([sinN[own], sinN[oth]], axis=0).T)
        cond = np.stack([c[b], c_ctx], axis=1)
        condT = np.ascontiguousarray(cond.reshape(16, 128, 2).transpose(1, 0, 2))
        masks = np.stack([rep4(m_prev), rep4(m_next),
                          rep4(m_prev) if hf == 1 else zeros,
                          rep4(m_next) if hf == 0 else zeros], axis=0)
        in_maps.append({
            "x": xp, "ctx": np.ascontiguousarray(ctx[b]), "condT": condT, "w_ada": w_ada, "b_ada": b_ada.reshape(1, -1),
            "wq": wq, "wkv": wkv, "w_out": w_out, "wg": wg, "wu": wu, "wd": wd,
            "cosT": cosT, "sinT": sinT, "ident": ident, "rotm": rotm, "gcol": gcol, "grow": grow,
            "sink": sink, "lnp": lnp, "masks": np.ascontiguousarray(masks),
        })
    return in_maps


def kernel(x, c, ctx, c_ctx, w_ada, b_ada, w_in, q_norm_g, k_norm_g, sink_logit,
           w_out, ln1_g, ln1_b, w_gate, w_up, w_down, ln2_g, ln2_b):
    in_maps = make_in_maps(x, c, ctx, c_ctx, w_ada, b_ada, w_in, q_norm_g, k_norm_g, sink_logit,
                           w_out, ln1_g, ln1_b, w_gate, w_up, w_down, ln2_g, ln2_b)
    nc = build_program()
    res = run_bass_kernel_spmd(nc, in_maps, core_ids=list(range(8)))
    outp = np.empty((4, SEQ, D), np.float32)
    for core in range(8):
        b, hf = core // 2, core % 2
        outp[b, hf * NOWN:(hf + 1) * NOWN] = res.results[core]["out"]
    return outp
```
